# Optimizing a Trainium2 kernel written in Bass

```python
import jax, jax.numpy as jnp
from jax import lax
import numpy as np

D_MODEL = 1024
BATCH = 4
SEQ = 4096
DEPTH = 2

GRID_W = 64
CTX_LEN = 256
D_INNER = 2 * D_MODEL
BRANCH_W = D_INNER // 4
RG_W = BRANCH_W
RG_HEADS = 8
RG_HD = RG_W // RG_HEADS
RG_CONV = 4
RG_C = 8.0
SC_W = BRANCH_W
SC_CONV = 3
FN_W = BRANCH_W
FN_GROUPS = 4
FN_GD = FN_W // FN_GROUPS
SSD_W = BRANCH_W
SSD_HD = 64
SSD_HEADS = SSD_W // SSD_HD
SSD_GROUPS = 2
SSD_STATE = 64
SSD_CONV = 4
SSD_CHUNK = 128
SSD_XBC = SSD_W + 2 * SSD_GROUPS * SSD_STATE
SSD_DT = 2 * SSD_HEADS
CONV4_PAD = (2, 1)
NORM_EPS = 1e-6
COL_WIDTHS = (RG_W, SSD_XBC, SSD_DT, RG_W, SSD_W, SC_W, SC_W, SC_W, SC_W, FN_W, FN_W)
SCAN_COLS = RG_W + SSD_XBC + SSD_DT
IN_COLS = sum(COL_WIDTHS)

kernel_name = "hybrid_rglru_conv_fourier_ssd_block"


def split_cols(y, widths):
    idx = [int(i) for i in np.cumsum(widths)[:-1]]
    return jnp.split(y, idx, axis=-1)


def rmsnorm(x, w):
    xf = x.astype(jnp.float32)
    y = xf * lax.rsqrt(jnp.mean(xf * xf, axis=-1, keepdims=True) + NORM_EPS)
    return (y * w.astype(jnp.float32)).astype(x.dtype)


def dwconv(x, w, b, pad):
    y = lax.conv_general_dilated(x, w[:, None, :].astype(x.dtype), window_strides=(1,), padding=[pad],
                                 dimension_numbers=('NWC', 'WIO', 'NWC'), feature_group_count=x.shape[-1])
    return y if b is None else y + b.astype(y.dtype)


def adaln(cvec, ada_w, ada_b):
    m = jax.nn.silu(cvec) @ ada_w + ada_b
    m = m.reshape(-1, 1, 3 * D_MODEL)
    return jnp.split(m, 3, axis=-1)


def linear_scan(a, v, h0, reverse):
    def comb(l, r):
        return (l[0] * r[0], r[0] * l[1] + r[1])
    acum, h = lax.associative_scan(comb, (a, v), axis=1, reverse=reverse)
    return h + acum * h0[:, None]


def rglru(u_raw, lp, h0, with_output):
    u = dwconv(u_raw, lp['rg_conv_w'], lp['rg_conv_b'], CONV4_PAD).astype(jnp.float32)
    bsz, t, _ = u.shape
    ub = u.reshape(bsz, t, RG_HEADS, RG_HD)
    ys, finals = [], []
    for d, reverse in enumerate((False, True)):
        r = jax.nn.sigmoid(jnp.einsum('bthi,hij->bthj', ub, lp['rg_gate_a_w'][d].astype(jnp.float32)).reshape(bsz, t, RG_W)
                           + lp['rg_gate_a_b'][d].astype(jnp.float32))
        i = jax.nn.sigmoid(jnp.einsum('bthi,hij->bthj', ub, lp['rg_gate_x_w'][d].astype(jnp.float32)).reshape(bsz, t, RG_W)
                           + lp['rg_gate_x_b'][d].astype(jnp.float32))
        log_a = -RG_C * r * jax.nn.softplus(-lp['rg_lambda'][d].astype(jnp.float32))
        v = jnp.sqrt(-jnp.expm1(2.0 * log_a)) * (i * u)
        h = linear_scan(jnp.exp(log_a), v, h0[d], reverse)
        ys.append(h)
        finals.append(h[:, 0] if reverse else h[:, -1])
    y = ys[0] + ys[1] if with_output else None
    return y, jnp.stack(finals)


def to_chunks(z):
    return z.reshape(z.shape[0], z.shape[1] // SSD_CHUNK, SSD_CHUNK, *z.shape[2:])


def ssd_inputs(xbc_raw, dt_raw, lp):
    xbc = jax.nn.silu(dwconv(xbc_raw, lp['ssd_conv_w'], lp['ssd_conv_b'], CONV4_PAD)).astype(jnp.float32)
    xs, bm, cm = split_cols(xbc, (SSD_W, SSD_GROUPS * SSD_STATE, SSD_GROUPS * SSD_STATE))
    bsz, t, _ = xs.shape
    rep = SSD_HEADS // SSD_GROUPS
    xs = xs.reshape(bsz, t, SSD_HEADS, SSD_HD)
    bm = jnp.repeat(bm.reshape(bsz, t, SSD_GROUPS, SSD_STATE), rep, axis=2)
    cm = jnp.repeat(cm.reshape(bsz, t, SSD_GROUPS, SSD_STATE), rep, axis=2)
    dt = jax.nn.softplus(dt_raw.astype(jnp.float32).reshape(bsz, t, 2, SSD_HEADS)
                         + lp['ssd_dt_bias'].astype(jnp.float32))
    a = -jnp.exp(lp['ssd_a_log'].astype(jnp.float32))
    return xs, bm, cm, dt, a


def chunk_states(xdt, bmc, cs, h0):
    decay_to_end = jnp.exp(cs[:, :, -1:] - cs)
    states = jnp.einsum('bcqhn,bcqh,bcqhp->bchpn', bmc, decay_to_end, xdt)
    chunk_decay = jnp.exp(cs[:, :, -1])

    def step(h, inp):
        dec, st = inp
        return dec[..., None, None] * h + st, h

    h_final, h_enter = lax.scan(step, h0, (jnp.moveaxis(chunk_decay, 1, 0), jnp.moveaxis(states, 1, 0)))
    return jnp.moveaxis(h_enter, 0, 1), h_final


def ssd_scan(xs, dt, a, bm, cm, h0, reverse, with_output):
    if reverse:
        xs, dt, bm, cm = xs[:, ::-1], dt[:, ::-1], bm[:, ::-1], cm[:, ::-1]
    bsz, t = xs.shape[0], xs.shape[1]
    xdt = to_chunks(xs * dt[..., None])
    cs = jnp.cumsum(to_chunks(dt * a), axis=2)
    bmc = to_chunks(bm)
    h_enter, h_final = chunk_states(xdt, bmc, cs, h0)
    if not with_output:
        return None, h_final
    cmc = to_chunks(cm)
    lower = jnp.tril(jnp.ones((SSD_CHUNK, SSD_CHUNK), dtype=bool))[None, None, :, :, None]
    seg = cs[:, :, :, None, :] - cs[:, :, None, :, :]
    lmat = jnp.where(lower, jnp.exp(jnp.where(lower, seg, 0.0)), 0.0)
    scores = jnp.einsum('bcihn,bcjhn->bcijh', cmc, bmc) * lmat
    y = (jnp.einsum('bcijh,bcjhp->bcihp', scores, xdt)
         + jnp.einsum('bcihn,bchpn,bcih->bcihp', cmc, h_enter, jnp.exp(cs)))
    y = y.reshape(bsz, t, SSD_HEADS, SSD_HD)
    if reverse:
        y = y[:, ::-1]
    return y, h_final


def ssd_branch(xbc_raw, dt_raw, z, lp, h0, with_output):
    xs, bm, cm, dt, a = ssd_inputs(xbc_raw, dt_raw, lp)
    yf, hf = ssd_scan(xs, dt[:, :, 0], a[0], bm, cm, h0[0], False, with_output)
    yb, hb = ssd_scan(xs, dt[:, :, 1], a[1], bm, cm, h0[1], True, with_output)
    states = jnp.stack((hf, hb))
    if not with_output:
        return None, states
    bsz, t = xs.shape[0], xs.shape[1]
    y = yf + yb + lp['ssd_d'].astype(jnp.float32)[:, None] * xs
    y = y.reshape(bsz, t, SSD_W) * jax.nn.silu(z.astype(jnp.float32))
    return rmsnorm(y, lp['ssd_norm_w']), states


def shortconv_branch(bg, cg, xs, lp, rows):
    v = cg * xs
    if rows is None:
        vc = dwconv(v, lp['sc_conv_w'], None, (1, 1))
    else:
        bsz, t, ch = v.shape
        vc = dwconv(v.reshape(bsz * rows, GRID_W, ch), lp['sc_conv_w'], None, (1, 1)).reshape(bsz, t, ch)
    return bg * vc


def fourier_branch(xs):
    bsz, t, _ = xs.shape
    v = xs.astype(jnp.float32).reshape(bsz, t, FN_GROUPS, FN_GD)
    return jnp.fft.fft2(v, axes=(1, 3), norm='ortho').real.reshape(bsz, t, FN_W)


def mix(h, lp, rows, rg_h0, ssd_h0):
    proj = h @ lp['w_in']
    rg_x, ssd_xbc, ssd_dt, rg_g, ssd_z, sc_b, sc_c, sc_x, sc_g, fn_x, fn_g = split_cols(proj, COL_WIDTHS)
    y_rg, rg_st = rglru(rg_x, lp, rg_h0, True)
    y_ssd, ssd_st = ssd_branch(ssd_xbc, ssd_dt, ssd_z, lp, ssd_h0, True)
    y_sc = shortconv_branch(sc_b, sc_c, sc_x, lp, rows)
    y_fn = fourier_branch(fn_x)
    ycat = jnp.concatenate([y_rg * jax.nn.silu(rg_g), y_sc * jax.nn.silu(sc_g),
                            y_fn * jax.nn.silu(fn_g), y_ssd], axis=-1)
    return ycat.astype(h.dtype) @ lp['w_out'], rg_st, ssd_st


def context_states(h, lp):
    proj = h @ lp['w_in'][:, :SCAN_COLS]
    rg_x, ssd_xbc, ssd_dt = split_cols(proj, (RG_W, SSD_XBC, SSD_DT))
    bsz = h.shape[0]
    _, rg_st = rglru(rg_x, lp, jnp.zeros((2, bsz, RG_W), jnp.float32), False)
    _, ssd_st = ssd_branch(ssd_xbc, ssd_dt, None, lp, jnp.zeros((2, bsz, SSD_HEADS, SSD_HD, SSD_STATE), jnp.float32), False)
    return rg_st, ssd_st


def setup_inputs(seed: int = 0) -> dict:
    key = jax.random.key(seed)
    ks = jax.random.split(key, 24)
    f32 = jnp.float32
    nrm = lambda k, shape, s: jax.random.normal(k, shape, f32) * s
    a8 = jax.random.uniform(ks[14], (DEPTH, 2, RG_W), f32, minval=0.9, maxval=0.999)
    a_base = a8 ** (1.0 / RG_C)
    dt0 = jnp.exp(jax.random.uniform(ks[18], (DEPTH, 2, SSD_HEADS), f32, minval=jnp.log(1e-3), maxval=jnp.log(1e-1)))
    return {
        'x': nrm(ks[0], (BATCH, SEQ, D_MODEL), 1.0),
        'c': nrm(ks[1], (BATCH, D_MODEL), 1.0),
        'ctx': nrm(ks[2], (BATCH, CTX_LEN, D_MODEL), 1.0),
        'c_ctx': nrm(ks[3], (D_MODEL,), 1.0),
        'ada_w': nrm(ks[4], (DEPTH, D_MODEL, 3 * D_MODEL), 0.5 * D_MODEL ** -0.5),
        'ada_b': nrm(ks[5], (DEPTH, 3 * D_MODEL), 0.02),
        'norm_w': 1.0 + nrm(ks[6], (DEPTH, D_MODEL), 0.02),
        'w_in': nrm(ks[7], (DEPTH, D_MODEL, IN_COLS), D_MODEL ** -0.5),
        'w_out': nrm(ks[8], (DEPTH, D_INNER, D_MODEL), D_INNER ** -0.5),
        'rg_conv_w': nrm(ks[9], (DEPTH, RG_CONV, RG_W), RG_CONV ** -0.5),
        'rg_conv_b': nrm(ks[10], (DEPTH, RG_W), 0.02),
        'rg_gate_a_w': nrm(ks[11], (DEPTH, 2, RG_HEADS, RG_HD, RG_HD), RG_HD ** -0.5),
        'rg_gate_a_b': nrm(ks[12], (DEPTH, 2, RG_W), 0.02),
        'rg_gate_x_w': nrm(ks[13], (DEPTH, 2, RG_HEADS, RG_HD, RG_HD), RG_HD ** -0.5),
        'rg_gate_x_b': nrm(ks[15], (DEPTH, 2, RG_W), 0.02),
        'rg_lambda': jnp.log(a_base) - jnp.log1p(-a_base),
        'sc_conv_w': nrm(ks[16], (DEPTH, SC_CONV, SC_W), SC_CONV ** -0.5),
        'ssd_conv_w': nrm(ks[17], (DEPTH, SSD_CONV, SSD_XBC), SSD_CONV ** -0.5),
        'ssd_conv_b': nrm(ks[19], (DEPTH, SSD_XBC), 0.02),
        'ssd_dt_bias': dt0 + jnp.log(-jnp.expm1(-dt0)),
        'ssd_a_log': jnp.log(jax.random.uniform(ks[20], (DEPTH, 2, SSD_HEADS), f32, minval=1.0, maxval=16.0)),
        'ssd_d': 1.0 + nrm(ks[21], (DEPTH, SSD_HEADS), 0.1),
        'ssd_norm_w': 1.0 + nrm(ks[22], (DEPTH, SSD_W), 0.02),
        'final_norm_w': 1.0 + nrm(ks[23], (D_MODEL,), 0.02),
    }


def reference(x, c, ctx, c_ctx, ada_w, ada_b, norm_w, w_in, w_out, rg_conv_w, rg_conv_b, rg_gate_a_w, rg_gate_a_b,
              rg_gate_x_w, rg_gate_x_b, rg_lambda, sc_conv_w, ssd_conv_w, ssd_conv_b, ssd_dt_bias, ssd_a_log, ssd_d,
              ssd_norm_w, final_norm_w):
    bsz = x.shape[0]
    rows = x.shape[1] // GRID_W
    for l in range(DEPTH):
        lp = dict(w_in=w_in[l], w_out=w_out[l], rg_conv_w=rg_conv_w[l], rg_conv_b=rg_conv_b[l],
                  rg_gate_a_w=rg_gate_a_w[l], rg_gate_a_b=rg_gate_a_b[l], rg_gate_x_w=rg_gate_x_w[l],
                  rg_gate_x_b=rg_gate_x_b[l], rg_lambda=rg_lambda[l], sc_conv_w=sc_conv_w[l],
                  ssd_conv_w=ssd_conv_w[l], ssd_conv_b=ssd_conv_b[l], ssd_dt_bias=ssd_dt_bias[l],
                  ssd_a_log=ssd_a_log[l], ssd_d=ssd_d[l], ssd_norm_w=ssd_norm_w[l])
        shift_c, scale_c, gate_c = adaln(c_ctx, ada_w[l], ada_b[l])
        h_ctx = rmsnorm(ctx, norm_w[l]) * (1.0 + scale_c) + shift_c
        if l < DEPTH - 1:
            out_ctx, rg_st, ssd_st = mix(h_ctx, lp, None, jnp.zeros((2, bsz, RG_W), jnp.float32),
                                         jnp.zeros((2, bsz, SSD_HEADS, SSD_HD, SSD_STATE), jnp.float32))
            new_ctx = ctx + gate_c * out_ctx
        else:
            rg_st, ssd_st = context_states(h_ctx, lp)
            new_ctx = ctx
        shift, scale, gate = adaln(c, ada_w[l], ada_b[l])
        h = rmsnorm(x, norm_w[l]) * (1.0 + scale) + shift
        out, _, _ = mix(h, lp, rows, rg_st, ssd_st)
        x = x + gate * out
        ctx = new_ctx
    return rmsnorm(x, final_norm_w)
```

```python
import os
from contextlib import ExitStack
import numpy as np
import ml_dtypes
import concourse.bass as bass
import concourse.mybir as mybir
from concourse.bass_utils import run_bass_kernel_spmd

F32 = mybir.dt.float32
BF16 = mybir.dt.bfloat16
AF = mybir.ActivationFunctionType
ALU = mybir.AluOpType

D = 1024
T = 4096
TC = 256
NL = 2
NCORES = 8
EPS = 1e-6
OFF = dict(rg_x=0, ssd_xbc=512, ssd_dt=1280, rg_g=1296, ssd_z=1808, sc_b=2320, sc_c=2832,
           sc_x=3344, sc_g=3856, fn_x=4368, fn_g=4880)
PV = dict(nw=0, adab=8, rgcw=32, rgcb=48, rgba=52, rgbx=60, rglam=68, sccw=76, sdcw=88, sdcb=112, fnw=118)
NPV = 128
RB = dict(dtb=0, alog=16, dbc=32, snw=544)
NRB = 1056
ARENA_BYTES = 108544
NEG = -30000.0

ENGS = ("tensor", "vector", "scalar", "gpsimd", "sync")
SAME_ENGINE_RAW = True
STORE_Q = "sync"


class Buf:
    __slots__ = ("name", "last_w", "reads")

    def __init__(self, name):
        self.name = name
        self.last_w = None
        self.reads = []


class Prog:
    def __init__(self, nc):
        self.nc = nc
        self.es = ExitStack()
        self.ops = {e: [] for e in ENGS}
        self.seq = {e: 0 for e in ENGS}
        self.waited = {e: {} for e in ENGS}
        self.sems = {}
        self.dmacount = {}

    def sbuf(self, name, shape, dtype):
        return self.es.enter_context(self.nc.sbuf_tensor("sb_" + name, list(shape), dtype))

    def psum(self, name, shape, dtype=F32):
        return self.es.enter_context(self.nc.psum_tensor(name, list(shape), dtype))

    def sem(self, key):
        if key not in self.sems:
            self.sems[key] = self.es.enter_context(self.nc.semaphore("s_" + str(key)))
        return self.sems[key]

    def _collect(self, eng, reads, writes, force=False):
        waits = {}

        def need(ev, is_raw):
            if ev is None:
                return
            key, val = ev
            if key == eng and not force and not (is_raw and SAME_ENGINE_RAW):
                return
            if self.waited[eng].get(key, 0) >= val:
                return
            if waits.get(key, 0) < val:
                waits[key] = val

        for b in reads:
            need(b.last_w, True)
        for b in writes:
            need(b.last_w, False)
            for r in b.reads:
                need(r, False)
        for k, v in waits.items():
            self.waited[eng][k] = v
        return waits

    def op(self, eng, fn, reads=(), writes=()):
        waits = self._collect(eng, reads, writes)
        self.seq[eng] += 1
        ev = (eng, self.seq[eng])
        for b in reads:
            b.reads.append(ev)
        for b in writes:
            b.last_w = ev
            b.reads = []
        self.sem(eng)
        for k in waits:
            self.sem(k)
        self.ops[eng].append((waits, fn, (eng, 1)))
        return ev

    def dma(self, eng, fn, reads=(), writes=(), semkey=None):
        if semkey is None:
            semkey = "d_" + (writes[0].name if writes else reads[0].name)
        waits = self._collect(eng, reads, writes, force=True)
        self.dmacount[semkey] = self.dmacount.get(semkey, 0) + 16
        ev = (semkey, self.dmacount[semkey])
        for b in reads:
            b.reads.append(ev)
        for b in writes:
            b.last_w = ev
            b.reads = []
        self.sem(semkey)
        for k in waits:
            self.sem(k)
        self.ops[eng].append((waits, fn, (semkey, 16)))
        return ev

    def barrier(self):
        tgt = {}
        for e in ENGS:
            if self.seq[e] > 0:
                tgt[e] = self.seq[e]
        for k, v in self.dmacount.items():
            tgt[k] = v
        for e in ENGS:
            waits = {}
            for k, v in tgt.items():
                if k == e:
                    continue
                if self.waited[e].get(k, 0) >= v:
                    continue
                waits[k] = v
                self.waited[e][k] = v
            if waits:
                self.seq[e] += 1
                self.sem(e)
                self.ops[e].append((waits, (lambda eng: eng.nop()), (e, 1)))

    def finalize(self):
        fw = {}
        for e in ENGS:
            if e != "sync" and self.seq[e] > 0:
                fw[e] = self.seq[e]
        for k, v in self.dmacount.items():
            fw[k] = max(fw.get(k, 0), v)
        nc = self.nc
        sems = self.sems
        ops = self.ops
        with nc.Block() as block:
            def run(engname):
                def body(eng):
                    for waits, fn, (ik, iv) in ops[engname]:
                        for k, v in waits.items():
                            eng.wait_ge(sems[k], v)
                        ins = fn(eng)
                        ins.then_inc(sems[ik], iv)
                    if engname == "sync":
                        for k, v in fw.items():
                            eng.wait_ge(sems[k], v)
                return body
            block.tensor(run("tensor"))
            block.vector(run("vector"))
            block.scalar(run("scalar"))
            block.gpsimd(run("gpsimd"))
            block.sync(run("sync"))
        self.es.close()


class _Cut(Exception):
    pass


def build_program(dbg=None):
    nc = bass.Bass("TRN2", target_bir_lowering=False)
    P = Prog(nc)
    cutn = int((dbg or {}).get("cut", 0))

    def CUT(n):
        if cutn == n:
            raise _Cut()

    def din(name, shape, dt=F32):
        return nc.dram_tensor(name, list(shape), dt, kind="ExternalInput").ap()

    xT_d = din("xT", [D, T])
    ctxT_d = din("ctxT", [D, TC])
    cc_d = din("cc", [128, 8, 2])
    adaw_d = din("ada_w", [NL, D, 3 * D])
    win_d = din("w_in", [NL, D, 5392])
    wout_d = din("w_out", [NL, 2 * D, D])
    pvec_d = din("pvec", [NL, 128, NPV])
    rowb_d = din("rowb", [NL, 128, NRB])
    gbd_d = din("gbd", [NL, 128, 16, 128])
    cf32_d = din("cf32", [128, 384])
    cbf_d = din("cbf", [128, 1794], BF16)
    dftc_d = din("dftc", [T // 512, 128, 16, 256], BF16)
    dfts_d = din("dfts", [T // 512, 128, 16, 256], BF16)
    dft256_d = din("dft256", [128, 2, 512], BF16)
    outT_d = nc.dram_tensor("outT", [D, T], F32, kind="ExternalOutput").ap()
    skind = dict(kind="ExternalOutput") if dbg else {}
    x1_d = nc.dram_tensor("x1s", [D, T], F32, **skind).ap()
    ycat_d = nc.dram_tensor("ycat", [2 * D, T], BF16, **skind).ap()
    ycatc_d = nc.dram_tensor("ycatc", [2 * D, TC], BF16, **skind).ap()
    if dbg:
        dbg_hT = nc.dram_tensor("dbg_hT", [128, 8, T], BF16, kind="ExternalOutput").ap()
        dbg_hTc = nc.dram_tensor("dbg_hTc", [128, 8, TC], BF16, kind="ExternalOutput").ap()
        dbg_mod = nc.dram_tensor("dbg_mod", [NL, 128, 48], F32, kind="ExternalOutput").ap()
        dbg_st = nc.dram_tensor("dbg_st", [128, 8 + 2 * 512], F32, kind="ExternalOutput").ap()
    heb_d = nc.dram_tensor("hebd", [T // 128, 128, 512], BF16).ap()
    b_x1 = [Buf("x1_%d" % i) for i in range(T // 256)]
    b_ycat = {}
    for ic in (0, 1):
        for c16 in range(16):
            b_ycat[(ic, c16)] = Buf("yc%d_%d" % (ic, c16))
    b_heb_d = [Buf("hebd%d" % i) for i in range(T // 128)]
    b_out = Buf("outd")

    hT = P.sbuf("hT", [128, 8, T], BF16)
    b_hT = [Buf("hT%d" % i) for i in range(T // 512)]
    hTc = P.sbuf("hTc", [128, 8, TC], BF16)
    b_hTc = [Buf("hTc")]
    wst = [P.sbuf("wst%d" % i, [128, 8, 128], F32) for i in range(2)]
    b_wst = [Buf("wst%d" % i) for i in range(2)]
    wbf = [P.sbuf("wbf%d" % i, [128, 8, 128], BF16) for i in range(2)]
    b_wbf = [Buf("wbf%d" % i) for i in range(2)]
    gatew = P.sbuf("gatew", [128, 16, 128], BF16)
    b_gatew = Buf("gatew")
    pvecs = [P.sbuf("pvec%d" % i, [128, NPV], F32) for i in range(NL)]
    b_pvecs = [Buf("pvec%d" % i) for i in range(NL)]
    rowb = P.sbuf("rowb", [128, NRB], F32)
    b_rowb = Buf("rowb")
    cf32 = P.sbuf("cf32", [128, 384], F32)
    cbf = P.sbuf("cbf", [128, 1794], BF16)
    b_const = Buf("const")
    dft256 = P.sbuf("dft256", [128, 2, 512], BF16)
    ccs = P.sbuf("ccs", [128, 8, 2], F32)
    b_ccs = Buf("ccs")
    modts = [P.sbuf("modt%d" % i, [128, 24, 2], F32) for i in range(NL)]
    b_modts = [Buf("modt%d" % i) for i in range(NL)]
    Amods = [P.sbuf("Amod%d" % i, [128, 8, 2], F32) for i in range(NL)]
    b_Amods = [Buf("Amod%d" % i) for i in range(NL)]
    cAt = P.sbuf("cAt", [128, 8], F32)
    b_cA = Buf("cA")
    smallt = P.sbuf("smallt", [128, 64], F32)
    b_small = Buf("small")
    epsc = P.sbuf("epsc", [128, 4], F32)
    b_eps = Buf("epsc")
    rgst = P.sbuf("rgst", [128, 8], F32)
    b_rgst = Buf("rgst")
    zero8 = P.sbuf("zero8", [128, 8], F32)
    ssdh0 = [P.sbuf("ssdh0_%d" % d, [128, 4, 128], F32) for d in range(2)]
    b_ssdh0 = [Buf("ssdh0_%d" % d) for d in range(2)]
    aneg = P.sbuf("aneg", [128, 16], F32)
    b_aneg = Buf("aneg")
    arena = P.sbuf("arena", [128, ARENA_BYTES // 4], F32)

    ident_f = cf32[:, 0:128]
    tri_f = cf32[:, 128:256]
    tri_b = cf32[:, 256:384]
    ident_b = cbf[:, 0:128]
    ones_b = cbf[:, 128:256]
    mask_bf = [cbf[:, 256:768], cbf[:, 768:1280]]
    cgsg = cbf[:, 1280:1536]
    alt2 = cbf[:, 1536:1538]
    altrow = cbf[:, 1538:1794]

    PS = [P.psum("ps%d" % i, [128, 512], F32) for i in range(7)]
    b_PS = [Buf("ps%d" % i) for i in range(7)]
    PSB = P.psum("psb", [128, 1024], BF16)
    b_PSB = Buf("psb")

    tog = [0]
    cpeng = ["vector"]

    def ACT(out, in_, func, r, w, scale=None, bias=None, accum=None):
        kw = {}
        if scale is not None:
            kw["scale"] = scale
        if bias is not None:
            kw["bias"] = bias
        if accum is not None:
            kw["accum_out"] = accum
        P.op("scalar", lambda e: e.activation(out=out, in_=in_, func=func, **kw), reads=r, writes=w)

    def TT(out, a, b, op, r, w, eng="vector"):
        P.op(eng, lambda e: e.tensor_tensor(out=out, in0=a, in1=b, op=op), reads=r, writes=w)

    def STT(out, in0, scalar, in1, op0, op1, r, w):
        P.op("vector", lambda e: e.scalar_tensor_tensor(out=out, in0=in0, scalar=scalar, in1=in1,
                                                         op0=op0, op1=op1), reads=r, writes=w)

    def TS(out, in0, s1, s2, op0, op1, r, w, eng="vector"):
        if op1 is None:
            P.op(eng, lambda e: e.tensor_scalar(out=out, in0=in0, scalar1=s1, scalar2=None, op0=op0),
                 reads=r, writes=w)
        else:
            P.op(eng, lambda e: e.tensor_scalar(out=out, in0=in0, scalar1=s1, scalar2=s2, op0=op0, op1=op1),
                 reads=r, writes=w)

    def MM(out, lhsT, rhs, start, stop, r, w):
        P.op("tensor", lambda e: e.matmul(out=out, lhsT=lhsT, rhs=rhs, start=start, stop=stop,
                                          skip_group_check=True), reads=r, writes=w)

    def TR(out, in_, ident, r, w):
        P.op("tensor", lambda e: e.transpose(out=out, in_=in_, identity=ident), reads=r, writes=w)

    def CP(out, in_, r, w, eng=None):
        if eng is None:
            tog[0] ^= 1
            eng = "scalar" if tog[0] else "vector"
        if eng == "scalar":
            P.op("scalar", lambda e: e.activation(out=out, in_=in_, func=AF.Identity), reads=r, writes=w)
        else:
            P.op(eng, lambda e: e.tensor_copy(out=out, in_=in_), reads=r, writes=w)

    def MEMSET(ap, val, w, eng="vector"):
        P.op(eng, lambda e: e.memset(ap, val), writes=w)

    def DMA(out, in_, r, w, q="sync", semkey=None, slow=False):
        if semkey is None and q == "gpsimd" and r:
            semkey = "dw_" + r[0].name
        if q == "gpsimd":
            q = STORE_Q
        if slow:
            P.dma(q, lambda e: e.dma_start(out=out, in_=in_, allow_slow_non_contiguous=True), reads=r, writes=w,
                  semkey=semkey)
            return
        P.dma(q, lambda e: e.dma_start(out=out, in_=in_), reads=r, writes=w, semkey=semkey)

    apos = [0]

    def arena_reset():
        P.barrier()
        apos[0] = 0

    def af32(shape):
        n = int(np.prod(shape[1:]))
        w0 = apos[0] // 4
        apos[0] += n * 4
        assert apos[0] <= ARENA_BYTES, ("arena overflow", apos[0])
        ap = arena[:, w0:w0 + n]
        if len(shape) == 3:
            ap = ap.rearrange("p (a b) -> p a b", b=shape[2])
        elif len(shape) == 4:
            ap = ap.rearrange("p (a b c) -> p a b c", b=shape[2], c=shape[3])
        return ap

    def abf(shape):
        n = int(np.prod(shape[1:]))
        assert n % 2 == 0
        w0 = apos[0] // 4
        apos[0] += n * 2
        assert apos[0] <= ARENA_BYTES, ("arena overflow", apos[0])
        ap = arena[:, w0:w0 + n // 2].bitcast(BF16)
        if len(shape) == 3:
            ap = ap.rearrange("p (a b) -> p a b", b=shape[2])
        elif len(shape) == 4:
            ap = ap.rearrange("p (a b c) -> p a b c", b=shape[2], c=shape[3])
        return ap

    DMA(cf32[:], cf32_d[:, :], [], [b_const], semkey="d_const")
    DMA(cbf[:], cbf_d[:, :], [], [b_const], semkey="d_const")
    DMA(dft256[:], dft256_d[:, :, :], [], [b_const], semkey="d_const")
    DMA(ccs[:], cc_d[:, :, :], [], [b_ccs], semkey="d_const")
    MEMSET(epsc[:, 0:1], EPS, [b_eps])
    MEMSET(epsc[:, 1:2], 1.0, [b_eps])
    MEMSET(epsc[:, 2:4], 0.0, [b_eps])
    MEMSET(zero8[:], 0.0, [b_eps])
    ACT(ccs[:], ccs[:], AF.Silu, [b_ccs], [b_ccs])

    wrot = [0]

    def load_w(src3d, k0, nk, col0, ncols, dst_ap, dst_bufs):
        s = wrot[0]
        wrot[0] ^= 1
        DMA(wst[s][:, 0:nk, 0:ncols], src3d[:, k0:k0 + nk, col0:col0 + ncols], [], [b_wst[s]])
        CP(dst_ap, wst[s][:, 0:nk, 0:ncols], [b_wst[s]], dst_bufs)

    wbrot = [0]

    def load_win(l, col0, ncols):
        s = wbrot[0]
        wbrot[0] ^= 1
        src = win_d[l].rearrange("(k p) c -> p k c", p=128)
        load_w(src, 0, 8, col0, ncols, wbf[s][:, :, 0:ncols], [b_wbf[s]])
        return wbf[s], b_wbf[s]

    psrot = [0]

    def proj_cm(l, col0, ncols, hsrc, hbufs, Tn, evac, banks=(0, 1, 2, 3)):
        w, bw = load_win(l, col0, ncols)
        tog[0] ^= 1
        cpeng[0] = "scalar" if tog[0] else "vector"
        nt = (Tn + 511) // 512
        for tt in range(nt):
            t0 = tt * 512
            n = min(512, Tn - t0)
            bi = banks[psrot[0] % len(banks)]
            psrot[0] += 1
            hb = [hbufs[min(tt, len(hbufs) - 1)]]
            for k in range(8):
                MM(PS[bi][0:ncols, 0:n], w[:, k, 0:ncols], hsrc[:, k, t0:t0 + n], k == 0, k == 7,
                   [bw] + hb, [b_PS[bi]])
            evac(PS[bi][0:ncols, 0:n], b_PS[bi], t0, n, tt)

    def softplus_tile(dst, src, n, scr1, scr2, r, w, bscr):
        TS(scr1, src, 30.0, None, ALU.min, None, r, bscr)
        ACT(scr1, scr1, AF.Exp, bscr, bscr)
        TS(scr2, scr1, -0.25, 1.0 / 3.0, ALU.mult, ALU.add, bscr, bscr)
        TT(scr2, scr2, scr1, ALU.mult, bscr, bscr)
        TS(scr2, scr2, -0.5, None, ALU.add, None, bscr, bscr)
        TT(scr2, scr2, scr1, ALU.mult, bscr, bscr)
        TS(scr2, scr2, 1.0, None, ALU.add, None, bscr, bscr)
        TT(scr2, scr2, scr1, ALU.mult, bscr, bscr)
        ACT(dst, scr1, AF.Ln, bscr + [b_eps], w, bias=epsc[:, 1:2])
        TS(scr1, scr1, 0.05, None, ALU.is_gt, None, bscr, bscr)
        TT(dst, dst, scr2, ALU.subtract, w + bscr, w)
        TT(dst, dst, scr1, ALU.mult, w + bscr, w)
        TT(dst, dst, scr2, ALU.add, w + bscr, w)

    cur = {}

    def set_layer(l):
        cur["pvec"] = pvecs[l]
        cur["b_pvec"] = b_pvecs[l]
        cur["modt"] = modts[l]
        cur["b_modt"] = b_modts[l]
        cur["Amod"] = Amods[l]
        cur["b_Amod"] = b_Amods[l]

    def ada_setup(l):
        pvec, b_pvec = pvecs[l], b_pvecs[l]
        modt, b_modt, Amod, b_Amod = modts[l], b_modts[l], Amods[l], b_Amods[l]
        DMA(pvec[:], pvec_d[l], [], [b_pvec])
        arena_reset()
        aw = [af32([128, 8, 512]) for _ in range(2)]
        b_aw = [Buf("aw0"), Buf("aw1")]
        mrow = af32([128, 3 * D])[0:2, :]
        b_mrow = Buf("mrow")
        src = adaw_d[l].rearrange("(k p) c -> p k c", p=128)
        for jb in range(6):
            s = jb % 2
            DMA(aw[s], src[:, :, jb * 512:(jb + 1) * 512], [], [b_aw[s]])
            bi = jb % 2
            for k in range(8):
                MM(PS[bi][0:2, :], ccs[:, k, :], aw[s][:, k, :], k == 0, k == 7, [b_aw[s], b_ccs], [b_PS[bi]])
            CP(mrow[:, jb * 512:(jb + 1) * 512], PS[bi][0:2, :], [b_PS[bi]], [b_mrow], eng="vector")
        for j in range(24):
            TR(PS[2][:, j * 2:j * 2 + 2], mrow[:, j * 128:(j + 1) * 128], ident_f[0:2, 0:2], [b_mrow, b_const], [b_PS[2]])
        adab = pvec[:, PV["adab"]:PV["adab"] + 24]
        TT(modt[:], PS[2][:, 0:48].rearrange("p (j c) -> p j c", c=2),
           adab.unsqueeze(2).broadcast_to([128, 24, 2]), ALU.add, [b_PS[2], b_pvec], [b_modt])
        nw = pvec[:, PV["nw"]:PV["nw"] + 8]
        TS(Amod[:], modt[:, 8:16, :], 1.0, None, ALU.add, None, [b_modt], [b_Amod])
        TT(Amod[:], Amod[:], nw.unsqueeze(2).broadcast_to([128, 8, 2]), ALU.mult, [b_Amod, b_pvec], [b_Amod])

    def layer_params(l):
        set_layer(l)
        pvec, b_pvec = pvecs[l], b_pvecs[l]
        P.barrier()
        DMA(rowb[:], rowb_d[l], [], [b_rowb])
        gsrc = gbd_d[l]
        for half in range(2):
            load_w(gsrc, half * 8, 8, 0, 128, gatew[:, half * 8:(half + 1) * 8, :], [b_gatew])
        lam = pvec[:, PV["rglam"]:PV["rglam"] + 8]
        TS(smallt[:, 0:8], lam, -1.0, None, ALU.mult, None, [b_pvec], [b_small])
        softplus_tile(cAt[:], smallt[:, 0:8], 8, smallt[:, 8:16], smallt[:, 16:24], [b_small], [b_cA], [b_small])
        TS(cAt[:], cAt[:], -8.0, None, ALU.mult, None, [b_cA], [b_cA])
        ACT(aneg[:], rowb[:, RB["alog"]:RB["alog"] + 16], AF.Exp, [b_rowb], [b_aneg])
        TS(aneg[:], aneg[:], -1.0, None, ALU.mult, None, [b_aneg], [b_aneg])

    def norm_tile(xt, bxt, n, A_ap, S_ap, dst_fn, dst_bufs, scr, prm_bufs):
        xsq, rstd, tmp, bscr = scr["xsq"], scr["rstd"], scr["tmp"], scr["b"]
        ACT(xsq[:, :, 0:n], xt, AF.Square, bxt, bscr)
        bi = 6
        for k in range(8):
            MM(PS[bi][:, 0:n], ones_b, xsq[:, k, 0:n], k == 0, k == 7, bscr + [b_const], [b_PS[bi]])
        ACT(rstd[:, 0:n], PS[bi][:, 0:n], AF.Ln, [b_PS[bi], b_eps], bscr, scale=1.0 / D, bias=epsc[:, 0:1])
        ACT(rstd[:, 0:n], rstd[:, 0:n], AF.Exp, bscr, bscr, scale=-0.5)
        for k in range(8):
            TT(tmp[:, 0:n], xt[:, k, :], rstd[:, 0:n], ALU.mult, bxt + bscr, bscr)
            if S_ap is None:
                ACT(dst_fn(k), tmp[:, 0:n], AF.Identity, bscr + prm_bufs, dst_bufs, scale=A_ap(k))
            else:
                ACT(dst_fn(k), tmp[:, 0:n], AF.Identity, bscr + prm_bufs, dst_bufs, scale=A_ap(k),
                    bias=S_ap(k))

    def rg_branch(l, hsrc, hbufs, Tn, is_ctx, with_output):
        pvec, b_pvec = cur["pvec"], cur["b_pvec"]
        arena_reset()
        S = [af32([128, T]) for _ in range(5)]
        bS = [Buf("S%d" % i) for i in range(5)]
        ubf = abf([128, T])
        b_ubf = Buf("ubf")
        obf = abf([128, T])
        b_obf = Buf("obf")
        raw, u, r, ig, hf = S
        b_raw, b_u, b_r, b_i, b_hf = bS
        ic = 1 if is_ctx else 0
        for cc in range(4):
            def ev_raw(ps, bps, t0, n, tt):
                CP(raw[:, t0:t0 + n], ps, [bps], [b_raw], eng=cpeng[0])
            proj_cm(l, OFF["rg_x"] + cc * 128, 128, hsrc, hbufs, Tn, ev_raw)
            cw = lambda k: pvec[:, PV["rgcw"] + cc * 4 + k:PV["rgcw"] + cc * 4 + k + 1]
            cb = pvec[:, PV["rgcb"] + cc:PV["rgcb"] + cc + 1]
            TS(u[:, 0:Tn], raw[:, 0:Tn], cw(2), cb, ALU.mult, ALU.add, [b_raw, b_pvec], [b_u])
            STT(u[:, 2:Tn], raw[:, 0:Tn - 2], cw(0), u[:, 2:Tn], ALU.mult, ALU.add, [b_raw, b_pvec, b_u], [b_u])
            STT(u[:, 1:Tn], raw[:, 0:Tn - 1], cw(1), u[:, 1:Tn], ALU.mult, ALU.add, [b_raw, b_pvec, b_u], [b_u])
            STT(u[:, 0:Tn - 1], raw[:, 1:Tn], cw(3), u[:, 0:Tn - 1], ALU.mult, ALU.add, [b_raw, b_pvec, b_u], [b_u])
            CP(ubf[:, 0:Tn], u[:, 0:Tn], [b_u], [b_ubf], eng="scalar")
            CUT(1)
            for d in range(2):
                nt = (Tn + 511) // 512
                for tt in range(nt):
                    t0 = tt * 512
                    n = min(512, Tn - t0)
                    for ax, (dst, bdst, bcol) in enumerate(((r, b_r, PV["rgba"]), (ig, b_i, PV["rgbx"]))):
                        bi = 4 + (psrot[0] % 2)
                        psrot[0] += 1
                        MM(PS[bi][:, 0:n], gatew[:, d * 8 + ax * 4 + cc, :], ubf[:, t0:t0 + n], True, True,
                           [b_gatew, b_ubf], [b_PS[bi]])
                        ACT(dst[:, t0:t0 + n], PS[bi][:, 0:n], AF.Sigmoid, [b_PS[bi], b_pvec], [bdst],
                            bias=pvec[:, bcol + d * 4 + cc:bcol + d * 4 + cc + 1])
                CUT(2)
                ACT(r[:, 0:Tn], r[:, 0:Tn], AF.Exp, [b_r, b_cA], [b_r], scale=cAt[:, d * 4 + cc:d * 4 + cc + 1])
                ACT(raw[:, 0:Tn], r[:, 0:Tn], AF.Square, [b_r], [b_raw])
                ACT(raw[:, 0:Tn], raw[:, 0:Tn], AF.Sqrt, [b_raw, b_eps], [b_raw], scale=-1.0, bias=epsc[:, 1:2])
                TT(ig[:, 0:Tn], ig[:, 0:Tn], u[:, 0:Tn], ALU.mult, [b_i, b_u], [b_i])
                TT(ig[:, 0:Tn], ig[:, 0:Tn], raw[:, 0:Tn], ALU.mult, [b_i, b_raw], [b_i])
                CUT(3)
                init = (zero8[:, 0:1] if is_ctx else rgst[:, cc * 2 + d:cc * 2 + d + 1])
                if d == 0:
                    P.op("vector", (lambda o, a, v, i0: (lambda e: e.tensor_tensor_scan(
                        out=o, data0=a, data1=v, initial=i0, op0=ALU.mult, op1=ALU.add)))(
                        hf[:, 0:Tn], r[:, 0:Tn], ig[:, 0:Tn], init), reads=[b_r, b_i, b_rgst, b_eps], writes=[b_hf])
                else:
                    P.op("vector", (lambda o, a, v, i0: (lambda e: e.tensor_tensor_scan(
                        out=o, data0=a, data1=v, initial=i0, op0=ALU.mult, op1=ALU.add)))(
                        raw[:, 0:Tn][:, ::-1], r[:, 0:Tn][:, ::-1], ig[:, 0:Tn][:, ::-1], init),
                        reads=[b_r, b_i, b_rgst, b_eps], writes=[b_raw])
            CUT(4)
            if is_ctx:
                CP(rgst[:, cc * 2:cc * 2 + 1], hf[:, Tn - 1:Tn], [b_hf], [b_rgst], eng="vector")
                CP(rgst[:, cc * 2 + 1:cc * 2 + 2], raw[:, 0:1], [b_raw], [b_rgst], eng="vector")
            if with_output:
                TT(hf[:, 0:Tn], hf[:, 0:Tn], raw[:, 0:Tn], ALU.add, [b_hf, b_raw], [b_hf])

                def ev_g(ps, bps, t0, n, tt):
                    ACT(r[:, t0:t0 + n], ps, AF.Silu, [bps], [b_r])
                proj_cm(l, OFF["rg_g"] + cc * 128, 128, hsrc, hbufs, Tn, ev_g)
                TT(obf[:, 0:Tn], hf[:, 0:Tn], r[:, 0:Tn], ALU.mult, [b_hf, b_r], [b_obf])
                yd = (ycatc_d if is_ctx else ycat_d)
                DMA(yd[cc * 128:(cc + 1) * 128, 0:Tn], obf[:, 0:Tn], [b_obf], [b_ycat[(ic, cc)]], q="gpsimd")

    def sc_branch(l, hsrc, hbufs, Tn, is_ctx):
        pvec, b_pvec = cur["pvec"], cur["b_pvec"]
        arena_reset()
        S = [af32([128, T]) for _ in range(3)]
        bS = [Buf("S%d" % i) for i in range(3)]
        obf = abf([128, T])
        b_obf = Buf("obf")
        s0, s1, s2 = S
        b0, b1, b2 = bS
        ic = 1 if is_ctx else 0
        rowlen = Tn if is_ctx else 64

        def v3(ap):
            return ap.rearrange("p (r c) -> p r c", c=rowlen)
        for cc in range(4):
            def ev_c(ps, bps, t0, n, tt):
                CP(s0[:, t0:t0 + n], ps, [bps], [b0], eng=cpeng[0])
            proj_cm(l, OFF["sc_c"] + cc * 128, 128, hsrc, hbufs, Tn, ev_c)

            def ev_x(ps, bps, t0, n, tt):
                TT(s1[:, t0:t0 + n], ps, s0[:, t0:t0 + n], ALU.mult, [bps, b0], [b1])
            proj_cm(l, OFF["sc_x"] + cc * 128, 128, hsrc, hbufs, Tn, ev_x)
            cw = lambda k: pvec[:, PV["sccw"] + cc * 3 + k:PV["sccw"] + cc * 3 + k + 1]
            TS(s2[:, 0:Tn], s1[:, 0:Tn], cw(1), None, ALU.mult, None, [b1, b_pvec], [b2])
            a2 = v3(s2[:, 0:Tn])
            a1 = v3(s1[:, 0:Tn])
            STT(a2[:, :, 1:rowlen], a1[:, :, 0:rowlen - 1], cw(0), a2[:, :, 1:rowlen], ALU.mult, ALU.add,
                [b1, b2, b_pvec], [b2])
            STT(a2[:, :, 0:rowlen - 1], a1[:, :, 1:rowlen], cw(2), a2[:, :, 0:rowlen - 1], ALU.mult, ALU.add,
                [b1, b2, b_pvec], [b2])

            def ev_b(ps, bps, t0, n, tt):
                TT(s2[:, t0:t0 + n], ps, s2[:, t0:t0 + n], ALU.mult, [bps, b2], [b2])
            proj_cm(l, OFF["sc_b"] + cc * 128, 128, hsrc, hbufs, Tn, ev_b)

            def ev_g(ps, bps, t0, n, tt):
                ACT(s0[:, t0:t0 + n], ps, AF.Silu, [bps], [b0])
            proj_cm(l, OFF["sc_g"] + cc * 128, 128, hsrc, hbufs, Tn, ev_g)
            TT(obf[:, 0:Tn], s2[:, 0:Tn], s0[:, 0:Tn], ALU.mult, [b2, b0], [b_obf])
            yd = (ycatc_d if is_ctx else ycat_d)
            DMA(yd[(4 + cc) * 128:(5 + cc) * 128, 0:Tn], obf[:, 0:Tn], [b_obf], [b_ycat[(ic, 4 + cc)]], q="gpsimd")

    def fn_branch(l, hsrc, hbufs, Tn, is_ctx):
        arena_reset()
        ic = 1 if is_ctx else 0
        ntt = Tn // 128
        fold = not is_ctx
        Hh = Tn // 2
        nth = Hh // 128
        PQ = [abf([128, (nth + 1) if fold else ntt, 256]) for _ in range(4)]
        b_PQ = [Buf("PQ%d" % g) for g in range(4)]
        SG = [abf([128, Tn]) for _ in range(4)]
        b_SG = [Buf("SG%d" % g) for g in range(4)]
        ost = [abf([128, 256]) for _ in range(2)]
        b_ost = [Buf("ost0"), Buf("ost1")]
        vtpos = apos[0]
        VT = abf([128, Tn])
        b_VT = Buf("VT")
        if fold:
            VP = abf([128, Hh])
            VM = abf([128, Hh])
            b_VF = Buf("VF")
        for g in range(4):
            def ev_v(ps, bps, t0, n, tt):
                CP(VT[:, t0:t0 + n], ps, [bps], [b_VT], eng=cpeng[0])
            proj_cm(l, OFF["fn_x"] + g * 128, 128, hsrc, hbufs, Tn, ev_v)
            if fold:
                rev = VT[:, Hh + 1:Tn][:, ::-1]
                CP(VP[:, 0:1], VT[:, 0:1], [b_VT], [b_VF], eng="vector")
                MEMSET(VM[:, 0:1], 0.0, [b_VF])
                TT(VP[:, 1:Hh], VT[:, 1:Hh], rev, ALU.add, [b_VT], [b_VF])
                TT(VM[:, 1:Hh], VT[:, 1:Hh], rev, ALU.subtract, [b_VT], [b_VF])
                for t4 in range(0, nth, 2):
                    bi = 4 + (psrot[0] % 2)
                    psrot[0] += 1
                    for j in range(2):
                        tt = t4 + j
                        MM(PS[bi][:, j * 256:j * 256 + 128], VP[:, tt * 128:(tt + 1) * 128], cgsg[:, 0:128], True, True,
                           [b_VF, b_const], [b_PS[bi]])
                        MM(PS[bi][:, j * 256 + 128:(j + 1) * 256], VM[:, tt * 128:(tt + 1) * 128], cgsg[:, 128:256], True,
                           True, [b_VF, b_const], [b_PS[bi]])
                    CP(PQ[g][:, t4:t4 + 2, :], PS[bi][:, 0:512].rearrange("p (a b) -> p a b", b=256),
                       [b_PS[bi]], [b_PQ[g]], eng=("scalar" if g % 2 else "vector"))
                bi = 4 + (psrot[0] % 2)
                psrot[0] += 1
                MM(PS[bi][:, 0:128], VT[:, Hh:Hh + 128], cgsg[:, 0:128], True, True, [b_VT, b_const], [b_PS[bi]])
                CP(PQ[g][:, nth, 0:128], PS[bi][:, 0:128], [b_PS[bi]], [b_PQ[g]], eng=("scalar" if g % 2 else "vector"))

                def ev_g2(ps, bps, t0, n, tt):
                    ACT(SG[g][:, t0:t0 + n], ps, AF.Silu, [bps], [b_SG[g]])
                proj_cm(l, OFF["fn_g"] + g * 128, 128, hsrc, hbufs, Tn, ev_g2)
                continue
            for t4 in range(0, ntt, 2):
                bi = 4 + (psrot[0] % 2)
                psrot[0] += 1
                nn = min(2, ntt - t4)
                for j in range(nn):
                    tt = t4 + j
                    MM(PS[bi][:, j * 256:(j + 1) * 256], VT[:, tt * 128:(tt + 1) * 128], cgsg, True, True,
                       [b_VT, b_const], [b_PS[bi]])
                CP(PQ[g][:, t4:t4 + nn, :], PS[bi][:, 0:nn * 256].rearrange("p (a b) -> p a b", b=256),
                   [b_PS[bi]], [b_PQ[g]], eng=("scalar" if g % 2 else "vector"))

            def ev_g(ps, bps, t0, n, tt):
                ACT(SG[g][:, t0:t0 + n], ps, AF.Silu, [bps], [b_SG[g]])
            proj_cm(l, OFF["fn_g"] + g * 128, 128, hsrc, hbufs, Tn, ev_g)
        scale = 1.0 / float(np.sqrt(Tn * 128.0))
        nkt = Tn // 256
        if is_ctx:
            dts = [(dft256[:, :, 0:256], dft256[:, :, 256:512])]
            b_dt = [[b_const, b_const]]
        else:
            P.barrier()
            hflat = hT[:].rearrange("p k t -> p (k t)")
            dts = []
            b_dt = []
            for s in range(2):
                c_ap = hflat[:, s * 16384:s * 16384 + 4096].rearrange("p (a b) -> p a b", b=256)
                s_ap = hflat[:, s * 16384 + 8192:s * 16384 + 8192 + 4096].rearrange("p (a b) -> p a b", b=256)
                dts.append((c_ap, s_ap))
                b_dt.append([Buf("dftc%d" % s), Buf("dfts%d" % s)])
        yd = (ycatc_d if is_ctx else ycat_d)
        orot = 0
        if is_ctx:
            c_ap, s_ap = dts[0]
            for g in range(4):
                bi = (psrot[0] % 4)
                psrot[0] += 1
                for tt in range(ntt):
                    MM(PS[bi][:, 0:256], PQ[g][:, tt, 0:128], c_ap[:, tt, :], tt == 0, False,
                       [b_PQ[g], b_const], [b_PS[bi]])
                for tt in range(ntt):
                    MM(PS[bi][:, 0:256], PQ[g][:, tt, 128:256], s_ap[:, tt, :], False, tt == ntt - 1,
                       [b_PQ[g], b_const], [b_PS[bi]])
                o = orot % 2
                orot += 1
                STT(ost[o][:, :], PS[bi][:, 0:256], scale, SG[g][:, 0:256], ALU.mult, ALU.mult,
                    [b_PS[bi], b_SG[g]], [b_ost[o]])
                DMA(yd[(8 + g) * 128:(9 + g) * 128, 0:256], ost[o][:, :], [b_ost[o]], [b_ycat[(ic, 8 + g)]], q="gpsimd")
            return
        apos[0] = vtpos
        Bs = af32([128, 256])
        t1 = af32([128, 256])
        t2 = af32([128, 256])
        b_t = Buf("fn_t")
        ostm = [abf([128, 256]) for _ in range(2)]
        b_ostm = [Buf("ostm0"), Buf("ostm1")]
        o2k = abf([128, 8])
        b_o2k = Buf("o2k")
        prot = 0
        def dft_load(kt):
            s_ = kt % 2
            DMA(dts[s_][0], dftc_d[kt], [], [b_dt[s_][0]])
            DMA(dts[s_][1], dfts_d[kt], [], [b_dt[s_][1]])
        dft_load(0)
        for kt in range(Tn // 512):
            s = kt % 2
            k0 = kt * 256
            c_ap, s_ap = dts[s]
            if kt + 1 < Tn // 512:
                dft_load(kt + 1)
            for g in range(4):
                ba, bb = ((0, 1) if prot % 2 == 0 else (2, 3))
                prot += 1
                for tt in range(nth):
                    MM(PS[ba][:, 0:256], PQ[g][:, tt, 0:128], c_ap[:, tt, :], tt == 0, False,
                       [b_PQ[g], b_dt[s][0]], [b_PS[ba]])
                MM(PS[ba][:, 0:256], PQ[g][0:1, nth, 0:128], altrow[0:1, :], False, True, [b_PQ[g], b_const], [b_PS[ba]])
                for tt in range(nth):
                    MM(PS[bb][:, 0:256], PQ[g][:, tt, 128:256], s_ap[:, tt, :], tt == 0, tt == nth - 1,
                       [b_PQ[g], b_dt[s][1]], [b_PS[bb]])
                CP(Bs, PS[bb][:, 0:256], [b_PS[bb]], [b_t], eng="scalar")
                TT(t1, PS[ba][:, 0:256], Bs, ALU.add, [b_PS[ba], b_t], [b_t])
                TT(t2, PS[ba][:, 0:256], Bs, ALU.subtract, [b_PS[ba], b_t], [b_t])
                o = orot % 2
                orot += 1
                STT(ost[o][:, :], t1, scale, SG[g][:, k0:k0 + 256], ALU.mult, ALU.mult, [b_t, b_SG[g]], [b_ost[o]])
                DMA(yd[(8 + g) * 128:(9 + g) * 128, k0:k0 + 256], ost[o][:, :], [b_ost[o]], [b_ycat[(ic, 8 + g)]],
                    q="gpsimd")
                j0 = 1 if kt == 0 else 0
                lo = Tn - k0 - 255
                hi = Tn - k0 - j0 + 1
                wd = 256 - j0
                STT(ostm[o][:, 0:wd][:, ::-1], t2[:, j0:256], scale, SG[g][:, lo:hi][:, ::-1], ALU.mult, ALU.mult,
                    [b_t, b_SG[g]], [b_ostm[o]])
                DMA(yd[(8 + g) * 128:(9 + g) * 128, lo:hi], ostm[o][:, 0:wd], [b_ostm[o]], [b_ycat[(ic, 8 + g)]],
                    q="gpsimd")
        half = Tn // 2
        for g in range(4):
            for tt in range(nth):
                MM(PS[0][:, 2 * g:2 * g + 2], PQ[g][:, tt, 0:128], alt2, tt == 0, False, [b_PQ[g], b_const],
                   [b_PS[0]])
            MM(PS[0][:, 2 * g:2 * g + 2], PQ[g][0:1, nth, 0:128], ones_b[0:1, 0:2], False, True, [b_PQ[g], b_const],
               [b_PS[0]])
        for g in range(4):
            STT(o2k[:, 2 * g:2 * g + 1], PS[0][:, 2 * g:2 * g + 1], scale, SG[g][:, half:half + 1], ALU.mult, ALU.mult,
                [b_PS[0], b_SG[g]], [b_o2k])
        for g in range(4):
            DMA(yd[(8 + g) * 128:(9 + g) * 128, half:half + 1], o2k[:, 2 * g:2 * g + 1], [b_o2k], [b_ycat[(ic, 8 + g)]],
                q="gpsimd", slow=True)

    def ssd_branch(l, hsrc, hbufs, Tn, is_ctx, with_output):
        pvec, b_pvec = cur["pvec"], cur["b_pvec"]
        arena_reset()
        ic = 1 if is_ctx else 0
        NC = Tn // 128
        NW = NC * 16
        xs_tok = abf([128, NC, 512])
        b_xs = Buf("xs_tok")
        B_tok = abf([128, NC, 128])
        b_Bt = Buf("B_tok")
        BTm = [abf([128, Tn]) for _ in range(2)]
        b_BT = Buf("BT")
        CT = abf([128, Tn])
        b_CT = Buf("CT")
        b_wz = Buf("wz")
        CS = af32([128, NW])
        BIAS = af32([128, NW])
        ECS = af32([128, NW])
        WEND = af32([128, NW])
        DEC2 = af32([128, NC * 8])
        b_dt = Buf("dtstuff")
        mark = apos[0]
        s0pos = apos[0]
        DTt = af32([128, NW])
        LNDT = af32([128, NW])
        sc1 = af32([128, NW])
        sc2 = af32([128, NW])
        b_scr = Buf("dtscr")
        apos[0] = s0pos
        s0 = af32([128, Tn])
        s1 = af32([128, Tn])
        b_s0 = Buf("ssd_s0")
        b_s1 = Buf("ssd_s1")
        xbf = arena[:, s0pos // 4:s0pos // 4 + Tn // 2].bitcast(BF16)
        b_xbf = b_s0

        def v16(ap):
            return ap.rearrange("p (c q) -> p c q", q=16)
        s = wbrot[0]
        wbrot[0] ^= 1
        load_w(win_d[l].rearrange("(k p) c -> p k c", p=128), 0, 8, OFF["ssd_dt"], 16, wbf[s][:, :, 0:16],
               [b_wbf[s]])
        for c in range(NC):
            hb = [hbufs[min(c // 4, len(hbufs) - 1)]]
            for k in range(8):
                MM(PS[0][:, c * 16:(c + 1) * 16], hsrc[:, k, c * 128:(c + 1) * 128], wbf[s][:, k, 0:16], k == 0, k == 7,
                   [b_wbf[s]] + hb, [b_PS[0]])
        TT(v16(DTt), v16(PS[0][:, 0:NW]), rowb[:, RB["dtb"]:RB["dtb"] + 16].unsqueeze(1).broadcast_to([128, NC, 16]),
           ALU.add, [b_PS[0], b_rowb], [b_scr])
        CUT(9)
        softplus_tile(DTt, DTt, NW, sc1, sc2, [b_scr], [b_scr], [b_scr])
        ACT(LNDT, DTt, AF.Ln, [b_scr], [b_scr])
        CUT(10)
        TT(v16(sc1), v16(DTt), aneg[:, :].unsqueeze(1).broadcast_to([128, NC, 16]), ALU.mult, [b_scr, b_aneg], [b_scr])
        CUT(101)
        MM(PS[1][:, 0:NW], tri_f, sc1, True, True, [b_scr, b_const], [b_PS[1]])
        MM(PS[2][:, 0:NW], tri_b, sc1, True, True, [b_scr, b_const], [b_PS[2]])
        CP(v16(CS)[:, :, 0:8], v16(PS[1][:, 0:NW])[:, :, 0:8], [b_PS[1]], [b_dt], eng="vector")
        CP(v16(CS)[:, :, 8:16], v16(PS[2][:, 0:NW])[:, :, 8:16], [b_PS[2]], [b_dt], eng="vector")
        CUT(102)
        TT(BIAS, LNDT, CS, ALU.subtract, [b_scr, b_dt], [b_dt])
        ACT(ECS, CS, AF.Exp, [b_dt], [b_dt])
        CUT(103)
        MM(PS[1][:, 0:NW], ident_f[:, 127:128].broadcast_to([128, 128]), CS, True, True, [b_dt, b_const], [b_PS[1]])
        CUT(1031)
        MM(PS[2][:, 0:NW], ident_f[:, 0:1].broadcast_to([128, 128]), CS, True, True, [b_dt, b_const], [b_PS[2]])
        CUT(1032)
        CP(sc1, PS[1][:, 0:NW], [b_PS[1]], [b_scr], eng="vector")
        CP(DTt, PS[2][:, 0:NW], [b_PS[2]], [b_scr], eng="vector")
        TT(v16(sc2)[:, :, 0:8], v16(sc1)[:, :, 0:8], v16(BIAS)[:, :, 0:8], ALU.add, [b_scr, b_dt], [b_scr])
        TT(v16(sc2)[:, :, 8:16], v16(DTt)[:, :, 8:16], v16(BIAS)[:, :, 8:16], ALU.add, [b_scr, b_dt], [b_scr])
        CUT(104)
        ACT(WEND, sc2, AF.Exp, [b_scr], [b_dt])
        CUT(105)
        d2 = DEC2.rearrange("p (c d h) -> p c d h", d=2, h=4)
        for d in range(2):
            srcd = sc1 if d == 0 else DTt
            pv = srcd.rearrange("p (c d h) -> p c d h", d=2, h=8)
            ACT(d2[0:64, :, d, :], pv[0:64, :, d, 0:4], AF.Exp, [b_scr], [b_dt])
            ACT(d2[64:128, :, d, :], pv[64:128, :, d, 4:8], AF.Exp, [b_scr], [b_dt])
        CUT(11)
        P.barrier()
        MEMSET(BTm[0][64:128, 0:Tn], 0.0, [b_BT])
        MEMSET(BTm[1][0:64, 0:Tn], 0.0, [b_BT])
        for cch in range(6):
            def ev_raw(ps, bps, t0, n, tt):
                CP(s0[:, t0:t0 + n], ps, [bps], [b_s0], eng=cpeng[0])
            proj_cm(l, OFF["ssd_xbc"] + cch * 128, 128, hsrc, hbufs, Tn, ev_raw)
            cw = lambda k: pvec[:, PV["sdcw"] + cch * 4 + k:PV["sdcw"] + cch * 4 + k + 1]
            cb = pvec[:, PV["sdcb"] + cch:PV["sdcb"] + cch + 1]
            TS(s1[:, 0:Tn], s0[:, 0:Tn], cw(2), None, ALU.mult, None, [b_s0, b_pvec], [b_s1])
            STT(s1[:, 2:Tn], s0[:, 0:Tn - 2], cw(0), s1[:, 2:Tn], ALU.mult, ALU.add, [b_s0, b_s1, b_pvec], [b_s1])
            STT(s1[:, 1:Tn], s0[:, 0:Tn - 1], cw(1), s1[:, 1:Tn], ALU.mult, ALU.add, [b_s0, b_s1, b_pvec], [b_s1])
            STT(s1[:, 0:Tn - 1], s0[:, 1:Tn], cw(3), s1[:, 0:Tn - 1], ALU.mult, ALU.add, [b_s0, b_s1, b_pvec], [b_s1])
            if cch < 5:
                dst, bdst = xbf, b_xbf
            else:
                dst, bdst = CT, b_CT
            ACT(dst[:, 0:Tn], s1[:, 0:Tn], AF.Silu, [b_s1, b_pvec], [bdst], bias=cb)
            if cch == 4:
                CP(BTm[0][0:64, 0:Tn], xbf[0:64, 0:Tn], [b_xbf], [b_BT], eng="vector")
                CP(BTm[1][64:128, 0:Tn], xbf[64:128, 0:Tn], [b_xbf], [b_BT], eng="vector")
            if cch < 5:
                for c8 in range(0, NC, 8):
                    nn = min(8, NC - c8)
                    for j in range(nn):
                        c = c8 + j
                        TR(PSB[:, j * 128:(j + 1) * 128], dst[:, c * 128:(c + 1) * 128], ident_b, [bdst, b_const],
                           [b_PSB])
                    src = PSB[:, 0:nn * 128].rearrange("p (a b) -> p a b", b=128)
                    if cch < 4:
                        CP(xs_tok[:, c8:c8 + nn, cch * 128:(cch + 1) * 128], src, [b_PSB], [b_xs],
                           eng=("scalar" if cch % 2 else "vector"))
                    else:
                        CP(B_tok[:, c8:c8 + nn, :], src, [b_PSB], [b_Bt], eng="vector")
        CUT(12)
        P.barrier()
        apos[0] = mark
        wz = abf([128, 8, 512])
        if with_output:
            for q4 in range(4):
                load_w(win_d[l].rearrange("(k p) c -> p k c", p=128), 0, 8, OFF["ssd_z"] + q4 * 128, 128,
                       wz[:, :, q4 * 128:(q4 + 1) * 128], [b_wz])
        Wt = [abf([128, 8, 128]) for _ in range(3)]
        b_W = [Buf("W0"), Buf("W1"), Buf("W2")]
        BW = [abf([128, 4, 2, 64]) for _ in range(2)]
        b_BW = [Buf("BW0"), Buf("BW1")]
        hTs = [af32([128, 4, 128]) for _ in range(2)]
        b_hTs = [Buf("hTs0"), Buf("hTs1")]
        hTbf = [abf([128, 8, 64]) for _ in range(2)]
        b_hTbf = [Buf("hTbf0"), Buf("hTbf1")]
        HEB = [abf([128, 8, 64]) for _ in range(2)]
        b_HEB = [Buf("HEB0"), Buf("HEB1")]
        for d in range(2):
            MEMSET(hTbf[d][:], 0.0, [b_hTbf[d]])
        YA = af32([128, 512])
        ybpos = apos[0]
        YB = af32([128, 512])
        ZS = af32([128, 512])
        b_YA, b_YB, b_ZS = Buf("YA"), Buf("YB"), Buf("ZS")
        ybf32 = YB
        b_y32 = b_YB
        tmpst = arena[:, ybpos // 4:ybpos // 4 + 512].rearrange("p (a b) -> p a b", b=128)
        b_tmpst = b_YB
        ybf = abf([128, 512])
        b_ybf = Buf("ybf")
        yT = [abf([128, 4, 128])] * 2
        b_yT = [Buf("yT0")] * 2
        SSt = af32([128, 4])
        b_SS = Buf("SS")
        STs = abf([128, 256])
        b_STs = Buf("STs")

        def init_state(d):
            if is_ctx:
                MEMSET(hTs[d][:], 0.0, [b_hTs[d]])
            else:
                CP(hTs[d][:], ssdh0[d][:], [b_ssdh0[d]], [b_hTs[d]], eng="vector")

        def make_hTbf(d):
            CP(hTbf[d][0:64, 0:4, :], hTs[d][0:64, :, 0:64], [b_hTs[d]], [b_hTbf[d]], eng="scalar")
            CP(hTbf[d][64:128, 4:8, :], hTs[d][64:128, :, 64:128], [b_hTs[d]], [b_hTbf[d]], eng="scalar")

        def state_update(d, c):
            for g in range(2):
                col = c * 16 + d * 8 + g * 4
                TT(BW[d][:, :, g, :], B_tok[:, c, g * 64:(g + 1) * 64].unsqueeze(1).broadcast_to([128, 4, 64]),
                   WEND[:, col:col + 4].unsqueeze(2).broadcast_to([128, 4, 64]), ALU.mult, [b_Bt, b_dt], [b_BW[d]])
            xv = xs_tok[:, c, :].rearrange("p (g hh n) -> p hh g n", g=2, hh=4)
            for hh in range(4):
                MM(PS[6][:, hh * 128:(hh + 1) * 128], BW[d][:, hh, :, :], xv[:, hh, :, :], True, True,
                   [b_BW[d], b_xs], [b_PS[6]])
            TT(tmpst[:], hTs[d][:], d2[:, c, d, :].unsqueeze(2).broadcast_to([128, 4, 128]), ALU.mult,
               [b_hTs[d], b_dt], [b_tmpst])
            TT(hTs[d][:], tmpst[:], PS[6][:, :].rearrange("p (a b) -> p a b", b=128), ALU.add,
               [b_tmpst, b_PS[6]], [b_hTs[d]])

        init_state(1)
        for c in range(NC - 1, -1, -1):
            if with_output:
                make_hTbf(1)
                DMA(heb_d[c].rearrange("p (a b) -> p a b", b=64), hTbf[1][:], [b_hTbf[1]], [b_heb_d[c]], q="gpsimd")
            state_update(1, c)
        if is_ctx:
            CP(ssdh0[1][:], hTs[1][:], [b_hTs[1]], [b_ssdh0[1]], eng="vector")
        CUT(13)
        init_state(0)

        def wgen(c, WA, bA, WB, bB):
            for d, (Wd, bW) in enumerate(((WA, bA), (WB, bB))):
                for g in range(2):
                    bi = (d * 2 + g) % 2
                    MM(PS[bi][:, :], ident_b, mask_bf[d], True, False, [b_const], [b_PS[bi]])
                    for hh in range(4):
                        col = c * 16 + d * 8 + g * 4 + hh
                        MM(PS[bi][:, hh * 128:(hh + 1) * 128], CS[:, col:col + 1].broadcast_to([128, 128]), ident_f,
                           False, hh == 3, [b_dt, b_const], [b_PS[bi]])
                    for hh in range(4):
                        col = c * 16 + d * 8 + g * 4 + hh
                        ACT(Wd[:, g * 4 + hh, :], PS[bi][:, hh * 128:(hh + 1) * 128], AF.Exp, [b_PS[bi], b_dt],
                            [bW], bias=BIAS[:, col:col + 1])

        def v64(ap):
            return ap.rearrange("p (h n) -> p h n", n=64)
        if with_output:
            DMA(HEB[0][:], heb_d[0].rearrange("p (a b) -> p a b", b=64), [b_heb_d[0]], [b_HEB[0]])
            wgen(0, Wt[0], b_W[0], Wt[1], b_W[1])
        for c in range(NC):
            if with_output:
                hs = c % 2
                WA, bA = Wt[c % 3], b_W[c % 3]
                WB, bB = Wt[(c + 1) % 3], b_W[(c + 1) % 3]
                WN, bN = Wt[(c + 2) % 3], b_W[(c + 2) % 3]
                if c + 1 < NC:
                    DMA(HEB[1 - hs][:], heb_d[c + 1].rearrange("p (a b) -> p a b", b=64), [b_heb_d[c + 1]],
                        [b_HEB[1 - hs]])
                make_hTbf(0)
                tok = slice(c * 128, (c + 1) * 128)
                TT(WA[:], WA[:], WB[:], ALU.add, [bA, bB], [bA])
                for g in range(2):
                    MM(PS[2][:, g * 128:(g + 1) * 128], BTm[g][:, tok], CT[:, tok], True, True, [b_BT, b_CT], [b_PS[2]])
                CP(STs[:, :], PS[2][:, 0:256], [b_PS[2]], [b_STs], eng="scalar")
                for g in range(2):
                    TT(WA[:, g * 4:(g + 1) * 4, :], WA[:, g * 4:(g + 1) * 4, :],
                       STs[:, g * 128:(g + 1) * 128].unsqueeze(1).broadcast_to([128, 4, 128]), ALU.mult,
                       [bA, b_STs], [bA])
                for h in range(8):
                    MM(PS[3][:, h * 64:(h + 1) * 64], WA[:, h, :], xs_tok[:, c, h * 64:(h + 1) * 64], True, True,
                       [bA, b_xs], [b_PS[3]])
                MM(PS[4][:, :], CT[:, tok], hTbf[0][:].rearrange("p a b -> p (a b)"), True, True, [b_CT, b_hTbf[0]],
                   [b_PS[4]])
                MM(PS[5][:, :], CT[:, tok], HEB[hs][:].rearrange("p a b -> p (a b)"), True, True, [b_CT, b_HEB[hs]],
                   [b_PS[5]])
                hb = [hbufs[min(c // 4, len(hbufs) - 1)]]
                for k in range(8):
                    MM(PS[2][:, :], hsrc[:, k, tok], wz[:, k, :], k == 0, k == 7, [b_wz] + hb, [b_PS[2]])
            state_update(0, c)
            if with_output:
                if c + 1 < NC:
                    wgen(c + 1, WB, bB, WN, bN)
                ACT(ZS, PS[2][:, :], AF.Silu, [b_PS[2]], [b_ZS])
                TT(v64(YA), v64(PS[4][:, :]), ECS[:, c * 16:c * 16 + 8].unsqueeze(2).broadcast_to([128, 8, 64]), ALU.mult,
                   [b_PS[4], b_dt], [b_YA])
                TT(v64(ybf32), v64(PS[5][:, :]), ECS[:, c * 16 + 8:c * 16 + 16].unsqueeze(2).broadcast_to([128, 8, 64]),
                   ALU.mult, [b_PS[5], b_dt], [b_y32])
                TT(YA, YA, ybf32, ALU.add, [b_YA, b_y32], [b_YA])
                TT(YA, YA, PS[3][:, :], ALU.add, [b_YA, b_PS[3]], [b_YA])
                TT(ybf32, xs_tok[:, c, :], rowb[:, RB["dbc"]:RB["dbc"] + 512], ALU.mult, [b_xs, b_rowb], [b_y32])
                TT(YA, YA, ybf32, ALU.add, [b_YA, b_y32], [b_YA])
                TT(YA, YA, ZS, ALU.mult, [b_YA, b_ZS], [b_YA])
                ACT(ybf32, YA, AF.Square, [b_YA], [b_y32, b_SS], accum=SSt[:, 0:1])
                ACT(SSt[:, 1:2], SSt[:, 0:1], AF.Ln, [b_SS, b_eps], [b_SS], scale=1.0 / 512.0, bias=epsc[:, 0:1])
                ACT(SSt[:, 2:3], SSt[:, 1:2], AF.Exp, [b_SS], [b_SS], scale=-0.5)
                STT(ybf, YA, SSt[:, 2:3], rowb[:, RB["snw"]:RB["snw"] + 512], ALU.mult, ALU.mult,
                    [b_YA, b_SS, b_rowb], [b_ybf])
                for j in range(4):
                    TR(PSB[:, j * 128:(j + 1) * 128], ybf[:, j * 128:(j + 1) * 128], ident_b, [b_ybf, b_const], [b_PSB])
                ys = c % 2
                CP(yT[ys][:], PSB[:, 0:512].rearrange("p (a b) -> p a b", b=128), [b_PSB], [b_yT[ys]], eng="scalar")
                yd = (ycatc_d if is_ctx else ycat_d)
                DMA(yd[12 * 128:16 * 128, tok].rearrange("(j p) t -> p j t", p=128), yT[ys][:], [b_yT[ys]],
                    [b_ycat[(ic, 12 + j)] for j in range(4)], q="gpsimd")
        if is_ctx:
            CP(ssdh0[0][:], hTs[0][:], [b_hTs[0]], [b_ssdh0[0]], eng="vector")

    def outproj_phase(l, Tn, is_ctx, x_src_d, x_dst_d, last):
        arena_reset()
        modt, b_modt = modts[l], b_modts[l]
        ic = 1 if is_ctx else 0
        col = 1 if is_ctx else 0
        NT = min(512, Tn)
        wo = abf([128, 16, D])
        b_wo = Buf("wo")
        yc, xsqs = [], []
        for _ in range(2):
            pos = apos[0]
            yc.append(abf([128, 16, NT]))
            xsqs.append(arena[:, pos // 4:pos // 4 + 8 * NT // 2].bitcast(BF16).rearrange("p (a b) -> p a b", b=NT))
        b_yc = [Buf("yc0"), Buf("yc1")]
        xt = [af32([128, 8, NT]) for _ in range(2)]
        b_xt = [Buf("xt0"), Buf("xt1")]
        xn = xt
        b_xn = b_xt
        rstd_t = af32([128, NT])
        tmp_t = af32([128, NT])
        b_nscr = Buf("nscr")
        scrs = [dict(xsq=xsqs[i], rstd=rstd_t, tmp=tmp_t, b=[b_yc[i], b_nscr]) for i in range(2)]
        ofin = xt
        b_ofin = b_xt
        wsrc = wout_d[l].rearrange("(k p) c -> p k c", p=128)
        for half in range(2):
            for m in range(8):
                load_w(wsrc, half * 8, 8, m * 128, 128, wo[:, half * 8:(half + 1) * 8, m * 128:(m + 1) * 128], [b_wo])
        yd = (ycatc_d if is_ctx else ycat_d)
        dst_h = hTc if is_ctx else hT
        dst_b = b_hTc if is_ctx else b_hT
        def issue_loads(tt):
            s = tt % 2
            tok = slice(tt * NT, (tt + 1) * NT)
            DMA(yc[s][:], yd[:, tok].rearrange("(k p) t -> p k t", p=128), [b_ycat[(ic, j)] for j in range(16)],
                [b_yc[s]])
            x1b = [b_x1[(tt * NT) // 256 + i] for i in range(max(1, NT // 256))]
            xrd = x1b if (x_src_d is x1_d) else []
            DMA(xt[s][:], x_src_d[:, tok].rearrange("(k p) t -> p k t", p=128), xrd, [b_xt[s]])
        for tt in range(Tn // NT):
            s = tt % 2
            tok = slice(tt * NT, (tt + 1) * NT)
            x1b = [b_x1[(tt * NT) // 256 + i] for i in range(max(1, NT // 256))]
            issue_loads(tt)
            for m in range(8):
                bi = psrot[0] % 4
                psrot[0] += 1
                for k in range(16):
                    MM(PS[bi][:, 0:NT], wo[:, k, m * 128:(m + 1) * 128], yc[s][:, k, :], k == 0, k == 15,
                       [b_wo, b_yc[s]], [b_PS[bi]])
                STT(xn[s][:, m, :], PS[bi][:, 0:NT], modt[:, 16 + m, col:col + 1], xt[s][:, m, :], ALU.mult, ALU.add,
                    [b_PS[bi], b_modt, b_xt[s]], [b_xn[s]])
            if x_dst_d is not None:
                DMA(x_dst_d[:, tok].rearrange("(k p) t -> p k t", p=128), xn[s][:], [b_xn[s]], x1b, q="gpsimd")
            if last:
                pv = pvecs[l]
                norm_tile(xn[s][:], [b_xn[s]], NT, (lambda k: pv[:, PV["fnw"] + k:PV["fnw"] + k + 1]), None,
                          (lambda k, s=s: ofin[s][:, k, :]), [b_ofin[s]], scrs[s], [b_pvecs[l]])
                DMA(outT_d[:, tok].rearrange("(k p) t -> p k t", p=128), ofin[s][:], [b_ofin[s]], [b_out], q="gpsimd")
            else:
                An, Mn = Amods[l + 1], modts[l + 1]
                norm_tile(xn[s][:], [b_xn[s]], NT, (lambda k: An[:, k, col:col + 1]), (lambda k: Mn[:, k, col:col + 1]),
                          (lambda k, tok=tok: dst_h[:, k, tok]), [dst_b[min(tt * NT // 512, len(dst_b) - 1)]], scrs[s],
                          [b_Amods[l + 1], b_modts[l + 1]])

    def input_norm(x_src_d, Tn, col, dst, dbufs):
        arena_reset()
        NT = 256
        xt = [af32([128, 8, NT]) for _ in range(2)]
        b_xt = [Buf("xt0"), Buf("xt1")]
        scr = dict(xsq=abf([128, 8, NT]), rstd=af32([128, NT]), tmp=af32([128, NT]), b=[Buf("nscr")])
        A0, M0 = Amods[0], modts[0]
        def ld(tt):
            DMA(xt[tt % 2][:], x_src_d[:, tt * NT:(tt + 1) * NT].rearrange("(k p) t -> p k t", p=128), [], [b_xt[tt % 2]])
        for tt in range(Tn // NT):
            s = tt % 2
            tok = slice(tt * NT, (tt + 1) * NT)
            ld(tt)
            norm_tile(xt[s][:], [b_xt[s]], NT, (lambda k: A0[:, k, col:col + 1]), (lambda k: M0[:, k, col:col + 1]),
                      (lambda k, tok=tok: dst[:, k, tok]), [dbufs[min(tt // 2, len(dbufs) - 1)]], scr,
                      [b_Amods[0], b_modts[0]])

    phases = []
    for l in range(NL):
        phases.append(("ada%d" % l, (lambda l=l: ada_setup(l))))
    phases.append(("nctx", lambda: input_norm(ctxT_d, TC, 1, hTc, b_hTc)))
    phases.append(("nx", lambda: input_norm(xT_d, T, 0, hT, b_hT)))
    for l in range(NL):
        last = (l == NL - 1)
        phases.append(("par%d" % l, (lambda l=l: layer_params(l))))
        phases.append(("crg%d" % l, (lambda l=l, last=last: rg_branch(l, hTc, b_hTc, TC, True, not last))))
        phases.append(("cssd%d" % l, (lambda l=l, last=last: ssd_branch(l, hTc, b_hTc, TC, True, not last))))
        if not last:
            phases.append(("csc%d" % l, (lambda l=l: sc_branch(l, hTc, b_hTc, TC, True))))
            phases.append(("cfn%d" % l, (lambda l=l: fn_branch(l, hTc, b_hTc, TC, True))))
            phases.append(("cout%d" % l, (lambda l=l: outproj_phase(l, TC, True, ctxT_d, None, False))))
        phases.append(("rg%d" % l, (lambda l=l: rg_branch(l, hT, b_hT, T, False, True))))
        phases.append(("sc%d" % l, (lambda l=l: sc_branch(l, hT, b_hT, T, False))))
        phases.append(("ssd%d" % l, (lambda l=l: ssd_branch(l, hT, b_hT, T, False, True))))
        phases.append(("fn%d" % l, (lambda l=l: fn_branch(l, hT, b_hT, T, False))))
        phases.append(("out%d" % l, (lambda l=l, last=last: outproj_phase(
            l, T, False, (xT_d if l == 0 else x1_d), (None if last else x1_d), last))))
    stop = (dbg or {}).get("stop")
    skip = (dbg or {}).get("skip", ())
    for name, fn in phases:
        if name not in skip:
            try:
                fn()
            except _Cut:
                break
        if stop == name:
            break
    if dbg:
        P.barrier()
        bd = Buf("dbgd")
        DMA(dbg_hT, hT[:], [], [bd], semkey="d_dbg")
        DMA(dbg_hTc, hTc[:], [], [bd], semkey="d_dbg")
        for l in range(NL):
            DMA(dbg_mod[l], modts[l][:].rearrange("p a b -> p (a b)"), [], [bd], semkey="d_dbg")
        DMA(dbg_st[:, 0:8], rgst[:], [], [bd], semkey="d_dbg")
        for d in range(2):
            DMA(dbg_st[:, 8 + d * 512:8 + (d + 1) * 512], ssdh0[d][:].rearrange("p a b -> p (a b)"), [], [bd],
                semkey="d_dbg")
    print("ops per engine:", {e: len(P.ops[e]) for e in ENGS})
    P.finalize()
    return nc


_BF = ml_dtypes.bfloat16


def _host_consts():
    p = np.arange(128)
    ident = np.eye(128, dtype=np.float32)
    tri_f = (p[:, None] <= p[None, :]).astype(np.float32)
    tri_b = (p[:, None] >= p[None, :]).astype(np.float32)
    cf32 = np.concatenate([ident, tri_f, tri_b], axis=1).astype(np.float32)
    ones = np.ones((128, 128), np.float32)
    j = p[:, None]
    i = p[None, :]
    mf = np.where(i < j, NEG, 0.0).astype(np.float32)
    mb = np.where(i > j, NEG, 0.0).astype(np.float32)
    ang = 2.0 * np.pi * ((p[:, None] * p[None, :]) % 128) / 128.0
    cg = np.cos(ang)
    sg = -np.sin(ang)
    alt = np.where(p % 2 == 0, 1.0, -1.0)[:, None] * np.ones((1, 2))
    altr = np.ones((128, 1)) * np.where(np.arange(256) % 2 == 0, 1.0, -1.0)[None, :]
    cbf = np.concatenate([ident, ones, np.tile(mf, (1, 4)), np.tile(mb, (1, 4)), cg, sg, alt, altr], axis=1).astype(_BF)
    t = np.arange(T, dtype=np.int64)
    kt = (t[:T // 2, None] * t[None, :T // 2]) % T
    angT = (2.0 * np.pi / T) * kt.astype(np.float64)
    def tile_dft(m):
        return np.ascontiguousarray(m.astype(np.float32).reshape(16, 128, T // 512, 256).transpose(2, 1, 0, 3)).astype(_BF)
    dftc = tile_dft(np.cos(angT))
    dfts = tile_dft(np.sin(angT))
    t2 = np.arange(TC, dtype=np.int64)
    a2 = (2.0 * np.pi / TC) * ((t2[:, None] * t2[None, :]) % TC).astype(np.float64)
    c2 = np.cos(a2).astype(np.float32).reshape(2, 128, TC).transpose(1, 0, 2)
    s2 = np.sin(a2).astype(np.float32).reshape(2, 128, TC).transpose(1, 0, 2)
    dft256 = np.concatenate([c2, s2], axis=2).astype(_BF)
    return dict(cf32=cf32, cbf=cbf, dftc=dftc, dfts=dfts, dft256=np.ascontiguousarray(dft256))


_CONSTS = None
_NC = None


def _fm(v, nchunk):
    return np.ascontiguousarray(np.asarray(v, np.float32).reshape(nchunk, 128).T)


def kernel(x, c, ctx, c_ctx, ada_w, ada_b, norm_w, w_in, w_out, rg_conv_w, rg_conv_b, rg_gate_a_w, rg_gate_a_b,
           rg_gate_x_w, rg_gate_x_b, rg_lambda, sc_conv_w, ssd_conv_w, ssd_conv_b, ssd_dt_bias, ssd_a_log, ssd_d,
           ssd_norm_w, final_norm_w):
    global _CONSTS, _NC
    f = lambda a: np.asarray(a, dtype=np.float32)
    x, c, ctx, c_ctx = f(x), f(c), f(ctx), f(c_ctx)
    if _CONSTS is None:
        _CONSTS = _host_consts()
    if _NC is None:
        _NC = build_program()
    pvec = np.zeros((NL, 128, NPV), np.float32)
    rowb = np.zeros((NL, 128, NRB), np.float32)
    gbd = np.zeros((NL, 128, 16, 128), np.float32)
    for l in range(NL):
        pvec[l, :, PV["nw"]:PV["nw"] + 8] = _fm(f(norm_w)[l], 8)
        pvec[l, :, PV["adab"]:PV["adab"] + 24] = _fm(f(ada_b)[l], 24)
        for cc in range(4):
            for k in range(4):
                pvec[l, :, PV["rgcw"] + cc * 4 + k] = f(rg_conv_w)[l, k, cc * 128:(cc + 1) * 128]
            pvec[l, :, PV["rgcb"] + cc] = f(rg_conv_b)[l, cc * 128:(cc + 1) * 128]
            for d in range(2):
                pvec[l, :, PV["rgba"] + d * 4 + cc] = f(rg_gate_a_b)[l, d, cc * 128:(cc + 1) * 128]
                pvec[l, :, PV["rgbx"] + d * 4 + cc] = f(rg_gate_x_b)[l, d, cc * 128:(cc + 1) * 128]
                pvec[l, :, PV["rglam"] + d * 4 + cc] = f(rg_lambda)[l, d, cc * 128:(cc + 1) * 128]
                for ax, W in enumerate((f(rg_gate_a_w), f(rg_gate_x_w))):
                    idx = d * 8 + ax * 4 + cc
                    gbd[l, 0:64, idx, 0:64] = W[l, d, 2 * cc]
                    gbd[l, 64:128, idx, 64:128] = W[l, d, 2 * cc + 1]
            for k in range(3):
                pvec[l, :, PV["sccw"] + cc * 3 + k] = f(sc_conv_w)[l, k, cc * 128:(cc + 1) * 128]
        for cch in range(6):
            for k in range(4):
                pvec[l, :, PV["sdcw"] + cch * 4 + k] = f(ssd_conv_w)[l, k, cch * 128:(cch + 1) * 128]
            pvec[l, :, PV["sdcb"] + cch] = f(ssd_conv_b)[l, cch * 128:(cch + 1) * 128]
        pvec[l, :, PV["fnw"]:PV["fnw"] + 8] = _fm(f(final_norm_w), 8)
        rowb[l, :, RB["dtb"]:RB["dtb"] + 16] = f(ssd_dt_bias)[l].reshape(1, 16)
        rowb[l, :, RB["alog"]:RB["alog"] + 16] = f(ssd_a_log)[l].reshape(1, 16)
        rowb[l, :, RB["dbc"]:RB["dbc"] + 512] = np.repeat(f(ssd_d)[l], 64)[None, :]
        rowb[l, :, RB["snw"]:RB["snw"] + 512] = f(ssd_norm_w)[l][None, :]
    shared = dict(ada_w=f(ada_w), w_in=f(w_in), w_out=f(w_out), pvec=pvec, rowb=rowb, gbd=gbd, **_CONSTS)
    in_maps = []
    for core in range(NCORES):
        b = core % 4
        cc = np.stack([_fm(c[b], 8), _fm(c_ctx, 8)], axis=2)
        m = dict(shared)
        m["xT"] = np.ascontiguousarray(x[b].T)
        m["ctxT"] = np.ascontiguousarray(ctx[b].T)
        m["cc"] = np.ascontiguousarray(cc)
        in_maps.append(m)
    res = run_bass_kernel_spmd(_NC, in_maps, core_ids=list(range(NCORES)))
    out = np.stack([np.ascontiguousarray(res.results[b]["outT"].T) for b in range(4)], axis=0)
    return out.astype(np.float32)
```

```python
import os
from contextlib import ExitStack
import numpy as np
import ml_dtypes
import concourse.bass as bass
import concourse.mybir as mybir
from concourse.bass_utils import run_bass_kernel_spmd

F32 = mybir.dt.float32
BF16 = mybir.dt.bfloat16
AF = mybir.ActivationFunctionType
ALU = mybir.AluOpType

D = 1024
T = 4096
TC = 256
NL = 2
NCORES = 8
EPS = 1e-6
OFF = dict(rg_x=0, ssd_xbc=512, ssd_dt=1280, rg_g=1296, ssd_z=1808, sc_b=2320, sc_c=2832,
           sc_x=3344, sc_g=3856, fn_x=4368, fn_g=4880)
PV = dict(nw=0, adab=8, rgcw=32, rgcb=48, rgba=52, rgbx=60, rglam=68, sccw=76, sdcw=88, sdcb=112, fnw=118)
NPV = 128
RB = dict(dtb=0, alog=16, dbc=32, snw=544)
NRB = 1056
ARENA_BYTES = 108544
NEG = -30000.0

ENGS = ("tensor", "vector", "scalar", "gpsimd", "sync")
SAME_ENGINE_RAW = True
STORE_Q = "sync"


class Buf:
    __slots__ = ("name", "last_w", "reads")

    def __init__(self, name):
        self.name = name
        self.last_w = None
        self.reads = []


class Prog:
    def __init__(self, nc):
        self.nc = nc
        self.es = ExitStack()
        self.ops = {e: [] for e in ENGS}
        self.seq = {e: 0 for e in ENGS}
        self.waited = {e: {} for e in ENGS}
        self.sems = {}
        self.dmacount = {}

    def sbuf(self, name, shape, dtype):
        return self.es.enter_context(self.nc.sbuf_tensor("sb_" + name, list(shape), dtype))

    def psum(self, name, shape, dtype=F32):
        return self.es.enter_context(self.nc.psum_tensor(name, list(shape), dtype))

    def sem(self, key):
        if key not in self.sems:
            self.sems[key] = self.es.enter_context(self.nc.semaphore("s_" + str(key)))
        return self.sems[key]

    def _collect(self, eng, reads, writes, force=False):
        waits = {}

        def need(ev, is_raw):
            if ev is None:
                return
            key, val = ev
            if key == eng and not force and not (is_raw and SAME_ENGINE_RAW):
                return
            if self.waited[eng].get(key, 0) >= val:
                return
            if waits.get(key, 0) < val:
                waits[key] = val

        for b in reads:
            need(b.last_w, True)
        for b in writes:
            need(b.last_w, False)
            for r in b.reads:
                need(r, False)
        for k, v in waits.items():
            self.waited[eng][k] = v
        return waits

    def op(self, eng, fn, reads=(), writes=()):
        waits = self._collect(eng, reads, writes)
        self.seq[eng] += 1
        ev = (eng, self.seq[eng])
        for b in reads:
            b.reads.append(ev)
        for b in writes:
            b.last_w = ev
            b.reads = []
        self.sem(eng)
        for k in waits:
            self.sem(k)
        self.ops[eng].append((waits, fn, (eng, 1)))
        return ev

    def dma(self, eng, fn, reads=(), writes=(), semkey=None):
        if semkey is None:
            semkey = "d_" + (writes[0].name if writes else reads[0].name)
        waits = self._collect(eng, reads, writes, force=True)
        self.dmacount[semkey] = self.dmacount.get(semkey, 0) + 16
        ev = (semkey, self.dmacount[semkey])
        for b in reads:
            b.reads.append(ev)
        for b in writes:
            b.last_w = ev
            b.reads = []
        self.sem(semkey)
        for k in waits:
            self.sem(k)
        self.ops[eng].append((waits, fn, (semkey, 16)))
        return ev

    def barrier(self):
        tgt = {}
        for e in ENGS:
            if self.seq[e] > 0:
                tgt[e] = self.seq[e]
        for k, v in self.dmacount.items():
            tgt[k] = v
        for e in ENGS:
            waits = {}
            for k, v in tgt.items():
                if k == e:
                    continue
                if self.waited[e].get(k, 0) >= v:
                    continue
                waits[k] = v
                self.waited[e][k] = v
            if waits:
                self.seq[e] += 1
                self.sem(e)
                self.ops[e].append((waits, (lambda eng: eng.nop()), (e, 1)))

    def finalize(self):
        fw = {}
        for e in ENGS:
            if e != "sync" and self.seq[e] > 0:
                fw[e] = self.seq[e]
        for k, v in self.dmacount.items():
            fw[k] = max(fw.get(k, 0), v)
        nc = self.nc
        sems = self.sems
        ops = self.ops
        with nc.Block() as block:
            def run(engname):
                def body(eng):
                    for waits, fn, (ik, iv) in ops[engname]:
                        for k, v in waits.items():
                            eng.wait_ge(sems[k], v)
                        ins = fn(eng)
                        ins.then_inc(sems[ik], iv)
                    if engname == "sync":
                        for k, v in fw.items():
                            eng.wait_ge(sems[k], v)
                return body
            block.tensor(run("tensor"))
            block.vector(run("vector"))
            block.scalar(run("scalar"))
            block.gpsimd(run("gpsimd"))
            block.sync(run("sync"))
        self.es.close()


class _Cut(Exception):
    pass


def build_program(dbg=None):
    nc = bass.Bass("TRN2", target_bir_lowering=False)
    P = Prog(nc)
    cutn = int((dbg or {}).get("cut", 0))

    def CUT(n):
        if cutn == n:
            raise _Cut()

    def din(name, shape, dt=F32):
        return nc.dram_tensor(name, list(shape), dt, kind="ExternalInput").ap()

    xT_d = din("xT", [D, T])
    ctxT_d = din("ctxT", [D, TC])
    cc_d = din("cc", [128, 8, 2])
    adaw_d = din("ada_w", [NL, D, 3 * D])
    win_d = din("w_in", [NL, D, 5392])
    wout_d = din("w_out", [NL, 2 * D, D])
    pvec_d = din("pvec", [NL, 128, NPV])
    rowb_d = din("rowb", [NL, 128, NRB])
    gbd_d = din("gbd", [NL, 128, 16, 128])
    cf32_d = din("cf32", [128, 384])
    cbf_d = din("cbf", [128, 1794], BF16)
    dftc_d = din("dftc", [T // 512, 128, 16, 256], BF16)
    dfts_d = din("dfts", [T // 512, 128, 16, 256], BF16)
    dft256_d = din("dft256", [128, 2, 512], BF16)
    outT_d = nc.dram_tensor("outT", [D, T], F32, kind="ExternalOutput").ap()
    skind = dict(kind="ExternalOutput") if dbg else {}
    x1_d = nc.dram_tensor("x1s", [D, T], F32, **skind).ap()
    ycat_d = nc.dram_tensor("ycat", [2 * D, T], BF16, **skind).ap()
    ycatc_d = nc.dram_tensor("ycatc", [2 * D, TC], BF16, **skind).ap()
    if dbg:
        dbg_hT = nc.dram_tensor("dbg_hT", [128, 8, T], BF16, kind="ExternalOutput").ap()
        dbg_hTc = nc.dram_tensor("dbg_hTc", [128, 8, TC], BF16, kind="ExternalOutput").ap()
        dbg_mod = nc.dram_tensor("dbg_mod", [NL, 128, 48], F32, kind="ExternalOutput").ap()
        dbg_st = nc.dram_tensor("dbg_st", [128, 8 + 2 * 512], F32, kind="ExternalOutput").ap()
    heb_d = nc.dram_tensor("hebd", [T // 128, 128, 512], BF16).ap()
    b_x1 = [Buf("x1_%d" % i) for i in range(T // 256)]
    b_ycat = {}
    for ic in (0, 1):
        for c16 in range(16):
            b_ycat[(ic, c16)] = Buf("yc%d_%d" % (ic, c16))
    b_heb_d = [Buf("hebd%d" % i) for i in range(T // 128)]
    b_out = Buf("outd")

    hT = P.sbuf("hT", [128, 8, T], BF16)
    b_hT = [Buf("hT%d" % i) for i in range(T // 512)]
    hTc = P.sbuf("hTc", [128, 8, TC], BF16)
    b_hTc = [Buf("hTc")]
    wst = [P.sbuf("wst%d" % i, [128, 8, 128], F32) for i in range(2)]
    b_wst = [Buf("wst%d" % i) for i in range(2)]
    wbf = [P.sbuf("wbf%d" % i, [128, 8, 128], BF16) for i in range(2)]
    b_wbf = [Buf("wbf%d" % i) for i in range(2)]
    gatew = P.sbuf("gatew", [128, 16, 128], BF16)
    b_gatew = Buf("gatew")
    pvecs = [P.sbuf("pvec%d" % i, [128, NPV], F32) for i in range(NL)]
    b_pvecs = [Buf("pvec%d" % i) for i in range(NL)]
    rowb = P.sbuf("rowb", [128, NRB], F32)
    b_rowb = Buf("rowb")
    cf32 = P.sbuf("cf32", [128, 384], F32)
    cbf = P.sbuf("cbf", [128, 1794], BF16)
    b_const = Buf("const")
    dft256 = P.sbuf("dft256", [128, 2, 512], BF16)
    ccs = P.sbuf("ccs", [128, 8, 2], F32)
    b_ccs = Buf("ccs")
    modts = [P.sbuf("modt%d" % i, [128, 24, 2], F32) for i in range(NL)]
    b_modts = [Buf("modt%d" % i) for i in range(NL)]
    Amods = [P.sbuf("Amod%d" % i, [128, 8, 2], F32) for i in range(NL)]
    b_Amods = [Buf("Amod%d" % i) for i in range(NL)]
    cAt = P.sbuf("cAt", [128, 8], F32)
    b_cA = Buf("cA")
    smallt = P.sbuf("smallt", [128, 64], F32)
    b_small = Buf("small")
    epsc = P.sbuf("epsc", [128, 4], F32)
    b_eps = Buf("epsc")
    rgst = P.sbuf("rgst", [128, 8], F32)
    b_rgst = Buf("rgst")
    zero8 = P.sbuf("zero8", [128, 8], F32)
    ssdh0 = [P.sbuf("ssdh0_%d" % d, [128, 4, 128], F32) for d in range(2)]
    b_ssdh0 = [Buf("ssdh0_%d" % d) for d in range(2)]
    aneg = P.sbuf("aneg", [128, 16], F32)
    b_aneg = Buf("aneg")
    arena = P.sbuf("arena", [128, ARENA_BYTES // 4], F32)

    ident_f = cf32[:, 0:128]
    tri_f = cf32[:, 128:256]
    tri_b = cf32[:, 256:384]
    ident_b = cbf[:, 0:128]
    ones_b = cbf[:, 128:256]
    mask_bf = [cbf[:, 256:768], cbf[:, 768:1280]]
    cgsg = cbf[:, 1280:1536]
    alt2 = cbf[:, 1536:1538]
    altrow = cbf[:, 1538:1794]

    PS = [P.psum("ps%d" % i, [128, 512], F32) for i in range(7)]
    b_PS = [Buf("ps%d" % i) for i in range(7)]
    PSB = P.psum("psb", [128, 1024], BF16)
    b_PSB = Buf("psb")

    tog = [0]
    cpeng = ["vector"]

    def ACT(out, in_, func, r, w, scale=None, bias=None, accum=None):
        kw = {}
        if scale is not None:
            kw["scale"] = scale
        if bias is not None:
            kw["bias"] = bias
        if accum is not None:
            kw["accum_out"] = accum
        P.op("scalar", lambda e: e.activation(out=out, in_=in_, func=func, **kw), reads=r, writes=w)

    def TT(out, a, b, op, r, w, eng="vector"):
        P.op(eng, lambda e: e.tensor_tensor(out=out, in0=a, in1=b, op=op), reads=r, writes=w)

    def STT(out, in0, scalar, in1, op0, op1, r, w):
        P.op("vector", lambda e: e.scalar_tensor_tensor(out=out, in0=in0, scalar=scalar, in1=in1,
                                                         op0=op0, op1=op1), reads=r, writes=w)

    def TS(out, in0, s1, s2, op0, op1, r, w, eng="vector"):
        if op1 is None:
            P.op(eng, lambda e: e.tensor_scalar(out=out, in0=in0, scalar1=s1, scalar2=None, op0=op0),
                 reads=r, writes=w)
        else:
            P.op(eng, lambda e: e.tensor_scalar(out=out, in0=in0, scalar1=s1, scalar2=s2, op0=op0, op1=op1),
                 reads=r, writes=w)

    def MM(out, lhsT, rhs, start, stop, r, w):
        P.op("tensor", lambda e: e.matmul(out=out, lhsT=lhsT, rhs=rhs, start=start, stop=stop,
                                          skip_group_check=True), reads=r, writes=w)

    def TR(out, in_, ident, r, w):
        P.op("tensor", lambda e: e.transpose(out=out, in_=in_, identity=ident), reads=r, writes=w)

    def CP(out, in_, r, w, eng=None):
        if eng is None:
            tog[0] ^= 1
            eng = "scalar" if tog[0] else "vector"
        if eng == "scalar":
            P.op("scalar", lambda e: e.activation(out=out, in_=in_, func=AF.Identity), reads=r, writes=w)
        else:
            P.op(eng, lambda e: e.tensor_copy(out=out, in_=in_), reads=r, writes=w)

    def MEMSET(ap, val, w, eng="vector"):
        P.op(eng, lambda e: e.memset(ap, val), writes=w)

    def DMA(out, in_, r, w, q="sync", semkey=None, slow=False):
        if semkey is None and q == "gpsimd" and r:
            semkey = "dw_" + r[0].name
        if q == "gpsimd":
            q = STORE_Q
        if slow:
            P.dma(q, lambda e: e.dma_start(out=out, in_=in_, allow_slow_non_contiguous=True), reads=r, writes=w,
                  semkey=semkey)
            return
        P.dma(q, lambda e: e.dma_start(out=out, in_=in_), reads=r, writes=w, semkey=semkey)

    apos = [0]

    def arena_reset():
        P.barrier()
        apos[0] = 0

    def af32(shape):
        n = int(np.prod(shape[1:]))
        w0 = apos[0] // 4
        apos[0] += n * 4
        assert apos[0] <= ARENA_BYTES, ("arena overflow", apos[0])
        ap = arena[:, w0:w0 + n]
        if len(shape) == 3:
            ap = ap.rearrange("p (a b) -> p a b", b=shape[2])
        elif len(shape) == 4:
            ap = ap.rearrange("p (a b c) -> p a b c", b=shape[2], c=shape[3])
        return ap

    def abf(shape):
        n = int(np.prod(shape[1:]))
        assert n % 2 == 0
        w0 = apos[0] // 4
        apos[0] += n * 2
        assert apos[0] <= ARENA_BYTES, ("arena overflow", apos[0])
        ap = arena[:, w0:w0 + n // 2].bitcast(BF16)
        if len(shape) == 3:
            ap = ap.rearrange("p (a b) -> p a b", b=shape[2])
        elif len(shape) == 4:
            ap = ap.rearrange("p (a b c) -> p a b c", b=shape[2], c=shape[3])
        return ap

    DMA(cf32[:], cf32_d[:, :], [], [b_const], semkey="d_const")
    DMA(cbf[:], cbf_d[:, :], [], [b_const], semkey="d_const")
    DMA(dft256[:], dft256_d[:, :, :], [], [b_const], semkey="d_const")
    DMA(ccs[:], cc_d[:, :, :], [], [b_ccs], semkey="d_const")
    MEMSET(epsc[:, 0:1], EPS, [b_eps])
    MEMSET(epsc[:, 1:2], 1.0, [b_eps])
    MEMSET(epsc[:, 2:4], 0.0, [b_eps])
    MEMSET(zero8[:], 0.0, [b_eps])
    ACT(ccs[:], ccs[:], AF.Silu, [b_ccs], [b_ccs])

    wrot = [0]

    def load_w(src3d, k0, nk, col0, ncols, dst_ap, dst_bufs):
        s = wrot[0]
        wrot[0] ^= 1
        DMA(wst[s][:, 0:nk, 0:ncols], src3d[:, k0:k0 + nk, col0:col0 + ncols], [], [b_wst[s]])
        CP(dst_ap, wst[s][:, 0:nk, 0:ncols], [b_wst[s]], dst_bufs)

    wbrot = [0]

    def load_win(l, col0, ncols):
        s = wbrot[0]
        wbrot[0] ^= 1
        src = win_d[l].rearrange("(k p) c -> p k c", p=128)
        load_w(src, 0, 8, col0, ncols, wbf[s][:, :, 0:ncols], [b_wbf[s]])
        return wbf[s], b_wbf[s]

    psrot = [0]

    def proj_cm(l, col0, ncols, hsrc, hbufs, Tn, evac, banks=(0, 1, 2, 3)):
        w, bw = load_win(l, col0, ncols)
        tog[0] ^= 1
        cpeng[0] = "scalar" if tog[0] else "vector"
        nt = (Tn + 511) // 512
        for tt in range(nt):
            t0 = tt * 512
            n = min(512, Tn - t0)
            bi = banks[psrot[0] % len(banks)]
            psrot[0] += 1
            hb = [hbufs[min(tt, len(hbufs) - 1)]]
            for k in range(8):
                MM(PS[bi][0:ncols, 0:n], w[:, k, 0:ncols], hsrc[:, k, t0:t0 + n], k == 0, k == 7,
                   [bw] + hb, [b_PS[bi]])
            evac(PS[bi][0:ncols, 0:n], b_PS[bi], t0, n, tt)

    def softplus_tile(dst, src, n, scr1, scr2, r, w, bscr):
        TS(scr1, src, 30.0, None, ALU.min, None, r, bscr)
        ACT(scr1, scr1, AF.Exp, bscr, bscr)
        TS(scr2, scr1, -0.25, 1.0 / 3.0, ALU.mult, ALU.add, bscr, bscr)
        TT(scr2, scr2, scr1, ALU.mult, bscr, bscr)
        TS(scr2, scr2, -0.5, None, ALU.add, None, bscr, bscr)
        TT(scr2, scr2, scr1, ALU.mult, bscr, bscr)
        TS(scr2, scr2, 1.0, None, ALU.add, None, bscr, bscr)
        TT(scr2, scr2, scr1, ALU.mult, bscr, bscr)
        ACT(dst, scr1, AF.Ln, bscr + [b_eps], w, bias=epsc[:, 1:2])
        TS(scr1, scr1, 0.05, None, ALU.is_gt, None, bscr, bscr)
        TT(dst, dst, scr2, ALU.subtract, w + bscr, w)
        TT(dst, dst, scr1, ALU.mult, w + bscr, w)
        TT(dst, dst, scr2, ALU.add, w + bscr, w)

    cur = {}

    def set_layer(l):
        cur["pvec"] = pvecs[l]
        cur["b_pvec"] = b_pvecs[l]
        cur["modt"] = modts[l]
        cur["b_modt"] = b_modts[l]
        cur["Amod"] = Amods[l]
        cur["b_Amod"] = b_Amods[l]

    def ada_setup(l):
        pvec, b_pvec = pvecs[l], b_pvecs[l]
        modt, b_modt, Amod, b_Amod = modts[l], b_modts[l], Amods[l], b_Amods[l]
        DMA(pvec[:], pvec_d[l], [], [b_pvec])
        arena_reset()
        aw = [af32([128, 8, 512]) for _ in range(2)]
        b_aw = [Buf("aw0"), Buf("aw1")]
        mrow = af32([128, 3 * D])[0:2, :]
        b_mrow = Buf("mrow")
        src = adaw_d[l].rearrange("(k p) c -> p k c", p=128)
        for jb in range(6):
            s = jb % 2
            DMA(aw[s], src[:, :, jb * 512:(jb + 1) * 512], [], [b_aw[s]])
            bi = jb % 2
            for k in range(8):
                MM(PS[bi][0:2, :], ccs[:, k, :], aw[s][:, k, :], k == 0, k == 7, [b_aw[s], b_ccs], [b_PS[bi]])
            CP(mrow[:, jb * 512:(jb + 1) * 512], PS[bi][0:2, :], [b_PS[bi]], [b_mrow], eng="vector")
        for j in range(24):
            TR(PS[2][:, j * 2:j * 2 + 2], mrow[:, j * 128:(j + 1) * 128], ident_f[0:2, 0:2], [b_mrow, b_const], [b_PS[2]])
        adab = pvec[:, PV["adab"]:PV["adab"] + 24]
        TT(modt[:], PS[2][:, 0:48].rearrange("p (j c) -> p j c", c=2),
           adab.unsqueeze(2).broadcast_to([128, 24, 2]), ALU.add, [b_PS[2], b_pvec], [b_modt])
        nw = pvec[:, PV["nw"]:PV["nw"] + 8]
        TS(Amod[:], modt[:, 8:16, :], 1.0, None, ALU.add, None, [b_modt], [b_Amod])
        TT(Amod[:], Amod[:], nw.unsqueeze(2).broadcast_to([128, 8, 2]), ALU.mult, [b_Amod, b_pvec], [b_Amod])

    def layer_params(l):
        set_layer(l)
        pvec, b_pvec = pvecs[l], b_pvecs[l]
        P.barrier()
        DMA(rowb[:], rowb_d[l], [], [b_rowb])
        gsrc = gbd_d[l]
        for half in range(2):
            load_w(gsrc, half * 8, 8, 0, 128, gatew[:, half * 8:(half + 1) * 8, :], [b_gatew])
        lam = pvec[:, PV["rglam"]:PV["rglam"] + 8]
        TS(smallt[:, 0:8], lam, -1.0, None, ALU.mult, None, [b_pvec], [b_small])
        softplus_tile(cAt[:], smallt[:, 0:8], 8, smallt[:, 8:16], smallt[:, 16:24], [b_small], [b_cA], [b_small])
        TS(cAt[:], cAt[:], -8.0, None, ALU.mult, None, [b_cA], [b_cA])
        ACT(aneg[:], rowb[:, RB["alog"]:RB["alog"] + 16], AF.Exp, [b_rowb], [b_aneg])
        TS(aneg[:], aneg[:], -1.0, None, ALU.mult, None, [b_aneg], [b_aneg])

    def norm_tile(xt, bxt, n, A_ap, S_ap, dst_fn, dst_bufs, scr, prm_bufs):
        xsq, rstd, tmp, bscr = scr["xsq"], scr["rstd"], scr["tmp"], scr["b"]
        ACT(xsq[:, :, 0:n], xt, AF.Square, bxt, bscr)
        bi = 6
        for k in range(8):
            MM(PS[bi][:, 0:n], ones_b, xsq[:, k, 0:n], k == 0, k == 7, bscr + [b_const], [b_PS[bi]])
        ACT(rstd[:, 0:n], PS[bi][:, 0:n], AF.Ln, [b_PS[bi], b_eps], bscr, scale=1.0 / D, bias=epsc[:, 0:1])
        ACT(rstd[:, 0:n], rstd[:, 0:n], AF.Exp, bscr, bscr, scale=-0.5)
        for k in range(8):
            TT(tmp[:, 0:n], xt[:, k, :], rstd[:, 0:n], ALU.mult, bxt + bscr, bscr)
            if S_ap is None:
                ACT(dst_fn(k), tmp[:, 0:n], AF.Identity, bscr + prm_bufs, dst_bufs, scale=A_ap(k))
            else:
                ACT(dst_fn(k), tmp[:, 0:n], AF.Identity, bscr + prm_bufs, dst_bufs, scale=A_ap(k),
                    bias=S_ap(k))

    def rg_branch(l, hsrc, hbufs, Tn, is_ctx, with_output):
        pvec, b_pvec = cur["pvec"], cur["b_pvec"]
        arena_reset()
        S = [af32([128, T]) for _ in range(5)]
        bS = [Buf("S%d" % i) for i in range(5)]
        ubf = abf([128, T])
        b_ubf = Buf("ubf")
        obf = abf([128, T])
        b_obf = Buf("obf")
        raw, u, r, ig, hf = S
        b_raw, b_u, b_r, b_i, b_hf = bS
        ic = 1 if is_ctx else 0
        for cc in range(4):
            def ev_raw(ps, bps, t0, n, tt):
                CP(raw[:, t0:t0 + n], ps, [bps], [b_raw], eng=cpeng[0])
            proj_cm(l, OFF["rg_x"] + cc * 128, 128, hsrc, hbufs, Tn, ev_raw)
            cw = lambda k: pvec[:, PV["rgcw"] + cc * 4 + k:PV["rgcw"] + cc * 4 + k + 1]
            cb = pvec[:, PV["rgcb"] + cc:PV["rgcb"] + cc + 1]
            TS(u[:, 0:Tn], raw[:, 0:Tn], cw(2), cb, ALU.mult, ALU.add, [b_raw, b_pvec], [b_u])
            STT(u[:, 2:Tn], raw[:, 0:Tn - 2], cw(0), u[:, 2:Tn], ALU.mult, ALU.add, [b_raw, b_pvec, b_u], [b_u])
            STT(u[:, 1:Tn], raw[:, 0:Tn - 1], cw(1), u[:, 1:Tn], ALU.mult, ALU.add, [b_raw, b_pvec, b_u], [b_u])
            STT(u[:, 0:Tn - 1], raw[:, 1:Tn], cw(3), u[:, 0:Tn - 1], ALU.mult, ALU.add, [b_raw, b_pvec, b_u], [b_u])
            CP(ubf[:, 0:Tn], u[:, 0:Tn], [b_u], [b_ubf], eng="scalar")
            CUT(1)
            for d in range(2):
                nt = (Tn + 511) // 512
                for tt in range(nt):
                    t0 = tt * 512
                    n = min(512, Tn - t0)
                    for ax, (dst, bdst, bcol) in enumerate(((r, b_r, PV["rgba"]), (ig, b_i, PV["rgbx"]))):
                        bi = 4 + (psrot[0] % 2)
                        psrot[0] += 1
                        MM(PS[bi][:, 0:n], gatew[:, d * 8 + ax * 4 + cc, :], ubf[:, t0:t0 + n], True, True,
                           [b_gatew, b_ubf], [b_PS[bi]])
                        ACT(dst[:, t0:t0 + n], PS[bi][:, 0:n], AF.Sigmoid, [b_PS[bi], b_pvec], [bdst],
                            bias=pvec[:, bcol + d * 4 + cc:bcol + d * 4 + cc + 1])
                CUT(2)
                ACT(r[:, 0:Tn], r[:, 0:Tn], AF.Exp, [b_r, b_cA], [b_r], scale=cAt[:, d * 4 + cc:d * 4 + cc + 1])
                ACT(raw[:, 0:Tn], r[:, 0:Tn], AF.Square, [b_r], [b_raw])
                ACT(raw[:, 0:Tn], raw[:, 0:Tn], AF.Sqrt, [b_raw, b_eps], [b_raw], scale=-1.0, bias=epsc[:, 1:2])
                TT(ig[:, 0:Tn], ig[:, 0:Tn], u[:, 0:Tn], ALU.mult, [b_i, b_u], [b_i])
                TT(ig[:, 0:Tn], ig[:, 0:Tn], raw[:, 0:Tn], ALU.mult, [b_i, b_raw], [b_i])
                CUT(3)
                init = (zero8[:, 0:1] if is_ctx else rgst[:, cc * 2 + d:cc * 2 + d + 1])
                if d == 0:
                    P.op("vector", (lambda o, a, v, i0: (lambda e: e.tensor_tensor_scan(
                        out=o, data0=a, data1=v, initial=i0, op0=ALU.mult, op1=ALU.add)))(
                        hf[:, 0:Tn], r[:, 0:Tn], ig[:, 0:Tn], init), reads=[b_r, b_i, b_rgst, b_eps], writes=[b_hf])
                else:
                    P.op("vector", (lambda o, a, v, i0: (lambda e: e.tensor_tensor_scan(
                        out=o, data0=a, data1=v, initial=i0, op0=ALU.mult, op1=ALU.add)))(
                        raw[:, 0:Tn][:, ::-1], r[:, 0:Tn][:, ::-1], ig[:, 0:Tn][:, ::-1], init),
                        reads=[b_r, b_i, b_rgst, b_eps], writes=[b_raw])
            CUT(4)
            if is_ctx:
                CP(rgst[:, cc * 2:cc * 2 + 1], hf[:, Tn - 1:Tn], [b_hf], [b_rgst], eng="vector")
                CP(rgst[:, cc * 2 + 1:cc * 2 + 2], raw[:, 0:1], [b_raw], [b_rgst], eng="vector")
            if with_output:
                TT(hf[:, 0:Tn], hf[:, 0:Tn], raw[:, 0:Tn], ALU.add, [b_hf, b_raw], [b_hf])

                def ev_g(ps, bps, t0, n, tt):
                    ACT(r[:, t0:t0 + n], ps, AF.Silu, [bps], [b_r])
                proj_cm(l, OFF["rg_g"] + cc * 128, 128, hsrc, hbufs, Tn, ev_g)
                TT(obf[:, 0:Tn], hf[:, 0:Tn], r[:, 0:Tn], ALU.mult, [b_hf, b_r], [b_obf])
                yd = (ycatc_d if is_ctx else ycat_d)
                DMA(yd[cc * 128:(cc + 1) * 128, 0:Tn], obf[:, 0:Tn], [b_obf], [b_ycat[(ic, cc)]], q="gpsimd")

    def sc_branch(l, hsrc, hbufs, Tn, is_ctx):
        pvec, b_pvec = cur["pvec"], cur["b_pvec"]
        arena_reset()
        S = [af32([128, T]) for _ in range(3)]
        bS = [Buf("S%d" % i) for i in range(3)]
        obf = abf([128, T])
        b_obf = Buf("obf")
        s0, s1, s2 = S
        b0, b1, b2 = bS
        ic = 1 if is_ctx else 0
        rowlen = Tn if is_ctx else 64

        def v3(ap):
            return ap.rearrange("p (r c) -> p r c", c=rowlen)
        for cc in range(4):
            def ev_c(ps, bps, t0, n, tt):
                CP(s0[:, t0:t0 + n], ps, [bps], [b0], eng=cpeng[0])
            proj_cm(l, OFF["sc_c"] + cc * 128, 128, hsrc, hbufs, Tn, ev_c)

            def ev_x(ps, bps, t0, n, tt):
                TT(s1[:, t0:t0 + n], ps, s0[:, t0:t0 + n], ALU.mult, [bps, b0], [b1])
            proj_cm(l, OFF["sc_x"] + cc * 128, 128, hsrc, hbufs, Tn, ev_x)
            cw = lambda k: pvec[:, PV["sccw"] + cc * 3 + k:PV["sccw"] + cc * 3 + k + 1]
            TS(s2[:, 0:Tn], s1[:, 0:Tn], cw(1), None, ALU.mult, None, [b1, b_pvec], [b2])
            a2 = v3(s2[:, 0:Tn])
            a1 = v3(s1[:, 0:Tn])
            STT(a2[:, :, 1:rowlen], a1[:, :, 0:rowlen - 1], cw(0), a2[:, :, 1:rowlen], ALU.mult, ALU.add,
                [b1, b2, b_pvec], [b2])
            STT(a2[:, :, 0:rowlen - 1], a1[:, :, 1:rowlen], cw(2), a2[:, :, 0:rowlen - 1], ALU.mult, ALU.add,
                [b1, b2, b_pvec], [b2])

            def ev_b(ps, bps, t0, n, tt):
                TT(s2[:, t0:t0 + n], ps, s2[:, t0:t0 + n], ALU.mult, [bps, b2], [b2])
            proj_cm(l, OFF["sc_b"] + cc * 128, 128, hsrc, hbufs, Tn, ev_b)

            def ev_g(ps, bps, t0, n, tt):
                ACT(s0[:, t0:t0 + n], ps, AF.Silu, [bps], [b0])
            proj_cm(l, OFF["sc_g"] + cc * 128, 128, hsrc, hbufs, Tn, ev_g)
            TT(obf[:, 0:Tn], s2[:, 0:Tn], s0[:, 0:Tn], ALU.mult, [b2, b0], [b_obf])
            yd = (ycatc_d if is_ctx else ycat_d)
            DMA(yd[(4 + cc) * 128:(5 + cc) * 128, 0:Tn], obf[:, 0:Tn], [b_obf], [b_ycat[(ic, 4 + cc)]], q="gpsimd")

    def fn_branch(l, hsrc, hbufs, Tn, is_ctx):
        arena_reset()
        ic = 1 if is_ctx else 0
        ntt = Tn // 128
        fold = not is_ctx
        Hh = Tn // 2
        nth = Hh // 128
        PQ = [abf([128, (nth + 1) if fold else ntt, 256]) for _ in range(4)]
        b_PQ = [Buf("PQ%d" % g) for g in range(4)]
        SG = [abf([128, Tn]) for _ in range(4)]
        b_SG = [Buf("SG%d" % g) for g in range(4)]
        ost = [abf([128, 256]) for _ in range(2)]
        b_ost = [Buf("ost0"), Buf("ost1")]
        vtpos = apos[0]
        VT = abf([128, Tn])
        b_VT = Buf("VT")
        if fold:
            VP = abf([128, Hh])
            VM = abf([128, Hh])
            b_VF = Buf("VF")
        for g in range(4):
            def ev_v(ps, bps, t0, n, tt):
                CP(VT[:, t0:t0 + n], ps, [bps], [b_VT], eng=cpeng[0])
            proj_cm(l, OFF["fn_x"] + g * 128, 128, hsrc, hbufs, Tn, ev_v)
            if fold:
                rev = VT[:, Hh + 1:Tn][:, ::-1]
                CP(VP[:, 0:1], VT[:, 0:1], [b_VT], [b_VF], eng="vector")
                MEMSET(VM[:, 0:1], 0.0, [b_VF])
                TT(VP[:, 1:Hh], VT[:, 1:Hh], rev, ALU.add, [b_VT], [b_VF])
                TT(VM[:, 1:Hh], VT[:, 1:Hh], rev, ALU.subtract, [b_VT], [b_VF])
                for t4 in range(0, nth, 2):
                    bi = 4 + (psrot[0] % 2)
                    psrot[0] += 1
                    for j in range(2):
                        tt = t4 + j
                        MM(PS[bi][:, j * 256:j * 256 + 128], VP[:, tt * 128:(tt + 1) * 128], cgsg[:, 0:128], True, True,
                           [b_VF, b_const], [b_PS[bi]])
                        MM(PS[bi][:, j * 256 + 128:(j + 1) * 256], VM[:, tt * 128:(tt + 1) * 128], cgsg[:, 128:256], True,
                           True, [b_VF, b_const], [b_PS[bi]])
                    CP(PQ[g][:, t4:t4 + 2, :], PS[bi][:, 0:512].rearrange("p (a b) -> p a b", b=256),
                       [b_PS[bi]], [b_PQ[g]], eng=("scalar" if g % 2 else "vector"))
                bi = 4 + (psrot[0] % 2)
                psrot[0] += 1
                MM(PS[bi][:, 0:128], VT[:, Hh:Hh + 128], cgsg[:, 0:128], True, True, [b_VT, b_const], [b_PS[bi]])
                CP(PQ[g][:, nth, 0:128], PS[bi][:, 0:128], [b_PS[bi]], [b_PQ[g]], eng=("scalar" if g % 2 else "vector"))

                def ev_g2(ps, bps, t0, n, tt):
                    ACT(SG[g][:, t0:t0 + n], ps, AF.Silu, [bps], [b_SG[g]])
                proj_cm(l, OFF["fn_g"] + g * 128, 128, hsrc, hbufs, Tn, ev_g2)
                continue
            for t4 in range(0, ntt, 2):
                bi = 4 + (psrot[0] % 2)
                psrot[0] += 1
                nn = min(2, ntt - t4)
                for j in range(nn):
                    tt = t4 + j
                    MM(PS[bi][:, j * 256:(j + 1) * 256], VT[:, tt * 128:(tt + 1) * 128], cgsg, True, True,
                       [b_VT, b_const], [b_PS[bi]])
                CP(PQ[g][:, t4:t4 + nn, :], PS[bi][:, 0:nn * 256].rearrange("p (a b) -> p a b", b=256),
                   [b_PS[bi]], [b_PQ[g]], eng=("scalar" if g % 2 else "vector"))

            def ev_g(ps, bps, t0, n, tt):
                ACT(SG[g][:, t0:t0 + n], ps, AF.Silu, [bps], [b_SG[g]])
            proj_cm(l, OFF["fn_g"] + g * 128, 128, hsrc, hbufs, Tn, ev_g)
        scale = 1.0 / float(np.sqrt(Tn * 128.0))
        nkt = Tn // 256
        if is_ctx:
            dts = [(dft256[:, :, 0:256], dft256[:, :, 256:512])]
            b_dt = [[b_const, b_const]]
        else:
            P.barrier()
            hflat = hT[:].rearrange("p k t -> p (k t)")
            dts = []
            b_dt = []
            for s in range(2):
                c_ap = hflat[:, s * 16384:s * 16384 + 4096].rearrange("p (a b) -> p a b", b=256)
                s_ap = hflat[:, s * 16384 + 8192:s * 16384 + 8192 + 4096].rearrange("p (a b) -> p a b", b=256)
                dts.append((c_ap, s_ap))
                b_dt.append([Buf("dftc%d" % s), Buf("dfts%d" % s)])
        yd = (ycatc_d if is_ctx else ycat_d)
        orot = 0
        if is_ctx:
            c_ap, s_ap = dts[0]
            for g in range(4):
                bi = (psrot[0] % 4)
                psrot[0] += 1
                for tt in range(ntt):
                    MM(PS[bi][:, 0:256], PQ[g][:, tt, 0:128], c_ap[:, tt, :], tt == 0, False,
                       [b_PQ[g], b_const], [b_PS[bi]])
                for tt in range(ntt):
                    MM(PS[bi][:, 0:256], PQ[g][:, tt, 128:256], s_ap[:, tt, :], False, tt == ntt - 1,
                       [b_PQ[g], b_const], [b_PS[bi]])
                o = orot % 2
                orot += 1
                STT(ost[o][:, :], PS[bi][:, 0:256], scale, SG[g][:, 0:256], ALU.mult, ALU.mult,
                    [b_PS[bi], b_SG[g]], [b_ost[o]])
                DMA(yd[(8 + g) * 128:(9 + g) * 128, 0:256], ost[o][:, :], [b_ost[o]], [b_ycat[(ic, 8 + g)]], q="gpsimd")
            return
        apos[0] = vtpos
        Bs = af32([128, 256])
        t1 = af32([128, 256])
        t2 = af32([128, 256])
        b_t = Buf("fn_t")
        ostm = [abf([128, 256]) for _ in range(2)]
        b_ostm = [Buf("ostm0"), Buf("ostm1")]
        o2k = abf([128, 8])
        b_o2k = Buf("o2k")
        prot = 0
        def dft_load(kt):
            s_ = kt % 2
            DMA(dts[s_][0], dftc_d[kt], [], [b_dt[s_][0]])
            DMA(dts[s_][1], dfts_d[kt], [], [b_dt[s_][1]])
        dft_load(0)
        for kt in range(Tn // 512):
            s = kt % 2
            k0 = kt * 256
            c_ap, s_ap = dts[s]
            if kt + 1 < Tn // 512:
                dft_load(kt + 1)
            for g in range(4):
                ba, bb = ((0, 1) if prot % 2 == 0 else (2, 3))
                prot += 1
                for tt in range(nth):
                    MM(PS[ba][:, 0:256], PQ[g][:, tt, 0:128], c_ap[:, tt, :], tt == 0, False,
                       [b_PQ[g], b_dt[s][0]], [b_PS[ba]])
                MM(PS[ba][:, 0:256], PQ[g][0:1, nth, 0:128], altrow[0:1, :], False, True, [b_PQ[g], b_const], [b_PS[ba]])
                for tt in range(nth):
                    MM(PS[bb][:, 0:256], PQ[g][:, tt, 128:256], s_ap[:, tt, :], tt == 0, tt == nth - 1,
                       [b_PQ[g], b_dt[s][1]], [b_PS[bb]])
                CP(Bs, PS[bb][:, 0:256], [b_PS[bb]], [b_t], eng="scalar")
                TT(t1, PS[ba][:, 0:256], Bs, ALU.add, [b_PS[ba], b_t], [b_t])
                TT(t2, PS[ba][:, 0:256], Bs, ALU.subtract, [b_PS[ba], b_t], [b_t])
                o = orot % 2
                orot += 1
                STT(ost[o][:, :], t1, scale, SG[g][:, k0:k0 + 256], ALU.mult, ALU.mult, [b_t, b_SG[g]], [b_ost[o]])
                DMA(yd[(8 + g) * 128:(9 + g) * 128, k0:k0 + 256], ost[o][:, :], [b_ost[o]], [b_ycat[(ic, 8 + g)]],
                    q="gpsimd")
                j0 = 1 if kt == 0 else 0
                lo = Tn - k0 - 255
                hi = Tn - k0 - j0 + 1
                wd = 256 - j0
                STT(ostm[o][:, 0:wd][:, ::-1], t2[:, j0:256], scale, SG[g][:, lo:hi][:, ::-1], ALU.mult, ALU.mult,
                    [b_t, b_SG[g]], [b_ostm[o]])
                DMA(yd[(8 + g) * 128:(9 + g) * 128, lo:hi], ostm[o][:, 0:wd], [b_ostm[o]], [b_ycat[(ic, 8 + g)]],
                    q="gpsimd")
        half = Tn // 2
        for g in range(4):
            for tt in range(nth):
                MM(PS[0][:, 2 * g:2 * g + 2], PQ[g][:, tt, 0:128], alt2, tt == 0, False, [b_PQ[g], b_const],
                   [b_PS[0]])
            MM(PS[0][:, 2 * g:2 * g + 2], PQ[g][0:1, nth, 0:128], ones_b[0:1, 0:2], False, True, [b_PQ[g], b_const],
               [b_PS[0]])
        for g in range(4):
            STT(o2k[:, 2 * g:2 * g + 1], PS[0][:, 2 * g:2 * g + 1], scale, SG[g][:, half:half + 1], ALU.mult, ALU.mult,
                [b_PS[0], b_SG[g]], [b_o2k])
        for g in range(4):
            DMA(yd[(8 + g) * 128:(9 + g) * 128, half:half + 1], o2k[:, 2 * g:2 * g + 1], [b_o2k], [b_ycat[(ic, 8 + g)]],
                q="gpsimd", slow=True)

    def ssd_branch(l, hsrc, hbufs, Tn, is_ctx, with_output):
        pvec, b_pvec = cur["pvec"], cur["b_pvec"]
        arena_reset()
        ic = 1 if is_ctx else 0
        NC = Tn // 128
        NW = NC * 16
        xs_tok = abf([128, NC, 512])
        b_xs = Buf("xs_tok")
        B_tok = abf([128, NC, 128])
        b_Bt = Buf("B_tok")
        BTm = [abf([128, Tn]) for _ in range(2)]
        b_BT = Buf("BT")
        CT = abf([128, Tn])
        b_CT = Buf("CT")
        b_wz = Buf("wz")
        CS = af32([128, NW])
        BIAS = af32([128, NW])
        ECS = af32([128, NW])
        WEND = af32([128, NW])
        DEC2 = af32([128, NC * 8])
        b_dt = Buf("dtstuff")
        mark = apos[0]
        s0pos = apos[0]
        DTt = af32([128, NW])
        LNDT = af32([128, NW])
        sc1 = af32([128, NW])
        sc2 = af32([128, NW])
        b_scr = Buf("dtscr")
        apos[0] = s0pos
        s0 = af32([128, Tn])
        s1 = af32([128, Tn])
        b_s0 = Buf("ssd_s0")
        b_s1 = Buf("ssd_s1")
        xbf = arena[:, s0pos // 4:s0pos // 4 + Tn // 2].bitcast(BF16)
        b_xbf = b_s0

        def v16(ap):
            return ap.rearrange("p (c q) -> p c q", q=16)
        s = wbrot[0]
        wbrot[0] ^= 1
        load_w(win_d[l].rearrange("(k p) c -> p k c", p=128), 0, 8, OFF["ssd_dt"], 16, wbf[s][:, :, 0:16],
               [b_wbf[s]])
        for c in range(NC):
            hb = [hbufs[min(c // 4, len(hbufs) - 1)]]
            for k in range(8):
                MM(PS[0][:, c * 16:(c + 1) * 16], hsrc[:, k, c * 128:(c + 1) * 128], wbf[s][:, k, 0:16], k == 0, k == 7,
                   [b_wbf[s]] + hb, [b_PS[0]])
        TT(v16(DTt), v16(PS[0][:, 0:NW]), rowb[:, RB["dtb"]:RB["dtb"] + 16].unsqueeze(1).broadcast_to([128, NC, 16]),
           ALU.add, [b_PS[0], b_rowb], [b_scr])
        CUT(9)
        softplus_tile(DTt, DTt, NW, sc1, sc2, [b_scr], [b_scr], [b_scr])
        ACT(LNDT, DTt, AF.Ln, [b_scr], [b_scr])
        CUT(10)
        TT(v16(sc1), v16(DTt), aneg[:, :].unsqueeze(1).broadcast_to([128, NC, 16]), ALU.mult, [b_scr, b_aneg], [b_scr])
        CUT(101)
        MM(PS[1][:, 0:NW], tri_f, sc1, True, True, [b_scr, b_const], [b_PS[1]])
        MM(PS[2][:, 0:NW], tri_b, sc1, True, True, [b_scr, b_const], [b_PS[2]])
        CP(v16(CS)[:, :, 0:8], v16(PS[1][:, 0:NW])[:, :, 0:8], [b_PS[1]], [b_dt], eng="vector")
        CP(v16(CS)[:, :, 8:16], v16(PS[2][:, 0:NW])[:, :, 8:16], [b_PS[2]], [b_dt], eng="vector")
        CUT(102)
        TT(BIAS, LNDT, CS, ALU.subtract, [b_scr, b_dt], [b_dt])
        ACT(ECS, CS, AF.Exp, [b_dt], [b_dt])
        CUT(103)
        MM(PS[1][:, 0:NW], ident_f[:, 127:128].broadcast_to([128, 128]), CS, True, True, [b_dt, b_const], [b_PS[1]])
        CUT(1031)
        MM(PS[2][:, 0:NW], ident_f[:, 0:1].broadcast_to([128, 128]), CS, True, True, [b_dt, b_const], [b_PS[2]])
        CUT(1032)
        CP(sc1, PS[1][:, 0:NW], [b_PS[1]], [b_scr], eng="vector")
        CP(DTt, PS[2][:, 0:NW], [b_PS[2]], [b_scr], eng="vector")
        TT(v16(sc2)[:, :, 0:8], v16(sc1)[:, :, 0:8], v16(BIAS)[:, :, 0:8], ALU.add, [b_scr, b_dt], [b_scr])
        TT(v16(sc2)[:, :, 8:16], v16(DTt)[:, :, 8:16], v16(BIAS)[:, :, 8:16], ALU.add, [b_scr, b_dt], [b_scr])
        CUT(104)
        ACT(WEND, sc2, AF.Exp, [b_scr], [b_dt])
        CUT(105)
        d2 = DEC2.rearrange("p (c d h) -> p c d h", d=2, h=4)
        for d in range(2):
            srcd = sc1 if d == 0 else DTt
            pv = srcd.rearrange("p (c d h) -> p c d h", d=2, h=8)
            ACT(d2[0:64, :, d, :], pv[0:64, :, d, 0:4], AF.Exp, [b_scr], [b_dt])
            ACT(d2[64:128, :, d, :], pv[64:128, :, d, 4:8], AF.Exp, [b_scr], [b_dt])
        CUT(11)
        P.barrier()
        MEMSET(BTm[0][64:128, 0:Tn], 0.0, [b_BT])
        MEMSET(BTm[1][0:64, 0:Tn], 0.0, [b_BT])
        for cch in range(6):
            def ev_raw(ps, bps, t0, n, tt):
                CP(s0[:, t0:t0 + n], ps, [bps], [b_s0], eng=cpeng[0])
            proj_cm(l, OFF["ssd_xbc"] + cch * 128, 128, hsrc, hbufs, Tn, ev_raw)
            cw = lambda k: pvec[:, PV["sdcw"] + cch * 4 + k:PV["sdcw"] + cch * 4 + k + 1]
            cb = pvec[:, PV["sdcb"] + cch:PV["sdcb"] + cch + 1]
            TS(s1[:, 0:Tn], s0[:, 0:Tn], cw(2), None, ALU.mult, None, [b_s0, b_pvec], [b_s1])
            STT(s1[:, 2:Tn], s0[:, 0:Tn - 2], cw(0), s1[:, 2:Tn], ALU.mult, ALU.add, [b_s0, b_s1, b_pvec], [b_s1])
            STT(s1[:, 1:Tn], s0[:, 0:Tn - 1], cw(1), s1[:, 1:Tn], ALU.mult, ALU.add, [b_s0, b_s1, b_pvec], [b_s1])
            STT(s1[:, 0:Tn - 1], s0[:, 1:Tn], cw(3), s1[:, 0:Tn - 1], ALU.mult, ALU.add, [b_s0, b_s1, b_pvec], [b_s1])
            if cch < 5:
                dst, bdst = xbf, b_xbf
            else:
                dst, bdst = CT, b_CT
            ACT(dst[:, 0:Tn], s1[:, 0:Tn], AF.Silu, [b_s1, b_pvec], [bdst], bias=cb)
            if cch == 4:
                CP(BTm[0][0:64, 0:Tn], xbf[0:64, 0:Tn], [b_xbf], [b_BT], eng="vector")
                CP(BTm[1][64:128, 0:Tn], xbf[64:128, 0:Tn], [b_xbf], [b_BT], eng="vector")
            if cch < 5:
                for c8 in range(0, NC, 8):
                    nn = min(8, NC - c8)
                    for j in range(nn):
                        c = c8 + j
                        TR(PSB[:, j * 128:(j + 1) * 128], dst[:, c * 128:(c + 1) * 128], ident_b, [bdst, b_const],
                           [b_PSB])
                    src = PSB[:, 0:nn * 128].rearrange("p (a b) -> p a b", b=128)
                    if cch < 4:
                        CP(xs_tok[:, c8:c8 + nn, cch * 128:(cch + 1) * 128], src, [b_PSB], [b_xs],
                           eng=("scalar" if cch % 2 else "vector"))
                    else:
                        CP(B_tok[:, c8:c8 + nn, :], src, [b_PSB], [b_Bt], eng="vector")
        CUT(12)
        P.barrier()
        apos[0] = mark
        wz = abf([128, 8, 512])
        if with_output:
            for q4 in range(4):
                load_w(win_d[l].rearrange("(k p) c -> p k c", p=128), 0, 8, OFF["ssd_z"] + q4 * 128, 128,
                       wz[:, :, q4 * 128:(q4 + 1) * 128], [b_wz])
        Wt = [abf([128, 8, 128]) for _ in range(3)]
        b_W = [Buf("W0"), Buf("W1"), Buf("W2")]
        BW = [abf([128, 4, 2, 64]) for _ in range(2)]
        b_BW = [Buf("BW0"), Buf("BW1")]
        hTs = [af32([128, 4, 128]) for _ in range(2)]
        b_hTs = [Buf("hTs0"), Buf("hTs1")]
        hTbf = [abf([128, 8, 64]) for _ in range(2)]
        b_hTbf = [Buf("hTbf0"), Buf("hTbf1")]
        HEB = [abf([128, 8, 64]) for _ in range(2)]
        b_HEB = [Buf("HEB0"), Buf("HEB1")]
        for d in range(2):
            MEMSET(hTbf[d][:], 0.0, [b_hTbf[d]])
        YA = af32([128, 512])
        ybpos = apos[0]
        YB = af32([128, 512])
        ZS = af32([128, 512])
        b_YA, b_YB, b_ZS = Buf("YA"), Buf("YB"), Buf("ZS")
        ybf32 = YB
        b_y32 = b_YB
        tmpst = arena[:, ybpos // 4:ybpos // 4 + 512].rearrange("p (a b) -> p a b", b=128)
        b_tmpst = b_YB
        ybf = abf([128, 512])
        b_ybf = Buf("ybf")
        yT = [abf([128, 4, 128])] * 2
        b_yT = [Buf("yT0")] * 2
        SSt = af32([128, 4])
        b_SS = Buf("SS")
        STs = abf([128, 256])
        b_STs = Buf("STs")

        def init_state(d):
            if is_ctx:
                MEMSET(hTs[d][:], 0.0, [b_hTs[d]])
            else:
                CP(hTs[d][:], ssdh0[d][:], [b_ssdh0[d]], [b_hTs[d]], eng="vector")

        def make_hTbf(d):
            CP(hTbf[d][0:64, 0:4, :], hTs[d][0:64, :, 0:64], [b_hTs[d]], [b_hTbf[d]], eng="scalar")
            CP(hTbf[d][64:128, 4:8, :], hTs[d][64:128, :, 64:128], [b_hTs[d]], [b_hTbf[d]], eng="scalar")

        def state_update(d, c):
            for g in range(2):
                col = c * 16 + d * 8 + g * 4
                TT(BW[d][:, :, g, :], B_tok[:, c, g * 64:(g + 1) * 64].unsqueeze(1).broadcast_to([128, 4, 64]),
                   WEND[:, col:col + 4].unsqueeze(2).broadcast_to([128, 4, 64]), ALU.mult, [b_Bt, b_dt], [b_BW[d]])
            xv = xs_tok[:, c, :].rearrange("p (g hh n) -> p hh g n", g=2, hh=4)
            for hh in range(4):
                MM(PS[6][:, hh * 128:(hh + 1) * 128], BW[d][:, hh, :, :], xv[:, hh, :, :], True, True,
                   [b_BW[d], b_xs], [b_PS[6]])
            TT(tmpst[:], hTs[d][:], d2[:, c, d, :].unsqueeze(2).broadcast_to([128, 4, 128]), ALU.mult,
               [b_hTs[d], b_dt], [b_tmpst])
            TT(hTs[d][:], tmpst[:], PS[6][:, :].rearrange("p (a b) -> p a b", b=128), ALU.add,
               [b_tmpst, b_PS[6]], [b_hTs[d]])

        init_state(1)
        for c in range(NC - 1, -1, -1):
            if with_output:
                make_hTbf(1)
                DMA(heb_d[c].rearrange("p (a b) -> p a b", b=64), hTbf[1][:], [b_hTbf[1]], [b_heb_d[c]], q="gpsimd")
            state_update(1, c)
        if is_ctx:
            CP(ssdh0[1][:], hTs[1][:], [b_hTs[1]], [b_ssdh0[1]], eng="vector")
        CUT(13)
        init_state(0)

        def wgen(c, WA, bA, WB, bB):
            for d, (Wd, bW) in enumerate(((WA, bA), (WB, bB))):
                for g in range(2):
                    bi = (d * 2 + g) % 2
                    MM(PS[bi][:, :], ident_b, mask_bf[d], True, False, [b_const], [b_PS[bi]])
                    for hh in range(4):
                        col = c * 16 + d * 8 + g * 4 + hh
                        MM(PS[bi][:, hh * 128:(hh + 1) * 128], CS[:, col:col + 1].broadcast_to([128, 128]), ident_f,
                           False, hh == 3, [b_dt, b_const], [b_PS[bi]])
                    for hh in range(4):
                        col = c * 16 + d * 8 + g * 4 + hh
                        ACT(Wd[:, g * 4 + hh, :], PS[bi][:, hh * 128:(hh + 1) * 128], AF.Exp, [b_PS[bi], b_dt],
                            [bW], bias=BIAS[:, col:col + 1])

        def v64(ap):
            return ap.rearrange("p (h n) -> p h n", n=64)
        if with_output:
            DMA(HEB[0][:], heb_d[0].rearrange("p (a b) -> p a b", b=64), [b_heb_d[0]], [b_HEB[0]])
            wgen(0, Wt[0], b_W[0], Wt[1], b_W[1])
        for c in range(NC):
            if with_output:
                hs = c % 2
                WA, bA = Wt[c % 3], b_W[c % 3]
                WB, bB = Wt[(c + 1) % 3], b_W[(c + 1) % 3]
                WN, bN = Wt[(c + 2) % 3], b_W[(c + 2) % 3]
                if c + 1 < NC:
                    DMA(HEB[1 - hs][:], heb_d[c + 1].rearrange("p (a b) -> p a b", b=64), [b_heb_d[c + 1]],
                        [b_HEB[1 - hs]])
                make_hTbf(0)
                tok = slice(c * 128, (c + 1) * 128)
                TT(WA[:], WA[:], WB[:], ALU.add, [bA, bB], [bA])
                for g in range(2):
                    MM(PS[2][:, g * 128:(g + 1) * 128], BTm[g][:, tok], CT[:, tok], True, True, [b_BT, b_CT], [b_PS[2]])
                CP(STs[:, :], PS[2][:, 0:256], [b_PS[2]], [b_STs], eng="scalar")
                for g in range(2):
                    TT(WA[:, g * 4:(g + 1) * 4, :], WA[:, g * 4:(g + 1) * 4, :],
                       STs[:, g * 128:(g + 1) * 128].unsqueeze(1).broadcast_to([128, 4, 128]), ALU.mult,
                       [bA, b_STs], [bA])
                for h in range(8):
                    MM(PS[3][:, h * 64:(h + 1) * 64], WA[:, h, :], xs_tok[:, c, h * 64:(h + 1) * 64], True, True,
                       [bA, b_xs], [b_PS[3]])
                MM(PS[4][:, :], CT[:, tok], hTbf[0][:].rearrange("p a b -> p (a b)"), True, True, [b_CT, b_hTbf[0]],
                   [b_PS[4]])
                MM(PS[5][:, :], CT[:, tok], HEB[hs][:].rearrange("p a b -> p (a b)"), True, True, [b_CT, b_HEB[hs]],
                   [b_PS[5]])
                hb = [hbufs[min(c // 4, len(hbufs) - 1)]]
                for k in range(8):
                    MM(PS[2][:, :], hsrc[:, k, tok], wz[:, k, :], k == 0, k == 7, [b_wz] + hb, [b_PS[2]])
            state_update(0, c)
            if with_output:
                if c + 1 < NC:
                    wgen(c + 1, WB, bB, WN, bN)
                ACT(ZS, PS[2][:, :], AF.Silu, [b_PS[2]], [b_ZS])
                TT(v64(YA), v64(PS[4][:, :]), ECS[:, c * 16:c * 16 + 8].unsqueeze(2).broadcast_to([128, 8, 64]), ALU.mult,
                   [b_PS[4], b_dt], [b_YA])
                TT(v64(ybf32), v64(PS[5][:, :]), ECS[:, c * 16 + 8:c * 16 + 16].unsqueeze(2).broadcast_to([128, 8, 64]),
                   ALU.mult, [b_PS[5], b_dt], [b_y32])
                TT(YA, YA, ybf32, ALU.add, [b_YA, b_y32], [b_YA])
                TT(YA, YA, PS[3][:, :], ALU.add, [b_YA, b_PS[3]], [b_YA])
                TT(ybf32, xs_tok[:, c, :], rowb[:, RB["dbc"]:RB["dbc"] + 512], ALU.mult, [b_xs, b_rowb], [b_y32])
                TT(YA, YA, ybf32, ALU.add, [b_YA, b_y32], [b_YA])
                TT(YA, YA, ZS, ALU.mult, [b_YA, b_ZS], [b_YA])
                ACT(ybf32, YA, AF.Square, [b_YA], [b_y32, b_SS], accum=SSt[:, 0:1])
                ACT(SSt[:, 1:2], SSt[:, 0:1], AF.Ln, [b_SS, b_eps], [b_SS], scale=1.0 / 512.0, bias=epsc[:, 0:1])
                ACT(SSt[:, 2:3], SSt[:, 1:2], AF.Exp, [b_SS], [b_SS], scale=-0.5)
                STT(ybf, YA, SSt[:, 2:3], rowb[:, RB["snw"]:RB["snw"] + 512], ALU.mult, ALU.mult,
                    [b_YA, b_SS, b_rowb], [b_ybf])
                for j in range(4):
                    TR(PSB[:, j * 128:(j + 1) * 128], ybf[:, j * 128:(j + 1) * 128], ident_b, [b_ybf, b_const], [b_PSB])
                ys = c % 2
                CP(yT[ys][:], PSB[:, 0:512].rearrange("p (a b) -> p a b", b=128), [b_PSB], [b_yT[ys]], eng="scalar")
                yd = (ycatc_d if is_ctx else ycat_d)
                DMA(yd[12 * 128:16 * 128, tok].rearrange("(j p) t -> p j t", p=128), yT[ys][:], [b_yT[ys]],
                    [b_ycat[(ic, 12 + j)] for j in range(4)], q="gpsimd")
        if is_ctx:
            CP(ssdh0[0][:], hTs[0][:], [b_hTs[0]], [b_ssdh0[0]], eng="vector")

    def outproj_phase(l, Tn, is_ctx, x_src_d, x_dst_d, last):
        arena_reset()
        modt, b_modt = modts[l], b_modts[l]
        ic = 1 if is_ctx else 0
        col = 1 if is_ctx else 0
        NT = min(512, Tn)
        wo = abf([128, 16, D])
        b_wo = Buf("wo")
        yc, xsqs = [], []
        for _ in range(2):
            pos = apos[0]
            yc.append(abf([128, 16, NT]))
            xsqs.append(arena[:, pos // 4:pos // 4 + 8 * NT // 2].bitcast(BF16).rearrange("p (a b) -> p a b", b=NT))
        b_yc = [Buf("yc0"), Buf("yc1")]
        xt = [af32([128, 8, NT]) for _ in range(2)]
        b_xt = [Buf("xt0"), Buf("xt1")]
        xn = xt
        b_xn = b_xt
        rstd_t = af32([128, NT])
        tmp_t = af32([128, NT])
        b_nscr = Buf("nscr")
        scrs = [dict(xsq=xsqs[i], rstd=rstd_t, tmp=tmp_t, b=[b_yc[i], b_nscr]) for i in range(2)]
        ofin = xt
        b_ofin = b_xt
        wsrc = wout_d[l].rearrange("(k p) c -> p k c", p=128)
        for half in range(2):
            for m in range(8):
                load_w(wsrc, half * 8, 8, m * 128, 128, wo[:, half * 8:(half + 1) * 8, m * 128:(m + 1) * 128], [b_wo])
        yd = (ycatc_d if is_ctx else ycat_d)
        dst_h = hTc if is_ctx else hT
        dst_b = b_hTc if is_ctx else b_hT
        def issue_loads(tt):
            s = tt % 2
            tok = slice(tt * NT, (tt + 1) * NT)
            DMA(yc[s][:], yd[:, tok].rearrange("(k p) t -> p k t", p=128), [b_ycat[(ic, j)] for j in range(16)],
                [b_yc[s]])
            x1b = [b_x1[(tt * NT) // 256 + i] for i in range(max(1, NT // 256))]
            xrd = x1b if (x_src_d is x1_d) else []
            DMA(xt[s][:], x_src_d[:, tok].rearrange("(k p) t -> p k t", p=128), xrd, [b_xt[s]])
        issue_loads(0)
        for tt in range(Tn // NT):
            s = tt % 2
            tok = slice(tt * NT, (tt + 1) * NT)
            x1b = [b_x1[(tt * NT) // 256 + i] for i in range(max(1, NT // 256))]
            if tt + 1 < Tn // NT:
                issue_loads(tt + 1)
            for m in range(8):
                bi = psrot[0] % 4
                psrot[0] += 1
                for k in range(16):
                    MM(PS[bi][:, 0:NT], wo[:, k, m * 128:(m + 1) * 128], yc[s][:, k, :], k == 0, k == 15,
                       [b_wo, b_yc[s]], [b_PS[bi]])
                STT(xn[s][:, m, :], PS[bi][:, 0:NT], modt[:, 16 + m, col:col + 1], xt[s][:, m, :], ALU.mult, ALU.add,
                    [b_PS[bi], b_modt, b_xt[s]], [b_xn[s]])
            if x_dst_d is not None:
                DMA(x_dst_d[:, tok].rearrange("(k p) t -> p k t", p=128), xn[s][:], [b_xn[s]], x1b, q="gpsimd")
            if last:
                pv = pvecs[l]
                norm_tile(xn[s][:], [b_xn[s]], NT, (lambda k: pv[:, PV["fnw"] + k:PV["fnw"] + k + 1]), None,
                          (lambda k, s=s: ofin[s][:, k, :]), [b_ofin[s]], scrs[s], [b_pvecs[l]])
                DMA(outT_d[:, tok].rearrange("(k p) t -> p k t", p=128), ofin[s][:], [b_ofin[s]], [b_out], q="gpsimd")
            else:
                An, Mn = Amods[l + 1], modts[l + 1]
                norm_tile(xn[s][:], [b_xn[s]], NT, (lambda k: An[:, k, col:col + 1]), (lambda k: Mn[:, k, col:col + 1]),
                          (lambda k, tok=tok: dst_h[:, k, tok]), [dst_b[min(tt * NT // 512, len(dst_b) - 1)]], scrs[s],
                          [b_Amods[l + 1], b_modts[l + 1]])

    def input_norm(x_src_d, Tn, col, dst, dbufs):
        arena_reset()
        NT = 256
        xt = [af32([128, 8, NT]) for _ in range(2)]
        b_xt = [Buf("xt0"), Buf("xt1")]
        scr = dict(xsq=abf([128, 8, NT]), rstd=af32([128, NT]), tmp=af32([128, NT]), b=[Buf("nscr")])
        A0, M0 = Amods[0], modts[0]
        def ld(tt):
            DMA(xt[tt % 2][:], x_src_d[:, tt * NT:(tt + 1) * NT].rearrange("(k p) t -> p k t", p=128), [], [b_xt[tt % 2]])
        for tt in range(Tn // NT):
            s = tt % 2
            tok = slice(tt * NT, (tt + 1) * NT)
            ld(tt)
            norm_tile(xt[s][:], [b_xt[s]], NT, (lambda k: A0[:, k, col:col + 1]), (lambda k: M0[:, k, col:col + 1]),
                      (lambda k, tok=tok: dst[:, k, tok]), [dbufs[min(tt // 2, len(dbufs) - 1)]], scr,
                      [b_Amods[0], b_modts[0]])

    phases = []
    for l in range(NL):
        phases.append(("ada%d" % l, (lambda l=l: ada_setup(l))))
    phases.append(("nctx", lambda: input_norm(ctxT_d, TC, 1, hTc, b_hTc)))
    phases.append(("nx", lambda: input_norm(xT_d, T, 0, hT, b_hT)))
    for l in range(NL):
        last = (l == NL - 1)
        phases.append(("par%d" % l, (lambda l=l: layer_params(l))))
        phases.append(("crg%d" % l, (lambda l=l, last=last: rg_branch(l, hTc, b_hTc, TC, True, not last))))
        phases.append(("cssd%d" % l, (lambda l=l, last=last: ssd_branch(l, hTc, b_hTc, TC, True, not last))))
        if not last:
            phases.append(("csc%d" % l, (lambda l=l: sc_branch(l, hTc, b_hTc, TC, True))))
            phases.append(("cfn%d" % l, (lambda l=l: fn_branch(l, hTc, b_hTc, TC, True))))
            phases.append(("cout%d" % l, (lambda l=l: outproj_phase(l, TC, True, ctxT_d, None, False))))
        phases.append(("rg%d" % l, (lambda l=l: rg_branch(l, hT, b_hT, T, False, True))))
        phases.append(("sc%d" % l, (lambda l=l: sc_branch(l, hT, b_hT, T, False))))
        phases.append(("ssd%d" % l, (lambda l=l: ssd_branch(l, hT, b_hT, T, False, True))))
        phases.append(("fn%d" % l, (lambda l=l: fn_branch(l, hT, b_hT, T, False))))
        phases.append(("out%d" % l, (lambda l=l, last=last: outproj_phase(
            l, T, False, (xT_d if l == 0 else x1_d), (None if last else x1_d), last))))
    stop = (dbg or {}).get("stop")
    skip = (dbg or {}).get("skip", ())
    for name, fn in phases:
        if name not in skip:
            try:
                fn()
            except _Cut:
                break
        if stop == name:
            break
    if dbg:
        P.barrier()
        bd = Buf("dbgd")
        DMA(dbg_hT, hT[:], [], [bd], semkey="d_dbg")
        DMA(dbg_hTc, hTc[:], [], [bd], semkey="d_dbg")
        for l in range(NL):
            DMA(dbg_mod[l], modts[l][:].rearrange("p a b -> p (a b)"), [], [bd], semkey="d_dbg")
        DMA(dbg_st[:, 0:8], rgst[:], [], [bd], semkey="d_dbg")
        for d in range(2):
            DMA(dbg_st[:, 8 + d * 512:8 + (d + 1) * 512], ssdh0[d][:].rearrange("p a b -> p (a b)"), [], [bd],
                semkey="d_dbg")
    print("ops per engine:", {e: len(P.ops[e]) for e in ENGS})
    P.finalize()
    return nc


_BF = ml_dtypes.bfloat16


def _host_consts():
    p = np.arange(128)
    ident = np.eye(128, dtype=np.float32)
    tri_f = (p[:, None] <= p[None, :]).astype(np.float32)
    tri_b = (p[:, None] >= p[None, :]).astype(np.float32)
    cf32 = np.concatenate([ident, tri_f, tri_b], axis=1).astype(np.float32)
    ones = np.ones((128, 128), np.float32)
    j = p[:, None]
    i = p[None, :]
    mf = np.where(i < j, NEG, 0.0).astype(np.float32)
    mb = np.where(i > j, NEG, 0.0).astype(np.float32)
    ang = 2.0 * np.pi * ((p[:, None] * p[None, :]) % 128) / 128.0
    cg = np.cos(ang)
    sg = -np.sin(ang)
    alt = np.where(p % 2 == 0, 1.0, -1.0)[:, None] * np.ones((1, 2))
    altr = np.ones((128, 1)) * np.where(np.arange(256) % 2 == 0, 1.0, -1.0)[None, :]
    cbf = np.concatenate([ident, ones, np.tile(mf, (1, 4)), np.tile(mb, (1, 4)), cg, sg, alt, altr], axis=1).astype(_BF)
    t = np.arange(T, dtype=np.int64)
    kt = (t[:T // 2, None] * t[None, :T // 2]) % T
    angT = (2.0 * np.pi / T) * kt.astype(np.float64)
    def tile_dft(m):
        return np.ascontiguousarray(m.astype(np.float32).reshape(16, 128, T // 512, 256).transpose(2, 1, 0, 3)).astype(_BF)
    dftc = tile_dft(np.cos(angT))
    dfts = tile_dft(np.sin(angT))
    t2 = np.arange(TC, dtype=np.int64)
    a2 = (2.0 * np.pi / TC) * ((t2[:, None] * t2[None, :]) % TC).astype(np.float64)
    c2 = np.cos(a2).astype(np.float32).reshape(2, 128, TC).transpose(1, 0, 2)
    s2 = np.sin(a2).astype(np.float32).reshape(2, 128, TC).transpose(1, 0, 2)
    dft256 = np.concatenate([c2, s2], axis=2).astype(_BF)
    return dict(cf32=cf32, cbf=cbf, dftc=dftc, dfts=dfts, dft256=np.ascontiguousarray(dft256))


_CONSTS = None
_NC = None


def _fm(v, nchunk):
    return np.ascontiguousarray(np.asarray(v, np.float32).reshape(nchunk, 128).T)


def kernel(x, c, ctx, c_ctx, ada_w, ada_b, norm_w, w_in, w_out, rg_conv_w, rg_conv_b, rg_gate_a_w, rg_gate_a_b,
           rg_gate_x_w, rg_gate_x_b, rg_lambda, sc_conv_w, ssd_conv_w, ssd_conv_b, ssd_dt_bias, ssd_a_log, ssd_d,
           ssd_norm_w, final_norm_w):
    global _CONSTS, _NC
    f = lambda a: np.asarray(a, dtype=np.float32)
    x, c, ctx, c_ctx = f(x), f(c), f(ctx), f(c_ctx)
    if _CONSTS is None:
        _CONSTS = _host_consts()
    if _NC is None:
        _NC = build_program()
    pvec = np.zeros((NL, 128, NPV), np.float32)
    rowb = np.zeros((NL, 128, NRB), np.float32)
    gbd = np.zeros((NL, 128, 16, 128), np.float32)
    for l in range(NL):
        pvec[l, :, PV["nw"]:PV["nw"] + 8] = _fm(f(norm_w)[l], 8)
        pvec[l, :, PV["adab"]:PV["adab"] + 24] = _fm(f(ada_b)[l], 24)
        for cc in range(4):
            for k in range(4):
                pvec[l, :, PV["rgcw"] + cc * 4 + k] = f(rg_conv_w)[l, k, cc * 128:(cc + 1) * 128]
            pvec[l, :, PV["rgcb"] + cc] = f(rg_conv_b)[l, cc * 128:(cc + 1) * 128]
            for d in range(2):
                pvec[l, :, PV["rgba"] + d * 4 + cc] = f(rg_gate_a_b)[l, d, cc * 128:(cc + 1) * 128]
                pvec[l, :, PV["rgbx"] + d * 4 + cc] = f(rg_gate_x_b)[l, d, cc * 128:(cc + 1) * 128]
                pvec[l, :, PV["rglam"] + d * 4 + cc] = f(rg_lambda)[l, d, cc * 128:(cc + 1) * 128]
                for ax, W in enumerate((f(rg_gate_a_w), f(rg_gate_x_w))):
                    idx = d * 8 + ax * 4 + cc
                    gbd[l, 0:64, idx, 0:64] = W[l, d, 2 * cc]
                    gbd[l, 64:128, idx, 64:128] = W[l, d, 2 * cc + 1]
            for k in range(3):
                pvec[l, :, PV["sccw"] + cc * 3 + k] = f(sc_conv_w)[l, k, cc * 128:(cc + 1) * 128]
        for cch in range(6):
            for k in range(4):
                pvec[l, :, PV["sdcw"] + cch * 4 + k] = f(ssd_conv_w)[l, k, cch * 128:(cch + 1) * 128]
            pvec[l, :, PV["sdcb"] + cch] = f(ssd_conv_b)[l, cch * 128:(cch + 1) * 128]
        pvec[l, :, PV["fnw"]:PV["fnw"] + 8] = _fm(f(final_norm_w), 8)
        rowb[l, :, RB["dtb"]:RB["dtb"] + 16] = f(ssd_dt_bias)[l].reshape(1, 16)
        rowb[l, :, RB["alog"]:RB["alog"] + 16] = f(ssd_a_log)[l].reshape(1, 16)
        rowb[l, :, RB["dbc"]:RB["dbc"] + 512] = np.repeat(f(ssd_d)[l], 64)[None, :]
        rowb[l, :, RB["snw"]:RB["snw"] + 512] = f(ssd_norm_w)[l][None, :]
    shared = dict(ada_w=f(ada_w), w_in=f(w_in), w_out=f(w_out), pvec=pvec, rowb=rowb, gbd=gbd, **_CONSTS)
    in_maps = []
    for core in range(NCORES):
        b = core % 4
        cc = np.stack([_fm(c[b], 8), _fm(c_ctx, 8)], axis=2)
        m = dict(shared)
        m["xT"] = np.ascontiguousarray(x[b].T)
        m["ctxT"] = np.ascontiguousarray(ctx[b].T)
        m["cc"] = np.ascontiguousarray(cc)
        in_maps.append(m)
    res = run_bass_kernel_spmd(_NC, in_maps, core_ids=list(range(NCORES)))
    out = np.stack([np.ascontiguousarray(res.results[b]["outT"].T) for b in range(4)], axis=0)
    return out.astype(np.float32)
```

```python
import os
from contextlib import ExitStack
import numpy as np
import ml_dtypes
import concourse.bass as bass
import concourse.mybir as mybir
from concourse.bass_utils import run_bass_kernel_spmd

F32 = mybir.dt.float32
BF16 = mybir.dt.bfloat16
AF = mybir.ActivationFunctionType
ALU = mybir.AluOpType

D = 1024
T = 4096
TC = 256
NL = 2
NCORES = 8
EPS = 1e-6
OFF = dict(rg_x=0, ssd_xbc=512, ssd_dt=1280, rg_g=1296, ssd_z=1808, sc_b=2320, sc_c=2832,
           sc_x=3344, sc_g=3856, fn_x=4368, fn_g=4880)
PV = dict(nw=0, adab=8, rgcw=32, rgcb=48, rgba=52, rgbx=60, rglam=68, sccw=76, sdcw=88, sdcb=112, fnw=118)
NPV = 128
RB = dict(dtb=0, alog=16, dbc=32, snw=544)
NRB = 1056
ARENA_BYTES = 108544
NEG = -30000.0

ENGS = ("tensor", "vector", "scalar", "gpsimd", "sync")
SAME_ENGINE_RAW = True
STORE_Q = "sync"


class Buf:
    __slots__ = ("name", "last_w", "reads")

    def __init__(self, name):
        self.name = name
        self.last_w = None
        self.reads = []


class Prog:
    def __init__(self, nc):
        self.nc = nc
        self.es = ExitStack()
        self.ops = {e: [] for e in ENGS}
        self.seq = {e: 0 for e in ENGS}
        self.waited = {e: {} for e in ENGS}
        self.sems = {}
        self.dmacount = {}

    def sbuf(self, name, shape, dtype):
        return self.es.enter_context(self.nc.sbuf_tensor("sb_" + name, list(shape), dtype))

    def psum(self, name, shape, dtype=F32):
        return self.es.enter_context(self.nc.psum_tensor(name, list(shape), dtype))

    def sem(self, key):
        if key not in self.sems:
            self.sems[key] = self.es.enter_context(self.nc.semaphore("s_" + str(key)))
        return self.sems[key]

    def _collect(self, eng, reads, writes, force=False):
        waits = {}

        def need(ev, is_raw):
            if ev is None:
                return
            key, val = ev
            if key == eng and not force and not (is_raw and SAME_ENGINE_RAW):
                return
            if self.waited[eng].get(key, 0) >= val:
                return
            if waits.get(key, 0) < val:
                waits[key] = val

        for b in reads:
            need(b.last_w, True)
        for b in writes:
            need(b.last_w, False)
            for r in b.reads:
                need(r, False)
        for k, v in waits.items():
            self.waited[eng][k] = v
        return waits

    def op(self, eng, fn, reads=(), writes=()):
        waits = self._collect(eng, reads, writes)
        self.seq[eng] += 1
        ev = (eng, self.seq[eng])
        for b in reads:
            b.reads.append(ev)
        for b in writes:
            b.last_w = ev
            b.reads = []
        self.sem(eng)
        for k in waits:
            self.sem(k)
        self.ops[eng].append((waits, fn, (eng, 1)))
        return ev

    def dma(self, eng, fn, reads=(), writes=(), semkey=None):
        if semkey is None:
            semkey = "d_" + (writes[0].name if writes else reads[0].name)
        waits = self._collect(eng, reads, writes, force=True)
        self.dmacount[semkey] = self.dmacount.get(semkey, 0) + 16
        ev = (semkey, self.dmacount[semkey])
        for b in reads:
            b.reads.append(ev)
        for b in writes:
            b.last_w = ev
            b.reads = []
        self.sem(semkey)
        for k in waits:
            self.sem(k)
        self.ops[eng].append((waits, fn, (semkey, 16)))
        return ev

    def barrier(self):
        tgt = {}
        for e in ENGS:
            if self.seq[e] > 0:
                tgt[e] = self.seq[e]
        for k, v in self.dmacount.items():
            tgt[k] = v
        for e in ENGS:
            waits = {}
            for k, v in tgt.items():
                if k == e:
                    continue
                if self.waited[e].get(k, 0) >= v:
                    continue
                waits[k] = v
                self.waited[e][k] = v
            if waits:
                self.seq[e] += 1
                self.sem(e)
                self.ops[e].append((waits, (lambda eng: eng.nop()), (e, 1)))

    def finalize(self):
        fw = {}
        for e in ENGS:
            if e != "sync" and self.seq[e] > 0:
                fw[e] = self.seq[e]
        for k, v in self.dmacount.items():
            fw[k] = max(fw.get(k, 0), v)
        nc = self.nc
        sems = self.sems
        ops = self.ops
        with nc.Block() as block:
            def run(engname):
                def body(eng):
                    for waits, fn, (ik, iv) in ops[engname]:
                        for k, v in waits.items():
                            eng.wait_ge(sems[k], v)
                        ins = fn(eng)
                        ins.then_inc(sems[ik], iv)
                    if engname == "sync":
                        for k, v in fw.items():
                            eng.wait_ge(sems[k], v)
                return body
            block.tensor(run("tensor"))
            block.vector(run("vector"))
            block.scalar(run("scalar"))
            block.gpsimd(run("gpsimd"))
            block.sync(run("sync"))
        self.es.close()


class _Cut(Exception):
    pass


def build_program(dbg=None):
    nc = bass.Bass("TRN2", target_bir_lowering=False)
    P = Prog(nc)
    cutn = int((dbg or {}).get("cut", 0))

    def CUT(n):
        if cutn == n:
            raise _Cut()

    def din(name, shape, dt=F32):
        return nc.dram_tensor(name, list(shape), dt, kind="ExternalInput").ap()

    xT_d = din("xT", [D, T])
    ctxT_d = din("ctxT", [D, TC])
    cc_d = din("cc", [128, 8, 2])
    adaw_d = din("ada_w", [NL, D, 3 * D])
    win_d = din("w_in", [NL, D, 5392])
    wout_d = din("w_out", [NL, 2 * D, D])
    pvec_d = din("pvec", [NL, 128, NPV])
    rowb_d = din("rowb", [NL, 128, NRB])
    gbd_d = din("gbd", [NL, 128, 16, 128])
    cf32_d = din("cf32", [128, 384])
    cbf_d = din("cbf", [128, 1794], BF16)
    dftc_d = din("dftc", [T // 512, 128, 16, 256], BF16)
    dfts_d = din("dfts", [T // 512, 128, 16, 256], BF16)
    dft256_d = din("dft256", [128, 2, 512], BF16)
    outT_d = nc.dram_tensor("outT", [D, T], F32, kind="ExternalOutput").ap()
    skind = dict(kind="ExternalOutput") if dbg else {}
    x1_d = nc.dram_tensor("x1s", [D, T], F32, **skind).ap()
    ycat_d = nc.dram_tensor("ycat", [2 * D, T], BF16, **skind).ap()
    ycatc_d = nc.dram_tensor("ycatc", [2 * D, TC], BF16, **skind).ap()
    if dbg:
        dbg_hT = nc.dram_tensor("dbg_hT", [128, 8, T], BF16, kind="ExternalOutput").ap()
        dbg_hTc = nc.dram_tensor("dbg_hTc", [128, 8, TC], BF16, kind="ExternalOutput").ap()
        dbg_mod = nc.dram_tensor("dbg_mod", [NL, 128, 48], F32, kind="ExternalOutput").ap()
        dbg_st = nc.dram_tensor("dbg_st", [128, 8 + 2 * 512], F32, kind="ExternalOutput").ap()
    heb_d = nc.dram_tensor("hebd", [T // 128, 128, 512], BF16).ap()
    b_x1 = [Buf("x1_%d" % i) for i in range(T // 256)]
    b_ycat = {}
    for ic in (0, 1):
        for c16 in range(16):
            b_ycat[(ic, c16)] = Buf("yc%d_%d" % (ic, c16))
    b_heb_d = [Buf("hebd%d" % i) for i in range(T // 128)]
    b_out = Buf("outd")

    hT = P.sbuf("hT", [128, 8, T], BF16)
    b_hT = [Buf("hT%d" % i) for i in range(T // 512)]
    hTc = P.sbuf("hTc", [128, 8, TC], BF16)
    b_hTc = [Buf("hTc")]
    wst = [P.sbuf("wst%d" % i, [128, 8, 128], F32) for i in range(2)]
    b_wst = [Buf("wst%d" % i) for i in range(2)]
    wbf = [P.sbuf("wbf%d" % i, [128, 8, 128], BF16) for i in range(2)]
    b_wbf = [Buf("wbf%d" % i) for i in range(2)]
    gatew = P.sbuf("gatew", [128, 16, 128], BF16)
    b_gatew = Buf("gatew")
    pvecs = [P.sbuf("pvec%d" % i, [128, NPV], F32) for i in range(NL)]
    b_pvecs = [Buf("pvec%d" % i) for i in range(NL)]
    rowb = P.sbuf("rowb", [128, NRB], F32)
    b_rowb = Buf("rowb")
    cf32 = P.sbuf("cf32", [128, 384], F32)
    cbf = P.sbuf("cbf", [128, 1794], BF16)
    b_const = Buf("const")
    dft256 = P.sbuf("dft256", [128, 2, 512], BF16)
    ccs = P.sbuf("ccs", [128, 8, 2], F32)
    b_ccs = Buf("ccs")
    modts = [P.sbuf("modt%d" % i, [128, 24, 2], F32) for i in range(NL)]
    b_modts = [Buf("modt%d" % i) for i in range(NL)]
    Amods = [P.sbuf("Amod%d" % i, [128, 8, 2], F32) for i in range(NL)]
    b_Amods = [Buf("Amod%d" % i) for i in range(NL)]
    cAt = P.sbuf("cAt", [128, 8], F32)
    b_cA = Buf("cA")
    smallt = P.sbuf("smallt", [128, 64], F32)
    b_small = Buf("small")
    epsc = P.sbuf("epsc", [128, 4], F32)
    b_eps = Buf("epsc")
    rgst = P.sbuf("rgst", [128, 8], F32)
    b_rgst = Buf("rgst")
    zero8 = P.sbuf("zero8", [128, 8], F32)
    ssdh0 = [P.sbuf("ssdh0_%d" % d, [128, 4, 128], F32) for d in range(2)]
    b_ssdh0 = [Buf("ssdh0_%d" % d) for d in range(2)]
    aneg = P.sbuf("aneg", [128, 16], F32)
    b_aneg = Buf("aneg")
    arena = P.sbuf("arena", [128, ARENA_BYTES // 4], F32)

    ident_f = cf32[:, 0:128]
    tri_f = cf32[:, 128:256]
    tri_b = cf32[:, 256:384]
    ident_b = cbf[:, 0:128]
    ones_b = cbf[:, 128:256]
    mask_bf = [cbf[:, 256:768], cbf[:, 768:1280]]
    cgsg = cbf[:, 1280:1536]
    alt2 = cbf[:, 1536:1538]
    altrow = cbf[:, 1538:1794]

    PS = [P.psum("ps%d" % i, [128, 512], F32) for i in range(7)]
    b_PS = [Buf("ps%d" % i) for i in range(7)]
    PSB = P.psum("psb", [128, 1024], BF16)
    b_PSB = Buf("psb")

    tog = [0]
    cpeng = ["vector"]

    def ACT(out, in_, func, r, w, scale=None, bias=None, accum=None):
        kw = {}
        if scale is not None:
            kw["scale"] = scale
        if bias is not None:
            kw["bias"] = bias
        if accum is not None:
            kw["accum_out"] = accum
        P.op("scalar", lambda e: e.activation(out=out, in_=in_, func=func, **kw), reads=r, writes=w)

    def TT(out, a, b, op, r, w, eng="vector"):
        P.op(eng, lambda e: e.tensor_tensor(out=out, in0=a, in1=b, op=op), reads=r, writes=w)

    def STT(out, in0, scalar, in1, op0, op1, r, w):
        P.op("vector", lambda e: e.scalar_tensor_tensor(out=out, in0=in0, scalar=scalar, in1=in1,
                                                         op0=op0, op1=op1), reads=r, writes=w)

    def TS(out, in0, s1, s2, op0, op1, r, w, eng="vector"):
        if op1 is None:
            P.op(eng, lambda e: e.tensor_scalar(out=out, in0=in0, scalar1=s1, scalar2=None, op0=op0),
                 reads=r, writes=w)
        else:
            P.op(eng, lambda e: e.tensor_scalar(out=out, in0=in0, scalar1=s1, scalar2=s2, op0=op0, op1=op1),
                 reads=r, writes=w)

    def MM(out, lhsT, rhs, start, stop, r, w):
        P.op("tensor", lambda e: e.matmul(out=out, lhsT=lhsT, rhs=rhs, start=start, stop=stop,
                                          skip_group_check=True), reads=r, writes=w)

    def TR(out, in_, ident, r, w):
        P.op("tensor", lambda e: e.transpose(out=out, in_=in_, identity=ident), reads=r, writes=w)

    def CP(out, in_, r, w, eng=None):
        if eng is None:
            tog[0] ^= 1
            eng = "scalar" if tog[0] else "vector"
        if eng == "scalar":
            P.op("scalar", lambda e: e.activation(out=out, in_=in_, func=AF.Identity), reads=r, writes=w)
        else:
            P.op(eng, lambda e: e.tensor_copy(out=out, in_=in_), reads=r, writes=w)

    def MEMSET(ap, val, w, eng="vector"):
        P.op(eng, lambda e: e.memset(ap, val), writes=w)

    def DMA(out, in_, r, w, q="sync", semkey=None, slow=False):
        if semkey is None and q == "gpsimd" and r:
            semkey = "dw_" + r[0].name
        if q == "gpsimd":
            q = STORE_Q
        if slow:
            P.dma(q, lambda e: e.dma_start(out=out, in_=in_, allow_slow_non_contiguous=True), reads=r, writes=w,
                  semkey=semkey)
            return
        P.dma(q, lambda e: e.dma_start(out=out, in_=in_), reads=r, writes=w, semkey=semkey)

    apos = [0]

    def arena_reset():
        P.barrier()
        apos[0] = 0

    def af32(shape):
        n = int(np.prod(shape[1:]))
        w0 = apos[0] // 4
        apos[0] += n * 4
        assert apos[0] <= ARENA_BYTES, ("arena overflow", apos[0])
        ap = arena[:, w0:w0 + n]
        if len(shape) == 3:
            ap = ap.rearrange("p (a b) -> p a b", b=shape[2])
        elif len(shape) == 4:
            ap = ap.rearrange("p (a b c) -> p a b c", b=shape[2], c=shape[3])
        return ap

    def abf(shape):
        n = int(np.prod(shape[1:]))
        assert n % 2 == 0
        w0 = apos[0] // 4
        apos[0] += n * 2
        assert apos[0] <= ARENA_BYTES, ("arena overflow", apos[0])
        ap = arena[:, w0:w0 + n // 2].bitcast(BF16)
        if len(shape) == 3:
            ap = ap.rearrange("p (a b) -> p a b", b=shape[2])
        elif len(shape) == 4:
            ap = ap.rearrange("p (a b c) -> p a b c", b=shape[2], c=shape[3])
        return ap

    DMA(cf32[:], cf32_d[:, :], [], [b_const], semkey="d_const")
    DMA(cbf[:], cbf_d[:, :], [], [b_const], semkey="d_const")
    DMA(dft256[:], dft256_d[:, :, :], [], [b_const], semkey="d_const")
    DMA(ccs[:], cc_d[:, :, :], [], [b_ccs], semkey="d_const")
    MEMSET(epsc[:, 0:1], EPS, [b_eps])
    MEMSET(epsc[:, 1:2], 1.0, [b_eps])
    MEMSET(epsc[:, 2:4], 0.0, [b_eps])
    MEMSET(zero8[:], 0.0, [b_eps])
    ACT(ccs[:], ccs[:], AF.Silu, [b_ccs], [b_ccs])

    wrot = [0]

    def load_w(src3d, k0, nk, col0, ncols, dst_ap, dst_bufs):
        s = wrot[0]
        wrot[0] ^= 1
        DMA(wst[s][:, 0:nk, 0:ncols], src3d[:, k0:k0 + nk, col0:col0 + ncols], [], [b_wst[s]])
        CP(dst_ap, wst[s][:, 0:nk, 0:ncols], [b_wst[s]], dst_bufs)

    wbrot = [0]

    wcache = {}

    def load_win(l, col0, ncols):
        key = (l, col0, ncols)
        if key in wcache:
            s = wcache.pop(key)
            return wbf[s], b_wbf[s]
        s = wbrot[0]
        wbrot[0] ^= 1
        src = win_d[l].rearrange("(k p) c -> p k c", p=128)
        load_w(src, 0, 8, col0, ncols, wbf[s][:, :, 0:ncols], [b_wbf[s]])
        return wbf[s], b_wbf[s]

    def prefetch_win(l, col0, ncols):
        wcache.clear()
        s = wbrot[0]
        wbrot[0] ^= 1
        src = win_d[l].rearrange("(k p) c -> p k c", p=128)
        load_w(src, 0, 8, col0, ncols, wbf[s][:, :, 0:ncols], [b_wbf[s]])
        wcache[(l, col0, ncols)] = s

    psrot = [0]

    def proj_cm(l, col0, ncols, hsrc, hbufs, Tn, evac, banks=(0, 1, 2, 3)):
        w, bw = load_win(l, col0, ncols)
        tog[0] ^= 1
        cpeng[0] = "scalar" if tog[0] else "vector"
        nt = (Tn + 511) // 512
        for tt in range(nt):
            t0 = tt * 512
            n = min(512, Tn - t0)
            bi = banks[psrot[0] % len(banks)]
            psrot[0] += 1
            hb = [hbufs[min(tt, len(hbufs) - 1)]]
            for k in range(8):
                MM(PS[bi][0:ncols, 0:n], w[:, k, 0:ncols], hsrc[:, k, t0:t0 + n], k == 0, k == 7,
                   [bw] + hb, [b_PS[bi]])
            evac(PS[bi][0:ncols, 0:n], b_PS[bi], t0, n, tt)

    def softplus_tile(dst, src, n, scr1, scr2, r, w, bscr):
        TS(scr1, src, 30.0, None, ALU.min, None, r, bscr)
        ACT(scr1, scr1, AF.Exp, bscr, bscr)
        TS(scr2, scr1, -0.25, 1.0 / 3.0, ALU.mult, ALU.add, bscr, bscr)
        TT(scr2, scr2, scr1, ALU.mult, bscr, bscr)
        TS(scr2, scr2, -0.5, None, ALU.add, None, bscr, bscr)
        TT(scr2, scr2, scr1, ALU.mult, bscr, bscr)
        TS(scr2, scr2, 1.0, None, ALU.add, None, bscr, bscr)
        TT(scr2, scr2, scr1, ALU.mult, bscr, bscr)
        ACT(dst, scr1, AF.Ln, bscr + [b_eps], w, bias=epsc[:, 1:2])
        TS(scr1, scr1, 0.05, None, ALU.is_gt, None, bscr, bscr)
        TT(dst, dst, scr2, ALU.subtract, w + bscr, w)
        TT(dst, dst, scr1, ALU.mult, w + bscr, w)
        TT(dst, dst, scr2, ALU.add, w + bscr, w)

    cur = {}

    def set_layer(l):
        cur["pvec"] = pvecs[l]
        cur["b_pvec"] = b_pvecs[l]
        cur["modt"] = modts[l]
        cur["b_modt"] = b_modts[l]
        cur["Amod"] = Amods[l]
        cur["b_Amod"] = b_Amods[l]

    def ada_setup(l):
        pvec, b_pvec = pvecs[l], b_pvecs[l]
        modt, b_modt, Amod, b_Amod = modts[l], b_modts[l], Amods[l], b_Amods[l]
        DMA(pvec[:], pvec_d[l], [], [b_pvec])
        arena_reset()
        aw = [af32([128, 8, 512]) for _ in range(2)]
        b_aw = [Buf("aw0"), Buf("aw1")]
        mrow = af32([128, 3 * D])[0:2, :]
        b_mrow = Buf("mrow")
        src = adaw_d[l].rearrange("(k p) c -> p k c", p=128)
        for jb in range(6):
            s = jb % 2
            DMA(aw[s], src[:, :, jb * 512:(jb + 1) * 512], [], [b_aw[s]])
            bi = jb % 2
            for k in range(8):
                MM(PS[bi][0:2, :], ccs[:, k, :], aw[s][:, k, :], k == 0, k == 7, [b_aw[s], b_ccs], [b_PS[bi]])
            CP(mrow[:, jb * 512:(jb + 1) * 512], PS[bi][0:2, :], [b_PS[bi]], [b_mrow], eng="vector")
        for j in range(24):
            TR(PS[2][:, j * 2:j * 2 + 2], mrow[:, j * 128:(j + 1) * 128], ident_f[0:2, 0:2], [b_mrow, b_const], [b_PS[2]])
        adab = pvec[:, PV["adab"]:PV["adab"] + 24]
        TT(modt[:], PS[2][:, 0:48].rearrange("p (j c) -> p j c", c=2),
           adab.unsqueeze(2).broadcast_to([128, 24, 2]), ALU.add, [b_PS[2], b_pvec], [b_modt])
        nw = pvec[:, PV["nw"]:PV["nw"] + 8]
        TS(Amod[:], modt[:, 8:16, :], 1.0, None, ALU.add, None, [b_modt], [b_Amod])
        TT(Amod[:], Amod[:], nw.unsqueeze(2).broadcast_to([128, 8, 2]), ALU.mult, [b_Amod, b_pvec], [b_Amod])

    def layer_params(l):
        set_layer(l)
        pvec, b_pvec = pvecs[l], b_pvecs[l]
        P.barrier()
        DMA(rowb[:], rowb_d[l], [], [b_rowb])
        gsrc = gbd_d[l]
        for half in range(2):
            load_w(gsrc, half * 8, 8, 0, 128, gatew[:, half * 8:(half + 1) * 8, :], [b_gatew])
        lam = pvec[:, PV["rglam"]:PV["rglam"] + 8]
        TS(smallt[:, 0:8], lam, -1.0, None, ALU.mult, None, [b_pvec], [b_small])
        softplus_tile(cAt[:], smallt[:, 0:8], 8, smallt[:, 8:16], smallt[:, 16:24], [b_small], [b_cA], [b_small])
        TS(cAt[:], cAt[:], -8.0, None, ALU.mult, None, [b_cA], [b_cA])
        ACT(aneg[:], rowb[:, RB["alog"]:RB["alog"] + 16], AF.Exp, [b_rowb], [b_aneg])
        TS(aneg[:], aneg[:], -1.0, None, ALU.mult, None, [b_aneg], [b_aneg])

    def norm_tile(xt, bxt, n, A_ap, S_ap, dst_fn, dst_bufs, scr, prm_bufs):
        xsq, rstd, tmp, bscr = scr["xsq"], scr["rstd"], scr["tmp"], scr["b"]
        ACT(xsq[:, :, 0:n], xt, AF.Square, bxt, bscr)
        bi = 6
        for k in range(8):
            MM(PS[bi][:, 0:n], ones_b, xsq[:, k, 0:n], k == 0, k == 7, bscr + [b_const], [b_PS[bi]])
        ACT(rstd[:, 0:n], PS[bi][:, 0:n], AF.Ln, [b_PS[bi], b_eps], bscr, scale=1.0 / D, bias=epsc[:, 0:1])
        ACT(rstd[:, 0:n], rstd[:, 0:n], AF.Exp, bscr, bscr, scale=-0.5)
        for k in range(8):
            TT(tmp[:, 0:n], xt[:, k, :], rstd[:, 0:n], ALU.mult, bxt + bscr, bscr)
            if S_ap is None:
                ACT(dst_fn(k), tmp[:, 0:n], AF.Identity, bscr + prm_bufs, dst_bufs, scale=A_ap(k))
            else:
                ACT(dst_fn(k), tmp[:, 0:n], AF.Identity, bscr + prm_bufs, dst_bufs, scale=A_ap(k),
                    bias=S_ap(k))

    def rg_branch(l, hsrc, hbufs, Tn, is_ctx, with_output):
        pvec, b_pvec = cur["pvec"], cur["b_pvec"]
        arena_reset()
        S = [af32([128, T]) for _ in range(5)]
        bS = [Buf("S%d" % i) for i in range(5)]
        ubf = abf([128, T])
        b_ubf = Buf("ubf")
        obf = abf([128, T])
        b_obf = Buf("obf")
        raw, u, r, ig, hf = S
        b_raw, b_u, b_r, b_i, b_hf = bS
        ic = 1 if is_ctx else 0
        for cc in range(4):
            def ev_raw(ps, bps, t0, n, tt):
                CP(raw[:, t0:t0 + n], ps, [bps], [b_raw], eng=cpeng[0])
            proj_cm(l, OFF["rg_x"] + cc * 128, 128, hsrc, hbufs, Tn, ev_raw)
            cw = lambda k: pvec[:, PV["rgcw"] + cc * 4 + k:PV["rgcw"] + cc * 4 + k + 1]
            cb = pvec[:, PV["rgcb"] + cc:PV["rgcb"] + cc + 1]
            TS(u[:, 0:Tn], raw[:, 0:Tn], cw(2), cb, ALU.mult, ALU.add, [b_raw, b_pvec], [b_u])
            STT(u[:, 2:Tn], raw[:, 0:Tn - 2], cw(0), u[:, 2:Tn], ALU.mult, ALU.add, [b_raw, b_pvec, b_u], [b_u])
            STT(u[:, 1:Tn], raw[:, 0:Tn - 1], cw(1), u[:, 1:Tn], ALU.mult, ALU.add, [b_raw, b_pvec, b_u], [b_u])
            STT(u[:, 0:Tn - 1], raw[:, 1:Tn], cw(3), u[:, 0:Tn - 1], ALU.mult, ALU.add, [b_raw, b_pvec, b_u], [b_u])
            CP(ubf[:, 0:Tn], u[:, 0:Tn], [b_u], [b_ubf], eng="scalar")
            CUT(1)
            for d in range(2):
                nt = (Tn + 511) // 512
                for tt in range(nt):
                    t0 = tt * 512
                    n = min(512, Tn - t0)
                    for ax, (dst, bdst, bcol) in enumerate(((r, b_r, PV["rgba"]), (ig, b_i, PV["rgbx"]))):
                        bi = 4 + (psrot[0] % 2)
                        psrot[0] += 1
                        MM(PS[bi][:, 0:n], gatew[:, d * 8 + ax * 4 + cc, :], ubf[:, t0:t0 + n], True, True,
                           [b_gatew, b_ubf], [b_PS[bi]])
                        ACT(dst[:, t0:t0 + n], PS[bi][:, 0:n], AF.Sigmoid, [b_PS[bi], b_pvec], [bdst],
                            bias=pvec[:, bcol + d * 4 + cc:bcol + d * 4 + cc + 1])
                CUT(2)
                ACT(r[:, 0:Tn], r[:, 0:Tn], AF.Exp, [b_r, b_cA], [b_r], scale=cAt[:, d * 4 + cc:d * 4 + cc + 1])
                ACT(raw[:, 0:Tn], r[:, 0:Tn], AF.Square, [b_r], [b_raw])
                ACT(raw[:, 0:Tn], raw[:, 0:Tn], AF.Sqrt, [b_raw, b_eps], [b_raw], scale=-1.0, bias=epsc[:, 1:2])
                TT(ig[:, 0:Tn], ig[:, 0:Tn], u[:, 0:Tn], ALU.mult, [b_i, b_u], [b_i])
                TT(ig[:, 0:Tn], ig[:, 0:Tn], raw[:, 0:Tn], ALU.mult, [b_i, b_raw], [b_i])
                CUT(3)
                init = (zero8[:, 0:1] if is_ctx else rgst[:, cc * 2 + d:cc * 2 + d + 1])
                if d == 0:
                    P.op("vector", (lambda o, a, v, i0: (lambda e: e.tensor_tensor_scan(
                        out=o, data0=a, data1=v, initial=i0, op0=ALU.mult, op1=ALU.add)))(
                        hf[:, 0:Tn], r[:, 0:Tn], ig[:, 0:Tn], init), reads=[b_r, b_i, b_rgst, b_eps], writes=[b_hf])
                else:
                    P.op("vector", (lambda o, a, v, i0: (lambda e: e.tensor_tensor_scan(
                        out=o, data0=a, data1=v, initial=i0, op0=ALU.mult, op1=ALU.add)))(
                        raw[:, 0:Tn][:, ::-1], r[:, 0:Tn][:, ::-1], ig[:, 0:Tn][:, ::-1], init),
                        reads=[b_r, b_i, b_rgst, b_eps], writes=[b_raw])
            CUT(4)
            if is_ctx:
                CP(rgst[:, cc * 2:cc * 2 + 1], hf[:, Tn - 1:Tn], [b_hf], [b_rgst], eng="vector")
                CP(rgst[:, cc * 2 + 1:cc * 2 + 2], raw[:, 0:1], [b_raw], [b_rgst], eng="vector")
            if with_output:
                TT(hf[:, 0:Tn], hf[:, 0:Tn], raw[:, 0:Tn], ALU.add, [b_hf, b_raw], [b_hf])

                def ev_g(ps, bps, t0, n, tt):
                    ACT(r[:, t0:t0 + n], ps, AF.Silu, [bps], [b_r])
                proj_cm(l, OFF["rg_g"] + cc * 128, 128, hsrc, hbufs, Tn, ev_g)
                TT(obf[:, 0:Tn], hf[:, 0:Tn], r[:, 0:Tn], ALU.mult, [b_hf, b_r], [b_obf])
                yd = (ycatc_d if is_ctx else ycat_d)
                if cc + 1 < 4:
                    prefetch_win(l, OFF["rg_x"] + (cc + 1) * 128, 128)
                DMA(yd[cc * 128:(cc + 1) * 128, 0:Tn], obf[:, 0:Tn], [b_obf], [b_ycat[(ic, cc)]], q="gpsimd")

    def sc_branch(l, hsrc, hbufs, Tn, is_ctx):
        pvec, b_pvec = cur["pvec"], cur["b_pvec"]
        arena_reset()
        S = [af32([128, T]) for _ in range(3)]
        bS = [Buf("S%d" % i) for i in range(3)]
        obf = abf([128, T])
        b_obf = Buf("obf")
        s0, s1, s2 = S
        b0, b1, b2 = bS
        ic = 1 if is_ctx else 0
        rowlen = Tn if is_ctx else 64

        def v3(ap):
            return ap.rearrange("p (r c) -> p r c", c=rowlen)
        for cc in range(4):
            def ev_c(ps, bps, t0, n, tt):
                CP(s0[:, t0:t0 + n], ps, [bps], [b0], eng=cpeng[0])
            proj_cm(l, OFF["sc_c"] + cc * 128, 128, hsrc, hbufs, Tn, ev_c)

            def ev_x(ps, bps, t0, n, tt):
                TT(s1[:, t0:t0 + n], ps, s0[:, t0:t0 + n], ALU.mult, [bps, b0], [b1])
            proj_cm(l, OFF["sc_x"] + cc * 128, 128, hsrc, hbufs, Tn, ev_x)
            cw = lambda k: pvec[:, PV["sccw"] + cc * 3 + k:PV["sccw"] + cc * 3 + k + 1]
            TS(s2[:, 0:Tn], s1[:, 0:Tn], cw(1), None, ALU.mult, None, [b1, b_pvec], [b2])
            a2 = v3(s2[:, 0:Tn])
            a1 = v3(s1[:, 0:Tn])
            STT(a2[:, :, 1:rowlen], a1[:, :, 0:rowlen - 1], cw(0), a2[:, :, 1:rowlen], ALU.mult, ALU.add,
                [b1, b2, b_pvec], [b2])
            STT(a2[:, :, 0:rowlen - 1], a1[:, :, 1:rowlen], cw(2), a2[:, :, 0:rowlen - 1], ALU.mult, ALU.add,
                [b1, b2, b_pvec], [b2])

            def ev_b(ps, bps, t0, n, tt):
                TT(s2[:, t0:t0 + n], ps, s2[:, t0:t0 + n], ALU.mult, [bps, b2], [b2])
            proj_cm(l, OFF["sc_b"] + cc * 128, 128, hsrc, hbufs, Tn, ev_b)

            def ev_g(ps, bps, t0, n, tt):
                ACT(s0[:, t0:t0 + n], ps, AF.Silu, [bps], [b0])
            proj_cm(l, OFF["sc_g"] + cc * 128, 128, hsrc, hbufs, Tn, ev_g)
            TT(obf[:, 0:Tn], s2[:, 0:Tn], s0[:, 0:Tn], ALU.mult, [b2, b0], [b_obf])
            yd = (ycatc_d if is_ctx else ycat_d)
            if cc + 1 < 4:
                prefetch_win(l, OFF["sc_c"] + (cc + 1) * 128, 128)
            DMA(yd[(4 + cc) * 128:(5 + cc) * 128, 0:Tn], obf[:, 0:Tn], [b_obf], [b_ycat[(ic, 4 + cc)]], q="gpsimd")

    def fn_branch(l, hsrc, hbufs, Tn, is_ctx):
        arena_reset()
        ic = 1 if is_ctx else 0
        ntt = Tn // 128
        fold = not is_ctx
        Hh = Tn // 2
        nth = Hh // 128
        PQ = [abf([128, (nth + 1) if fold else ntt, 256]) for _ in range(4)]
        b_PQ = [Buf("PQ%d" % g) for g in range(4)]
        SG = [abf([128, Tn]) for _ in range(4)]
        b_SG = [Buf("SG%d" % g) for g in range(4)]
        ost = [abf([128, 256]) for _ in range(2)]
        b_ost = [Buf("ost0"), Buf("ost1")]
        vtpos = apos[0]
        VT = abf([128, Tn])
        b_VT = Buf("VT")
        if fold:
            VP = abf([128, Hh])
            VM = abf([128, Hh])
            b_VF = Buf("VF")
        for g in range(4):
            def ev_v(ps, bps, t0, n, tt):
                CP(VT[:, t0:t0 + n], ps, [bps], [b_VT], eng=cpeng[0])
            proj_cm(l, OFF["fn_x"] + g * 128, 128, hsrc, hbufs, Tn, ev_v)
            if fold:
                rev = VT[:, Hh + 1:Tn][:, ::-1]
                CP(VP[:, 0:1], VT[:, 0:1], [b_VT], [b_VF], eng="vector")
                MEMSET(VM[:, 0:1], 0.0, [b_VF])
                TT(VP[:, 1:Hh], VT[:, 1:Hh], rev, ALU.add, [b_VT], [b_VF])
                TT(VM[:, 1:Hh], VT[:, 1:Hh], rev, ALU.subtract, [b_VT], [b_VF])
                for t4 in range(0, nth, 2):
                    bi = 4 + (psrot[0] % 2)
                    psrot[0] += 1
                    for j in range(2):
                        tt = t4 + j
                        MM(PS[bi][:, j * 256:j * 256 + 128], VP[:, tt * 128:(tt + 1) * 128], cgsg[:, 0:128], True, True,
                           [b_VF, b_const], [b_PS[bi]])
                        MM(PS[bi][:, j * 256 + 128:(j + 1) * 256], VM[:, tt * 128:(tt + 1) * 128], cgsg[:, 128:256], True,
                           True, [b_VF, b_const], [b_PS[bi]])
                    CP(PQ[g][:, t4:t4 + 2, :], PS[bi][:, 0:512].rearrange("p (a b) -> p a b", b=256),
                       [b_PS[bi]], [b_PQ[g]], eng=("scalar" if g % 2 else "vector"))
                bi = 4 + (psrot[0] % 2)
                psrot[0] += 1
                MM(PS[bi][:, 0:128], VT[:, Hh:Hh + 128], cgsg[:, 0:128], True, True, [b_VT, b_const], [b_PS[bi]])
                CP(PQ[g][:, nth, 0:128], PS[bi][:, 0:128], [b_PS[bi]], [b_PQ[g]], eng=("scalar" if g % 2 else "vector"))

                def ev_g2(ps, bps, t0, n, tt):
                    ACT(SG[g][:, t0:t0 + n], ps, AF.Silu, [bps], [b_SG[g]])
                proj_cm(l, OFF["fn_g"] + g * 128, 128, hsrc, hbufs, Tn, ev_g2)
                continue
            for t4 in range(0, ntt, 2):
                bi = 4 + (psrot[0] % 2)
                psrot[0] += 1
                nn = min(2, ntt - t4)
                for j in range(nn):
                    tt = t4 + j
                    MM(PS[bi][:, j * 256:(j + 1) * 256], VT[:, tt * 128:(tt + 1) * 128], cgsg, True, True,
                       [b_VT, b_const], [b_PS[bi]])
                CP(PQ[g][:, t4:t4 + nn, :], PS[bi][:, 0:nn * 256].rearrange("p (a b) -> p a b", b=256),
                   [b_PS[bi]], [b_PQ[g]], eng=("scalar" if g % 2 else "vector"))

            def ev_g(ps, bps, t0, n, tt):
                ACT(SG[g][:, t0:t0 + n], ps, AF.Silu, [bps], [b_SG[g]])
            proj_cm(l, OFF["fn_g"] + g * 128, 128, hsrc, hbufs, Tn, ev_g)
        scale = 1.0 / float(np.sqrt(Tn * 128.0))
        nkt = Tn // 256
        if is_ctx:
            dts = [(dft256[:, :, 0:256], dft256[:, :, 256:512])]
            b_dt = [[b_const, b_const]]
        else:
            P.barrier()
            hflat = hT[:].rearrange("p k t -> p (k t)")
            dts = []
            b_dt = []
            for s in range(2):
                c_ap = hflat[:, s * 16384:s * 16384 + 4096].rearrange("p (a b) -> p a b", b=256)
                s_ap = hflat[:, s * 16384 + 8192:s * 16384 + 8192 + 4096].rearrange("p (a b) -> p a b", b=256)
                dts.append((c_ap, s_ap))
                b_dt.append([Buf("dftc%d" % s), Buf("dfts%d" % s)])
        yd = (ycatc_d if is_ctx else ycat_d)
        orot = 0
        if is_ctx:
            c_ap, s_ap = dts[0]
            for g in range(4):
                bi = (psrot[0] % 4)
                psrot[0] += 1
                for tt in range(ntt):
                    MM(PS[bi][:, 0:256], PQ[g][:, tt, 0:128], c_ap[:, tt, :], tt == 0, False,
                       [b_PQ[g], b_const], [b_PS[bi]])
                for tt in range(ntt):
                    MM(PS[bi][:, 0:256], PQ[g][:, tt, 128:256], s_ap[:, tt, :], False, tt == ntt - 1,
                       [b_PQ[g], b_const], [b_PS[bi]])
                o = orot % 2
                orot += 1
                STT(ost[o][:, :], PS[bi][:, 0:256], scale, SG[g][:, 0:256], ALU.mult, ALU.mult,
                    [b_PS[bi], b_SG[g]], [b_ost[o]])
                DMA(yd[(8 + g) * 128:(9 + g) * 128, 0:256], ost[o][:, :], [b_ost[o]], [b_ycat[(ic, 8 + g)]], q="gpsimd")
            return
        apos[0] = vtpos
        Bs = af32([128, 256])
        t1 = af32([128, 256])
        t2 = af32([128, 256])
        b_t = Buf("fn_t")
        ostm = [abf([128, 256]) for _ in range(2)]
        b_ostm = [Buf("ostm0"), Buf("ostm1")]
        o2k = abf([128, 8])
        b_o2k = Buf("o2k")
        prot = 0
        def dft_load(kt):
            s_ = kt % 2
            DMA(dts[s_][0], dftc_d[kt], [], [b_dt[s_][0]])
            DMA(dts[s_][1], dfts_d[kt], [], [b_dt[s_][1]])
        dft_load(0)
        for kt in range(Tn // 512):
            s = kt % 2
            k0 = kt * 256
            c_ap, s_ap = dts[s]
            if kt + 1 < Tn // 512:
                dft_load(kt + 1)
            for g in range(4):
                ba, bb = ((0, 1) if prot % 2 == 0 else (2, 3))
                prot += 1
                for tt in range(nth):
                    MM(PS[ba][:, 0:256], PQ[g][:, tt, 0:128], c_ap[:, tt, :], tt == 0, False,
                       [b_PQ[g], b_dt[s][0]], [b_PS[ba]])
                MM(PS[ba][:, 0:256], PQ[g][0:1, nth, 0:128], altrow[0:1, :], False, True, [b_PQ[g], b_const], [b_PS[ba]])
                for tt in range(nth):
                    MM(PS[bb][:, 0:256], PQ[g][:, tt, 128:256], s_ap[:, tt, :], tt == 0, tt == nth - 1,
                       [b_PQ[g], b_dt[s][1]], [b_PS[bb]])
                CP(Bs, PS[bb][:, 0:256], [b_PS[bb]], [b_t], eng="scalar")
                TT(t1, PS[ba][:, 0:256], Bs, ALU.add, [b_PS[ba], b_t], [b_t])
                TT(t2, PS[ba][:, 0:256], Bs, ALU.subtract, [b_PS[ba], b_t], [b_t])
                o = orot % 2
                orot += 1
                STT(ost[o][:, :], t1, scale, SG[g][:, k0:k0 + 256], ALU.mult, ALU.mult, [b_t, b_SG[g]], [b_ost[o]])
                DMA(yd[(8 + g) * 128:(9 + g) * 128, k0:k0 + 256], ost[o][:, :], [b_ost[o]], [b_ycat[(ic, 8 + g)]],
                    q="gpsimd")
                j0 = 1 if kt == 0 else 0
                lo = Tn - k0 - 255
                hi = Tn - k0 - j0 + 1
                wd = 256 - j0
                STT(ostm[o][:, 0:wd][:, ::-1], t2[:, j0:256], scale, SG[g][:, lo:hi][:, ::-1], ALU.mult, ALU.mult,
                    [b_t, b_SG[g]], [b_ostm[o]])
                DMA(yd[(8 + g) * 128:(9 + g) * 128, lo:hi], ostm[o][:, 0:wd], [b_ostm[o]], [b_ycat[(ic, 8 + g)]],
                    q="gpsimd")
        half = Tn // 2
        for g in range(4):
            for tt in range(nth):
                MM(PS[0][:, 2 * g:2 * g + 2], PQ[g][:, tt, 0:128], alt2, tt == 0, False, [b_PQ[g], b_const],
                   [b_PS[0]])
            MM(PS[0][:, 2 * g:2 * g + 2], PQ[g][0:1, nth, 0:128], ones_b[0:1, 0:2], False, True, [b_PQ[g], b_const],
               [b_PS[0]])
        for g in range(4):
            STT(o2k[:, 2 * g:2 * g + 1], PS[0][:, 2 * g:2 * g + 1], scale, SG[g][:, half:half + 1], ALU.mult, ALU.mult,
                [b_PS[0], b_SG[g]], [b_o2k])
        for g in range(4):
            DMA(yd[(8 + g) * 128:(9 + g) * 128, half:half + 1], o2k[:, 2 * g:2 * g + 1], [b_o2k], [b_ycat[(ic, 8 + g)]],
                q="gpsimd", slow=True)

    def ssd_branch(l, hsrc, hbufs, Tn, is_ctx, with_output):
        pvec, b_pvec = cur["pvec"], cur["b_pvec"]
        arena_reset()
        ic = 1 if is_ctx else 0
        NC = Tn // 128
        NW = NC * 16
        xs_tok = abf([128, NC, 512])
        b_xs = Buf("xs_tok")
        B_tok = abf([128, NC, 128])
        b_Bt = Buf("B_tok")
        BTm = [abf([128, Tn]) for _ in range(2)]
        b_BT = Buf("BT")
        CT = abf([128, Tn])
        b_CT = Buf("CT")
        b_wz = Buf("wz")
        CS = af32([128, NW])
        BIAS = af32([128, NW])
        ECS = af32([128, NW])
        WEND = af32([128, NW])
        DEC2 = af32([128, NC * 8])
        b_dt = Buf("dtstuff")
        mark = apos[0]
        s0pos = apos[0]
        DTt = af32([128, NW])
        LNDT = af32([128, NW])
        sc1 = af32([128, NW])
        sc2 = af32([128, NW])
        b_scr = Buf("dtscr")
        apos[0] = s0pos
        s0 = af32([128, Tn])
        s1 = af32([128, Tn])
        b_s0 = Buf("ssd_s0")
        b_s1 = Buf("ssd_s1")
        xbf = arena[:, s0pos // 4:s0pos // 4 + Tn // 2].bitcast(BF16)
        b_xbf = b_s0

        def v16(ap):
            return ap.rearrange("p (c q) -> p c q", q=16)
        s = wbrot[0]
        wbrot[0] ^= 1
        load_w(win_d[l].rearrange("(k p) c -> p k c", p=128), 0, 8, OFF["ssd_dt"], 16, wbf[s][:, :, 0:16],
               [b_wbf[s]])
        for c in range(NC):
            hb = [hbufs[min(c // 4, len(hbufs) - 1)]]
            for k in range(8):
                MM(PS[0][:, c * 16:(c + 1) * 16], hsrc[:, k, c * 128:(c + 1) * 128], wbf[s][:, k, 0:16], k == 0, k == 7,
                   [b_wbf[s]] + hb, [b_PS[0]])
        TT(v16(DTt), v16(PS[0][:, 0:NW]), rowb[:, RB["dtb"]:RB["dtb"] + 16].unsqueeze(1).broadcast_to([128, NC, 16]),
           ALU.add, [b_PS[0], b_rowb], [b_scr])
        CUT(9)
        softplus_tile(DTt, DTt, NW, sc1, sc2, [b_scr], [b_scr], [b_scr])
        ACT(LNDT, DTt, AF.Ln, [b_scr], [b_scr])
        CUT(10)
        TT(v16(sc1), v16(DTt), aneg[:, :].unsqueeze(1).broadcast_to([128, NC, 16]), ALU.mult, [b_scr, b_aneg], [b_scr])
        CUT(101)
        MM(PS[1][:, 0:NW], tri_f, sc1, True, True, [b_scr, b_const], [b_PS[1]])
        MM(PS[2][:, 0:NW], tri_b, sc1, True, True, [b_scr, b_const], [b_PS[2]])
        CP(v16(CS)[:, :, 0:8], v16(PS[1][:, 0:NW])[:, :, 0:8], [b_PS[1]], [b_dt], eng="vector")
        CP(v16(CS)[:, :, 8:16], v16(PS[2][:, 0:NW])[:, :, 8:16], [b_PS[2]], [b_dt], eng="vector")
        CUT(102)
        TT(BIAS, LNDT, CS, ALU.subtract, [b_scr, b_dt], [b_dt])
        ACT(ECS, CS, AF.Exp, [b_dt], [b_dt])
        CUT(103)
        MM(PS[1][:, 0:NW], ident_f[:, 127:128].broadcast_to([128, 128]), CS, True, True, [b_dt, b_const], [b_PS[1]])
        CUT(1031)
        MM(PS[2][:, 0:NW], ident_f[:, 0:1].broadcast_to([128, 128]), CS, True, True, [b_dt, b_const], [b_PS[2]])
        CUT(1032)
        CP(sc1, PS[1][:, 0:NW], [b_PS[1]], [b_scr], eng="vector")
        CP(DTt, PS[2][:, 0:NW], [b_PS[2]], [b_scr], eng="vector")
        TT(v16(sc2)[:, :, 0:8], v16(sc1)[:, :, 0:8], v16(BIAS)[:, :, 0:8], ALU.add, [b_scr, b_dt], [b_scr])
        TT(v16(sc2)[:, :, 8:16], v16(DTt)[:, :, 8:16], v16(BIAS)[:, :, 8:16], ALU.add, [b_scr, b_dt], [b_scr])
        CUT(104)
        ACT(WEND, sc2, AF.Exp, [b_scr], [b_dt])
        CUT(105)
        d2 = DEC2.rearrange("p (c d h) -> p c d h", d=2, h=4)
        for d in range(2):
            srcd = sc1 if d == 0 else DTt
            pv = srcd.rearrange("p (c d h) -> p c d h", d=2, h=8)
            ACT(d2[0:64, :, d, :], pv[0:64, :, d, 0:4], AF.Exp, [b_scr], [b_dt])
            ACT(d2[64:128, :, d, :], pv[64:128, :, d, 4:8], AF.Exp, [b_scr], [b_dt])
        CUT(11)
        P.barrier()
        MEMSET(BTm[0][64:128, 0:Tn], 0.0, [b_BT])
        MEMSET(BTm[1][0:64, 0:Tn], 0.0, [b_BT])
        for cch in range(6):
            def ev_raw(ps, bps, t0, n, tt):
                CP(s0[:, t0:t0 + n], ps, [bps], [b_s0], eng=cpeng[0])
            proj_cm(l, OFF["ssd_xbc"] + cch * 128, 128, hsrc, hbufs, Tn, ev_raw)
            cw = lambda k: pvec[:, PV["sdcw"] + cch * 4 + k:PV["sdcw"] + cch * 4 + k + 1]
            cb = pvec[:, PV["sdcb"] + cch:PV["sdcb"] + cch + 1]
            TS(s1[:, 0:Tn], s0[:, 0:Tn], cw(2), None, ALU.mult, None, [b_s0, b_pvec], [b_s1])
            STT(s1[:, 2:Tn], s0[:, 0:Tn - 2], cw(0), s1[:, 2:Tn], ALU.mult, ALU.add, [b_s0, b_s1, b_pvec], [b_s1])
            STT(s1[:, 1:Tn], s0[:, 0:Tn - 1], cw(1), s1[:, 1:Tn], ALU.mult, ALU.add, [b_s0, b_s1, b_pvec], [b_s1])
            STT(s1[:, 0:Tn - 1], s0[:, 1:Tn], cw(3), s1[:, 0:Tn - 1], ALU.mult, ALU.add, [b_s0, b_s1, b_pvec], [b_s1])
            if cch < 5:
                dst, bdst = xbf, b_xbf
            else:
                dst, bdst = CT, b_CT
            ACT(dst[:, 0:Tn], s1[:, 0:Tn], AF.Silu, [b_s1, b_pvec], [bdst], bias=cb)
            if cch == 4:
                CP(BTm[0][0:64, 0:Tn], xbf[0:64, 0:Tn], [b_xbf], [b_BT], eng="vector")
                CP(BTm[1][64:128, 0:Tn], xbf[64:128, 0:Tn], [b_xbf], [b_BT], eng="vector")
            if cch < 5:
                for c8 in range(0, NC, 8):
                    nn = min(8, NC - c8)
                    for j in range(nn):
                        c = c8 + j
                        TR(PSB[:, j * 128:(j + 1) * 128], dst[:, c * 128:(c + 1) * 128], ident_b, [bdst, b_const],
                           [b_PSB])
                    src = PSB[:, 0:nn * 128].rearrange("p (a b) -> p a b", b=128)
                    if cch < 4:
                        CP(xs_tok[:, c8:c8 + nn, cch * 128:(cch + 1) * 128], src, [b_PSB], [b_xs],
                           eng=("scalar" if cch % 2 else "vector"))
                    else:
                        CP(B_tok[:, c8:c8 + nn, :], src, [b_PSB], [b_Bt], eng="vector")
        CUT(12)
        P.barrier()
        apos[0] = mark
        wz = abf([128, 8, 512])
        if with_output:
            for q4 in range(4):
                load_w(win_d[l].rearrange("(k p) c -> p k c", p=128), 0, 8, OFF["ssd_z"] + q4 * 128, 128,
                       wz[:, :, q4 * 128:(q4 + 1) * 128], [b_wz])
        Wt = [abf([128, 8, 128]) for _ in range(3)]
        b_W = [Buf("W0"), Buf("W1"), Buf("W2")]
        BW = [abf([128, 4, 2, 64]) for _ in range(2)]
        b_BW = [Buf("BW0"), Buf("BW1")]
        hTs = [af32([128, 4, 128]) for _ in range(2)]
        b_hTs = [Buf("hTs0"), Buf("hTs1")]
        hTbf = [abf([128, 8, 64]) for _ in range(2)]
        b_hTbf = [Buf("hTbf0"), Buf("hTbf1")]
        HEB = [abf([128, 8, 64]) for _ in range(2)]
        b_HEB = [Buf("HEB0"), Buf("HEB1")]
        for d in range(2):
            MEMSET(hTbf[d][:], 0.0, [b_hTbf[d]])
        YA = af32([128, 512])
        ybpos = apos[0]
        YB = af32([128, 512])
        ZS = af32([128, 512])
        b_YA, b_YB, b_ZS = Buf("YA"), Buf("YB"), Buf("ZS")
        ybf32 = YB
        b_y32 = b_YB
        tmpst = arena[:, ybpos // 4:ybpos // 4 + 512].rearrange("p (a b) -> p a b", b=128)
        b_tmpst = b_YB
        ybf = abf([128, 512])
        b_ybf = Buf("ybf")
        yT = [abf([128, 4, 128])] * 2
        b_yT = [Buf("yT0")] * 2
        SSt = af32([128, 4])
        b_SS = Buf("SS")
        STs = abf([128, 256])
        b_STs = Buf("STs")

        def init_state(d):
            if is_ctx:
                MEMSET(hTs[d][:], 0.0, [b_hTs[d]])
            else:
                CP(hTs[d][:], ssdh0[d][:], [b_ssdh0[d]], [b_hTs[d]], eng="vector")

        def make_hTbf(d):
            CP(hTbf[d][0:64, 0:4, :], hTs[d][0:64, :, 0:64], [b_hTs[d]], [b_hTbf[d]], eng="scalar")
            CP(hTbf[d][64:128, 4:8, :], hTs[d][64:128, :, 64:128], [b_hTs[d]], [b_hTbf[d]], eng="scalar")

        def state_update(d, c):
            for g in range(2):
                col = c * 16 + d * 8 + g * 4
                TT(BW[d][:, :, g, :], B_tok[:, c, g * 64:(g + 1) * 64].unsqueeze(1).broadcast_to([128, 4, 64]),
                   WEND[:, col:col + 4].unsqueeze(2).broadcast_to([128, 4, 64]), ALU.mult, [b_Bt, b_dt], [b_BW[d]])
            xv = xs_tok[:, c, :].rearrange("p (g hh n) -> p hh g n", g=2, hh=4)
            for hh in range(4):
                MM(PS[6][:, hh * 128:(hh + 1) * 128], BW[d][:, hh, :, :], xv[:, hh, :, :], True, True,
                   [b_BW[d], b_xs], [b_PS[6]])
            TT(tmpst[:], hTs[d][:], d2[:, c, d, :].unsqueeze(2).broadcast_to([128, 4, 128]), ALU.mult,
               [b_hTs[d], b_dt], [b_tmpst])
            TT(hTs[d][:], tmpst[:], PS[6][:, :].rearrange("p (a b) -> p a b", b=128), ALU.add,
               [b_tmpst, b_PS[6]], [b_hTs[d]])

        init_state(1)
        for c in range(NC - 1, -1, -1):
            if with_output:
                make_hTbf(1)
                DMA(heb_d[c].rearrange("p (a b) -> p a b", b=64), hTbf[1][:], [b_hTbf[1]], [b_heb_d[c]], q="gpsimd")
            state_update(1, c)
        if is_ctx:
            CP(ssdh0[1][:], hTs[1][:], [b_hTs[1]], [b_ssdh0[1]], eng="vector")
        CUT(13)
        init_state(0)

        def wgen(c, WA, bA, WB, bB):
            for d, (Wd, bW) in enumerate(((WA, bA), (WB, bB))):
                for g in range(2):
                    bi = (d * 2 + g) % 2
                    MM(PS[bi][:, :], ident_b, mask_bf[d], True, False, [b_const], [b_PS[bi]])
                    for hh in range(4):
                        col = c * 16 + d * 8 + g * 4 + hh
                        MM(PS[bi][:, hh * 128:(hh + 1) * 128], CS[:, col:col + 1].broadcast_to([128, 128]), ident_f,
                           False, hh == 3, [b_dt, b_const], [b_PS[bi]])
                    for hh in range(4):
                        col = c * 16 + d * 8 + g * 4 + hh
                        ACT(Wd[:, g * 4 + hh, :], PS[bi][:, hh * 128:(hh + 1) * 128], AF.Exp, [b_PS[bi], b_dt],
                            [bW], bias=BIAS[:, col:col + 1])

        def v64(ap):
            return ap.rearrange("p (h n) -> p h n", n=64)
        if with_output:
            DMA(HEB[0][:], heb_d[0].rearrange("p (a b) -> p a b", b=64), [b_heb_d[0]], [b_HEB[0]])
            wgen(0, Wt[0], b_W[0], Wt[1], b_W[1])
        for c in range(NC):
            if with_output:
                hs = c % 2
                WA, bA = Wt[c % 3], b_W[c % 3]
                WB, bB = Wt[(c + 1) % 3], b_W[(c + 1) % 3]
                WN, bN = Wt[(c + 2) % 3], b_W[(c + 2) % 3]
                if c + 1 < NC:
                    DMA(HEB[1 - hs][:], heb_d[c + 1].rearrange("p (a b) -> p a b", b=64), [b_heb_d[c + 1]],
                        [b_HEB[1 - hs]])
                make_hTbf(0)
                tok = slice(c * 128, (c + 1) * 128)
                TT(WA[:], WA[:], WB[:], ALU.add, [bA, bB], [bA])
                for g in range(2):
                    MM(PS[2][:, g * 128:(g + 1) * 128], BTm[g][:, tok], CT[:, tok], True, True, [b_BT, b_CT], [b_PS[2]])
                CP(STs[:, :], PS[2][:, 0:256], [b_PS[2]], [b_STs], eng="scalar")
                for g in range(2):
                    TT(WA[:, g * 4:(g + 1) * 4, :], WA[:, g * 4:(g + 1) * 4, :],
                       STs[:, g * 128:(g + 1) * 128].unsqueeze(1).broadcast_to([128, 4, 128]), ALU.mult,
                       [bA, b_STs], [bA])
                for h in range(8):
                    MM(PS[3][:, h * 64:(h + 1) * 64], WA[:, h, :], xs_tok[:, c, h * 64:(h + 1) * 64], True, True,
                       [bA, b_xs], [b_PS[3]])
                MM(PS[4][:, :], CT[:, tok], hTbf[0][:].rearrange("p a b -> p (a b)"), True, True, [b_CT, b_hTbf[0]],
                   [b_PS[4]])
                MM(PS[5][:, :], CT[:, tok], HEB[hs][:].rearrange("p a b -> p (a b)"), True, True, [b_CT, b_HEB[hs]],
                   [b_PS[5]])
                hb = [hbufs[min(c // 4, len(hbufs) - 1)]]
                for k in range(8):
                    MM(PS[2][:, :], hsrc[:, k, tok], wz[:, k, :], k == 0, k == 7, [b_wz] + hb, [b_PS[2]])
            state_update(0, c)
            if with_output:
                if c + 1 < NC:
                    wgen(c + 1, WB, bB, WN, bN)
                ACT(ZS, PS[2][:, :], AF.Silu, [b_PS[2]], [b_ZS])
                TT(v64(YA), v64(PS[4][:, :]), ECS[:, c * 16:c * 16 + 8].unsqueeze(2).broadcast_to([128, 8, 64]), ALU.mult,
                   [b_PS[4], b_dt], [b_YA])
                TT(v64(ybf32), v64(PS[5][:, :]), ECS[:, c * 16 + 8:c * 16 + 16].unsqueeze(2).broadcast_to([128, 8, 64]),
                   ALU.mult, [b_PS[5], b_dt], [b_y32])
                TT(YA, YA, ybf32, ALU.add, [b_YA, b_y32], [b_YA])
                TT(YA, YA, PS[3][:, :], ALU.add, [b_YA, b_PS[3]], [b_YA])
                TT(ybf32, xs_tok[:, c, :], rowb[:, RB["dbc"]:RB["dbc"] + 512], ALU.mult, [b_xs, b_rowb], [b_y32])
                TT(YA, YA, ybf32, ALU.add, [b_YA, b_y32], [b_YA])
                TT(YA, YA, ZS, ALU.mult, [b_YA, b_ZS], [b_YA])
                ACT(ybf32, YA, AF.Square, [b_YA], [b_y32, b_SS], accum=SSt[:, 0:1])
                ACT(SSt[:, 1:2], SSt[:, 0:1], AF.Ln, [b_SS, b_eps], [b_SS], scale=1.0 / 512.0, bias=epsc[:, 0:1])
                ACT(SSt[:, 2:3], SSt[:, 1:2], AF.Exp, [b_SS], [b_SS], scale=-0.5)
                STT(ybf, YA, SSt[:, 2:3], rowb[:, RB["snw"]:RB["snw"] + 512], ALU.mult, ALU.mult,
                    [b_YA, b_SS, b_rowb], [b_ybf])
                for j in range(4):
                    TR(PSB[:, j * 128:(j + 1) * 128], ybf[:, j * 128:(j + 1) * 128], ident_b, [b_ybf, b_const], [b_PSB])
                ys = c % 2
                CP(yT[ys][:], PSB[:, 0:512].rearrange("p (a b) -> p a b", b=128), [b_PSB], [b_yT[ys]], eng="scalar")
                yd = (ycatc_d if is_ctx else ycat_d)
                DMA(yd[12 * 128:16 * 128, tok].rearrange("(j p) t -> p j t", p=128), yT[ys][:], [b_yT[ys]],
                    [b_ycat[(ic, 12 + j)] for j in range(4)], q="gpsimd")
        if is_ctx:
            CP(ssdh0[0][:], hTs[0][:], [b_hTs[0]], [b_ssdh0[0]], eng="vector")

    def outproj_phase(l, Tn, is_ctx, x_src_d, x_dst_d, last):
        arena_reset()
        modt, b_modt = modts[l], b_modts[l]
        ic = 1 if is_ctx else 0
        col = 1 if is_ctx else 0
        NT = min(512, Tn)
        wo = abf([128, 16, D])
        b_wo = Buf("wo")
        yc, xsqs = [], []
        for _ in range(2):
            pos = apos[0]
            yc.append(abf([128, 16, NT]))
            xsqs.append(arena[:, pos // 4:pos // 4 + 8 * NT // 2].bitcast(BF16).rearrange("p (a b) -> p a b", b=NT))
        b_yc = [Buf("yc0"), Buf("yc1")]
        xt = [af32([128, 8, NT]) for _ in range(2)]
        b_xt = [Buf("xt0"), Buf("xt1")]
        xn = xt
        b_xn = b_xt
        rstd_t = af32([128, NT])
        tmp_t = af32([128, NT])
        b_nscr = Buf("nscr")
        scrs = [dict(xsq=xsqs[i], rstd=rstd_t, tmp=tmp_t, b=[b_yc[i], b_nscr]) for i in range(2)]
        ofin = xt
        b_ofin = b_xt
        wsrc = wout_d[l].rearrange("(k p) c -> p k c", p=128)
        for half in range(2):
            for m in range(8):
                load_w(wsrc, half * 8, 8, m * 128, 128, wo[:, half * 8:(half + 1) * 8, m * 128:(m + 1) * 128], [b_wo])
        yd = (ycatc_d if is_ctx else ycat_d)
        dst_h = hTc if is_ctx else hT
        dst_b = b_hTc if is_ctx else b_hT
        def issue_loads(tt):
            s = tt % 2
            tok = slice(tt * NT, (tt + 1) * NT)
            DMA(yc[s][:], yd[:, tok].rearrange("(k p) t -> p k t", p=128), [b_ycat[(ic, j)] for j in range(16)],
                [b_yc[s]])
            x1b = [b_x1[(tt * NT) // 256 + i] for i in range(max(1, NT // 256))]
            xrd = x1b if (x_src_d is x1_d) else []
            DMA(xt[s][:], x_src_d[:, tok].rearrange("(k p) t -> p k t", p=128), xrd, [b_xt[s]])
        issue_loads(0)
        for tt in range(Tn // NT):
            s = tt % 2
            tok = slice(tt * NT, (tt + 1) * NT)
            x1b = [b_x1[(tt * NT) // 256 + i] for i in range(max(1, NT // 256))]
            if tt + 1 < Tn // NT:
                issue_loads(tt + 1)
            for m in range(8):
                bi = psrot[0] % 4
                psrot[0] += 1
                for k in range(16):
                    MM(PS[bi][:, 0:NT], wo[:, k, m * 128:(m + 1) * 128], yc[s][:, k, :], k == 0, k == 15,
                       [b_wo, b_yc[s]], [b_PS[bi]])
                STT(xn[s][:, m, :], PS[bi][:, 0:NT], modt[:, 16 + m, col:col + 1], xt[s][:, m, :], ALU.mult, ALU.add,
                    [b_PS[bi], b_modt, b_xt[s]], [b_xn[s]])
            if x_dst_d is not None:
                DMA(x_dst_d[:, tok].rearrange("(k p) t -> p k t", p=128), xn[s][:], [b_xn[s]], x1b, q="gpsimd")
            if last:
                pv = pvecs[l]
                norm_tile(xn[s][:], [b_xn[s]], NT, (lambda k: pv[:, PV["fnw"] + k:PV["fnw"] + k + 1]), None,
                          (lambda k, s=s: ofin[s][:, k, :]), [b_ofin[s]], scrs[s], [b_pvecs[l]])
                DMA(outT_d[:, tok].rearrange("(k p) t -> p k t", p=128), ofin[s][:], [b_ofin[s]], [b_out], q="gpsimd")
            else:
                An, Mn = Amods[l + 1], modts[l + 1]
                norm_tile(xn[s][:], [b_xn[s]], NT, (lambda k: An[:, k, col:col + 1]), (lambda k: Mn[:, k, col:col + 1]),
                          (lambda k, tok=tok: dst_h[:, k, tok]), [dst_b[min(tt * NT // 512, len(dst_b) - 1)]], scrs[s],
                          [b_Amods[l + 1], b_modts[l + 1]])

    def input_norm(x_src_d, Tn, col, dst, dbufs):
        arena_reset()
        NT = 256
        xt = [af32([128, 8, NT]) for _ in range(2)]
        b_xt = [Buf("xt0"), Buf("xt1")]
        scr = dict(xsq=abf([128, 8, NT]), rstd=af32([128, NT]), tmp=af32([128, NT]), b=[Buf("nscr")])
        A0, M0 = Amods[0], modts[0]
        def ld(tt):
            DMA(xt[tt % 2][:], x_src_d[:, tt * NT:(tt + 1) * NT].rearrange("(k p) t -> p k t", p=128), [], [b_xt[tt % 2]])
        for tt in range(Tn // NT):
            s = tt % 2
            tok = slice(tt * NT, (tt + 1) * NT)
            ld(tt)
            norm_tile(xt[s][:], [b_xt[s]], NT, (lambda k: A0[:, k, col:col + 1]), (lambda k: M0[:, k, col:col + 1]),
                      (lambda k, tok=tok: dst[:, k, tok]), [dbufs[min(tt // 2, len(dbufs) - 1)]], scr,
                      [b_Amods[0], b_modts[0]])

    phases = []
    for l in range(NL):
        phases.append(("ada%d" % l, (lambda l=l: ada_setup(l))))
    phases.append(("nctx", lambda: input_norm(ctxT_d, TC, 1, hTc, b_hTc)))
    phases.append(("nx", lambda: input_norm(xT_d, T, 0, hT, b_hT)))
    for l in range(NL):
        last = (l == NL - 1)
        phases.append(("par%d" % l, (lambda l=l: layer_params(l))))
        phases.append(("crg%d" % l, (lambda l=l, last=last: rg_branch(l, hTc, b_hTc, TC, True, not last))))
        phases.append(("cssd%d" % l, (lambda l=l, last=last: ssd_branch(l, hTc, b_hTc, TC, True, not last))))
        if not last:
            phases.append(("csc%d" % l, (lambda l=l: sc_branch(l, hTc, b_hTc, TC, True))))
            phases.append(("cfn%d" % l, (lambda l=l: fn_branch(l, hTc, b_hTc, TC, True))))
            phases.append(("cout%d" % l, (lambda l=l: outproj_phase(l, TC, True, ctxT_d, None, False))))
        phases.append(("rg%d" % l, (lambda l=l: rg_branch(l, hT, b_hT, T, False, True))))
        phases.append(("sc%d" % l, (lambda l=l: sc_branch(l, hT, b_hT, T, False))))
        phases.append(("ssd%d" % l, (lambda l=l: ssd_branch(l, hT, b_hT, T, False, True))))
        phases.append(("fn%d" % l, (lambda l=l: fn_branch(l, hT, b_hT, T, False))))
        phases.append(("out%d" % l, (lambda l=l, last=last: outproj_phase(
            l, T, False, (xT_d if l == 0 else x1_d), (None if last else x1_d), last))))
    stop = (dbg or {}).get("stop")
    skip = (dbg or {}).get("skip", ())
    for name, fn in phases:
        if name not in skip:
            try:
                fn()
            except _Cut:
                break
        if stop == name:
            break
    if dbg:
        P.barrier()
        bd = Buf("dbgd")
        DMA(dbg_hT, hT[:], [], [bd], semkey="d_dbg")
        DMA(dbg_hTc, hTc[:], [], [bd], semkey="d_dbg")
        for l in range(NL):
            DMA(dbg_mod[l], modts[l][:].rearrange("p a b -> p (a b)"), [], [bd], semkey="d_dbg")
        DMA(dbg_st[:, 0:8], rgst[:], [], [bd], semkey="d_dbg")
        for d in range(2):
            DMA(dbg_st[:, 8 + d * 512:8 + (d + 1) * 512], ssdh0[d][:].rearrange("p a b -> p (a b)"), [], [bd],
                semkey="d_dbg")
    print("ops per engine:", {e: len(P.ops[e]) for e in ENGS})
    P.finalize()
    return nc


_BF = ml_dtypes.bfloat16


def _host_consts():
    p = np.arange(128)
    ident = np.eye(128, dtype=np.float32)
    tri_f = (p[:, None] <= p[None, :]).astype(np.float32)
    tri_b = (p[:, None] >= p[None, :]).astype(np.float32)
    cf32 = np.concatenate([ident, tri_f, tri_b], axis=1).astype(np.float32)
    ones = np.ones((128, 128), np.float32)
    j = p[:, None]
    i = p[None, :]
    mf = np.where(i < j, NEG, 0.0).astype(np.float32)
    mb = np.where(i > j, NEG, 0.0).astype(np.float32)
    ang = 2.0 * np.pi * ((p[:, None] * p[None, :]) % 128) / 128.0
    cg = np.cos(ang)
    sg = -np.sin(ang)
    alt = np.where(p % 2 == 0, 1.0, -1.0)[:, None] * np.ones((1, 2))
    altr = np.ones((128, 1)) * np.where(np.arange(256) % 2 == 0, 1.0, -1.0)[None, :]
    cbf = np.concatenate([ident, ones, np.tile(mf, (1, 4)), np.tile(mb, (1, 4)), cg, sg, alt, altr], axis=1).astype(_BF)
    t = np.arange(T, dtype=np.int64)
    kt = (t[:T // 2, None] * t[None, :T // 2]) % T
    angT = (2.0 * np.pi / T) * kt.astype(np.float64)
    def tile_dft(m):
        return np.ascontiguousarray(m.astype(np.float32).reshape(16, 128, T // 512, 256).transpose(2, 1, 0, 3)).astype(_BF)
    dftc = tile_dft(np.cos(angT))
    dfts = tile_dft(np.sin(angT))
    t2 = np.arange(TC, dtype=np.int64)
    a2 = (2.0 * np.pi / TC) * ((t2[:, None] * t2[None, :]) % TC).astype(np.float64)
    c2 = np.cos(a2).astype(np.float32).reshape(2, 128, TC).transpose(1, 0, 2)
    s2 = np.sin(a2).astype(np.float32).reshape(2, 128, TC).transpose(1, 0, 2)
    dft256 = np.concatenate([c2, s2], axis=2).astype(_BF)
    return dict(cf32=cf32, cbf=cbf, dftc=dftc, dfts=dfts, dft256=np.ascontiguousarray(dft256))


_CONSTS = None
_NC = None


def _fm(v, nchunk):
    return np.ascontiguousarray(np.asarray(v, np.float32).reshape(nchunk, 128).T)


def kernel(x, c, ctx, c_ctx, ada_w, ada_b, norm_w, w_in, w_out, rg_conv_w, rg_conv_b, rg_gate_a_w, rg_gate_a_b,
           rg_gate_x_w, rg_gate_x_b, rg_lambda, sc_conv_w, ssd_conv_w, ssd_conv_b, ssd_dt_bias, ssd_a_log, ssd_d,
           ssd_norm_w, final_norm_w):
    global _CONSTS, _NC
    f = lambda a: np.asarray(a, dtype=np.float32)
    x, c, ctx, c_ctx = f(x), f(c), f(ctx), f(c_ctx)
    if _CONSTS is None:
        _CONSTS = _host_consts()
    if _NC is None:
        _NC = build_program()
    pvec = np.zeros((NL, 128, NPV), np.float32)
    rowb = np.zeros((NL, 128, NRB), np.float32)
    gbd = np.zeros((NL, 128, 16, 128), np.float32)
    for l in range(NL):
        pvec[l, :, PV["nw"]:PV["nw"] + 8] = _fm(f(norm_w)[l], 8)
        pvec[l, :, PV["adab"]:PV["adab"] + 24] = _fm(f(ada_b)[l], 24)
        for cc in range(4):
            for k in range(4):
                pvec[l, :, PV["rgcw"] + cc * 4 + k] = f(rg_conv_w)[l, k, cc * 128:(cc + 1) * 128]
            pvec[l, :, PV["rgcb"] + cc] = f(rg_conv_b)[l, cc * 128:(cc + 1) * 128]
            for d in range(2):
                pvec[l, :, PV["rgba"] + d * 4 + cc] = f(rg_gate_a_b)[l, d, cc * 128:(cc + 1) * 128]
                pvec[l, :, PV["rgbx"] + d * 4 + cc] = f(rg_gate_x_b)[l, d, cc * 128:(cc + 1) * 128]
                pvec[l, :, PV["rglam"] + d * 4 + cc] = f(rg_lambda)[l, d, cc * 128:(cc + 1) * 128]
                for ax, W in enumerate((f(rg_gate_a_w), f(rg_gate_x_w))):
                    idx = d * 8 + ax * 4 + cc
                    gbd[l, 0:64, idx, 0:64] = W[l, d, 2 * cc]
                    gbd[l, 64:128, idx, 64:128] = W[l, d, 2 * cc + 1]
            for k in range(3):
                pvec[l, :, PV["sccw"] + cc * 3 + k] = f(sc_conv_w)[l, k, cc * 128:(cc + 1) * 128]
        for cch in range(6):
            for k in range(4):
                pvec[l, :, PV["sdcw"] + cch * 4 + k] = f(ssd_conv_w)[l, k, cch * 128:(cch + 1) * 128]
            pvec[l, :, PV["sdcb"] + cch] = f(ssd_conv_b)[l, cch * 128:(cch + 1) * 128]
        pvec[l, :, PV["fnw"]:PV["fnw"] + 8] = _fm(f(final_norm_w), 8)
        rowb[l, :, RB["dtb"]:RB["dtb"] + 16] = f(ssd_dt_bias)[l].reshape(1, 16)
        rowb[l, :, RB["alog"]:RB["alog"] + 16] = f(ssd_a_log)[l].reshape(1, 16)
        rowb[l, :, RB["dbc"]:RB["dbc"] + 512] = np.repeat(f(ssd_d)[l], 64)[None, :]
        rowb[l, :, RB["snw"]:RB["snw"] + 512] = f(ssd_norm_w)[l][None, :]
    shared = dict(ada_w=f(ada_w), w_in=f(w_in), w_out=f(w_out), pvec=pvec, rowb=rowb, gbd=gbd, **_CONSTS)
    in_maps = []
    for core in range(NCORES):
        b = core % 4
        cc = np.stack([_fm(c[b], 8), _fm(c_ctx, 8)], axis=2)
        m = dict(shared)
        m["xT"] = np.ascontiguousarray(x[b].T)
        m["ctxT"] = np.ascontiguousarray(ctx[b].T)
        m["cc"] = np.ascontiguousarray(cc)
        in_maps.append(m)
    res = run_bass_kernel_spmd(_NC, in_maps, core_ids=list(range(NCORES)))
    out = np.stack([np.ascontiguousarray(res.results[b]["outT"].T) for b in range(4)], axis=0)
    return out.astype(np.float32)
```
